# Optimizing a Trainium2 kernel written in Bass

```python
import math
import jax, jax.numpy as jnp
from jax import lax
import numpy as np

D_MODEL = 2048
BATCH = 1
SEQ = 8192
DEPTH = 4
DEC_BATCH = 4
DEC_SEQ = 2048
PAST_LEN = 128

HEAD_DIM = 128
NA_HEADS = D_MODEL // HEAD_DIM
NA_WIN_ROWS = 8
NA_WIN_COLS = 16
GRID_W = 64
DIL_CONFIGS = ((128, 1), (512, 4), (2048, 16))
NB_GROUPS = len(DIL_CONFIGS)
NB_HEADS = D_MODEL // HEAD_DIM
T5_BUCKETS = 32
T5_MAX_DIST = 1024
X_HEADS = 4
N_MEM = 256
D_FF = 4 * D_MODEL
N_A_LAYERS = (DEPTH + 1) // 2
N_B_LAYERS = DEPTH // 2
RMS_EPS = 1e-6
NEG_INF = -1e30
ATTN_SCALE = 1.0 / math.sqrt(HEAD_DIM)

kernel_name = "hybrid_natten_dilated_encoder"


def rmsnorm(x, g):
    x32 = x.astype(jnp.float32)
    y = x32 * lax.rsqrt(jnp.mean(x32 * x32, axis=-1, keepdims=True) + RMS_EPS)
    return (y * g.astype(jnp.float32)).astype(x.dtype)


def t5_bucket(rel):
    nb = T5_BUCKETS // 2
    max_exact = nb // 2
    ret = jnp.where(rel > 0, nb, 0)
    n = jnp.abs(rel)
    n_f = jnp.maximum(n, 1).astype(jnp.float32)
    large = max_exact + (jnp.log(n_f / max_exact) / math.log(T5_MAX_DIST / max_exact)
                         * (nb - max_exact)).astype(jnp.int32)
    large = jnp.minimum(large, nb - 1)
    return ret + jnp.where(n < max_exact, n, large)


def neighborhood_mixer(h, w_qkv, qn, kn, rpb, w_o):
    bn, s, _ = h.shape
    rows = s // GRID_W
    kh = min(NA_WIN_ROWS, rows)
    qkv = (h @ w_qkv).reshape(bn, s, 3, NA_HEADS, HEAD_DIM)
    q = rmsnorm(qkv[:, :, 0], qn)
    k = rmsnorm(qkv[:, :, 1], kn)
    v = qkv[:, :, 2]
    r = jnp.arange(rows)
    rs = jnp.clip(r - kh // 2, 0, rows - kh)
    key_rows = rs[:, None] + jnp.arange(kh)[None, :]
    c = jnp.arange(GRID_W)
    cs = jnp.clip(c - NA_WIN_COLS // 2, 0, GRID_W - NA_WIN_COLS)
    col_ok = (c[None, :] >= cs[:, None]) & (c[None, :] < cs[:, None] + NA_WIN_COLS)
    qg = q.reshape(bn, rows, GRID_W, NA_HEADS, HEAD_DIM)
    kg = k.reshape(bn, rows, GRID_W, NA_HEADS, HEAD_DIM)[:, key_rows].reshape(bn, rows, kh * GRID_W, NA_HEADS, HEAD_DIM)
    vg = v.reshape(bn, rows, GRID_W, NA_HEADS, HEAD_DIM)[:, key_rows].reshape(bn, rows, kh * GRID_W, NA_HEADS, HEAD_DIM)
    logits = jnp.einsum('brqhd,brkhd->brhqk', qg, kg).astype(jnp.float32) * ATTN_SCALE
    dr = key_rows - r[:, None] + (NA_WIN_ROWS - 1)
    dc = jnp.clip(c[None, :] - c[:, None], -(NA_WIN_COLS - 1), NA_WIN_COLS - 1) + (NA_WIN_COLS - 1)
    bias = rpb[:, dr][..., dc]
    bias = bias.transpose(1, 0, 3, 2, 4).reshape(rows, NA_HEADS, GRID_W, kh * GRID_W)
    mask = jnp.tile(col_ok, (1, kh))
    logits = jnp.where(mask, logits + bias.astype(jnp.float32)[None], NEG_INF)
    p = jax.nn.softmax(logits, axis=-1).astype(v.dtype)
    o = jnp.einsum('brhqk,brkhd->brqhd', p, vg).reshape(bn, s, NA_HEADS * HEAD_DIM)
    return o @ w_o


def dilated_group_attn(q, k, v, t5_g, dil, radius):
    bn, s, nh, dh = q.shape
    L = s // dil
    n = bn * dil

    def to_res(t):
        return t.reshape(bn, L, dil, nh, dh).transpose(0, 2, 1, 3, 4).reshape(n, L, nh, dh)

    q, k, v = to_res(q), to_res(k), to_res(v)
    blk = radius
    nb = -(-L // blk)
    lp = nb * blk
    qp = jnp.pad(q, ((0, 0), (0, lp - L), (0, 0), (0, 0))).reshape(n, nb, blk, nh, dh)

    def windows(t):
        tp = jnp.pad(t, ((0, 0), (radius, lp - L + radius), (0, 0), (0, 0))).reshape(n, nb + 2, blk, nh, dh)
        return jnp.concatenate([tp[:, :nb], tp[:, 1:nb + 1], tp[:, 2:nb + 2]], axis=2)

    kw, vw = windows(k), windows(v)
    i = jnp.arange(blk)
    j = jnp.arange(3 * blk)
    dm = j[None, :] - blk - i[:, None]
    key_idx = jnp.arange(nb)[:, None] * blk - radius + j[None, :]
    valid = (jnp.abs(dm) <= radius)[None] & ((key_idx >= 0) & (key_idx < L))[:, None, :]
    bias = t5_g[t5_bucket(dm * dil)].transpose(2, 0, 1).astype(jnp.float32)
    logits = jnp.einsum('nbqhd,nbkhd->nbhqk', qp, kw).astype(jnp.float32) * ATTN_SCALE + bias[None, None]
    logits = jnp.where(valid[None, :, None], logits, NEG_INF)
    m = jnp.max(logits, axis=-1, keepdims=True)
    p = jnp.exp(logits - m)
    den = jnp.sum(p, axis=-1, keepdims=True)
    o = jnp.einsum('nbhqk,nbkhd->nbqhd', (p / den).astype(v.dtype), vw)
    lse = (m + jnp.log(den))[..., 0].transpose(0, 1, 3, 2)
    o = o.reshape(n, lp, nh, dh)[:, :L]
    lse = lse.reshape(n, lp, nh)[:, :L]
    o = o.reshape(bn, dil, L, nh, dh).transpose(0, 2, 1, 3, 4).reshape(bn, s, nh, dh)
    lse = lse.reshape(bn, dil, L, nh).transpose(0, 2, 1, 3).reshape(bn, s, nh)
    return o, lse


def dilated_mixer(h, w_qkv, qn, kn, t5_table, w_o):
    bn, s, _ = h.shape
    qkv = (h @ w_qkv).reshape(bn, s, NB_GROUPS, 3, NB_HEADS, HEAD_DIM)
    outs, lses = [], []
    for g, (window, dil) in enumerate(DIL_CONFIGS):
        q = rmsnorm(qkv[:, :, g, 0], qn[g])
        k = rmsnorm(qkv[:, :, g, 1], kn[g])
        v = qkv[:, :, g, 2]
        o, lse = dilated_group_attn(q, k, v, t5_table[:, g], dil, window // (2 * dil))
        outs.append(o)
        lses.append(lse)
    wts = jax.nn.softmax(jnp.stack(lses, axis=0), axis=0)
    o = jnp.sum(wts[..., None].astype(outs[0].dtype) * jnp.stack(outs, axis=0), axis=0)
    return o.reshape(bn, s, NB_HEADS * HEAD_DIM) @ w_o


def memory_cross_attn(h, mem_n, w_q, w_kv, qn, kn, w_o):
    bn, s, _ = h.shape
    nm = mem_n.shape[1]
    q = rmsnorm((h @ w_q).reshape(bn, s, X_HEADS, HEAD_DIM), qn)
    kv = (mem_n @ w_kv).reshape(bn, nm, 2, X_HEADS, HEAD_DIM)
    k = rmsnorm(kv[:, :, 0], kn)
    v = kv[:, :, 1]
    logits = jnp.einsum('bshd,bmhd->bhsm', q, k).astype(jnp.float32) * ATTN_SCALE
    p = jax.nn.softmax(logits, axis=-1).astype(v.dtype)
    o = jnp.einsum('bhsm,bmhd->bshd', p, v).reshape(bn, s, X_HEADS * HEAD_DIM)
    return o @ w_o


def sqrelu_mlp(h, w_up, w_down):
    a = jax.nn.relu(h @ w_up)
    return (a * a) @ w_down


def trunk(x, mem, g_mix, g_cross, g_mem, g_mlp, w_qkv_a, q_norm_a, k_norm_a, rpb_a, w_o_a,
          w_qkv_b, q_norm_b, k_norm_b, t5_table, w_o_b, w_q_x, w_kv_x, q_norm_x, k_norm_x, w_o_x,
          w_up, w_down):
    for i in range(DEPTH):
        h = rmsnorm(x, g_mix[i])
        li = i // 2
        if i % 2 == 0:
            x = x + neighborhood_mixer(h, w_qkv_a[li], q_norm_a[li], k_norm_a[li], rpb_a[li], w_o_a[li])
        else:
            x = x + dilated_mixer(h, w_qkv_b[li], q_norm_b[li], k_norm_b[li], t5_table, w_o_b[li])
        h = rmsnorm(x, g_cross[i])
        m = rmsnorm(mem, g_mem[i])
        x = x + memory_cross_attn(h, m, w_q_x[i], w_kv_x[i], q_norm_x[i], k_norm_x[i], w_o_x[i])
        h = rmsnorm(x, g_mlp[i])
        x = x + sqrelu_mlp(h, w_up[i], w_down[i])
    return x


def setup_inputs(seed: int = 0) -> dict:
    key = jax.random.key(seed)
    ks = jax.random.split(key, 32)

    def nrm(k, shape, scale):
        return jax.random.normal(k, shape, jnp.float32) * scale

    def gain(k, shape):
        return 1.0 + nrm(k, shape, 0.05)

    d = D_MODEL
    return {
        "x_prompt": nrm(ks[0], (BATCH, SEQ, d), 1.0),
        "x_sample": nrm(ks[1], (DEC_BATCH, DEC_SEQ, d), 1.0),
        "mem_prompt": nrm(ks[2], (BATCH, N_MEM, d), 1.0),
        "mem_sample": nrm(ks[3], (DEC_BATCH, N_MEM, d), 1.0),
        "g_mix": gain(ks[4], (DEPTH, d)),
        "g_cross": gain(ks[5], (DEPTH, d)),
        "g_mem": gain(ks[6], (DEPTH, d)),
        "g_mlp": gain(ks[7], (DEPTH, d)),
        "w_qkv_a": nrm(ks[8], (N_A_LAYERS, d, 3 * NA_HEADS * HEAD_DIM), d ** -0.5),
        "q_norm_a": gain(ks[9], (N_A_LAYERS, HEAD_DIM)),
        "k_norm_a": gain(ks[10], (N_A_LAYERS, HEAD_DIM)),
        "rpb_a": nrm(ks[11], (N_A_LAYERS, NA_HEADS, 2 * NA_WIN_ROWS - 1, 2 * NA_WIN_COLS - 1), 0.1),
        "w_o_a": nrm(ks[12], (N_A_LAYERS, NA_HEADS * HEAD_DIM, d), (NA_HEADS * HEAD_DIM) ** -0.5),
        "w_qkv_b": nrm(ks[13], (N_B_LAYERS, d, NB_GROUPS * 3 * NB_HEADS * HEAD_DIM), d ** -0.5),
        "q_norm_b": gain(ks[14], (N_B_LAYERS, NB_GROUPS, HEAD_DIM)),
        "k_norm_b": gain(ks[15], (N_B_LAYERS, NB_GROUPS, HEAD_DIM)),
        "t5_table": nrm(ks[16], (T5_BUCKETS, NB_GROUPS, NB_HEADS), 0.1),
        "w_o_b": nrm(ks[17], (N_B_LAYERS, NB_HEADS * HEAD_DIM, d), (NB_HEADS * HEAD_DIM) ** -0.5),
        "w_q_x": nrm(ks[18], (DEPTH, d, X_HEADS * HEAD_DIM), d ** -0.5),
        "w_kv_x": nrm(ks[19], (DEPTH, d, 2 * X_HEADS * HEAD_DIM), d ** -0.5),
        "q_norm_x": gain(ks[20], (DEPTH, HEAD_DIM)),
        "k_norm_x": gain(ks[21], (DEPTH, HEAD_DIM)),
        "w_o_x": nrm(ks[22], (DEPTH, X_HEADS * HEAD_DIM, d), (X_HEADS * HEAD_DIM) ** -0.5),
        "w_up": nrm(ks[23], (DEPTH, d, D_FF), d ** -0.5),
        "w_down": nrm(ks[24], (DEPTH, D_FF, d), D_FF ** -0.5),
    }


def reference(x_prompt, x_sample, mem_prompt, mem_sample, g_mix, g_cross, g_mem, g_mlp,
              w_qkv_a, q_norm_a, k_norm_a, rpb_a, w_o_a, w_qkv_b, q_norm_b, k_norm_b, t5_table, w_o_b,
              w_q_x, w_kv_x, q_norm_x, k_norm_x, w_o_x, w_up, w_down):
    y_prompt = trunk(x_prompt, mem_prompt, g_mix, g_cross, g_mem, g_mlp, w_qkv_a, q_norm_a, k_norm_a,
                     rpb_a, w_o_a, w_qkv_b, q_norm_b, k_norm_b, t5_table, w_o_b, w_q_x, w_kv_x,
                     q_norm_x, k_norm_x, w_o_x, w_up, w_down)
    y_sample = trunk(x_sample, mem_sample, g_mix, g_cross, g_mem, g_mlp, w_qkv_a, q_norm_a, k_norm_a,
                     rpb_a, w_o_a, w_qkv_b, q_norm_b, k_norm_b, t5_table, w_o_b, w_q_x, w_kv_x,
                     q_norm_x, k_norm_x, w_o_x, w_up, w_down)
    return (y_prompt, y_sample)
```

```python
import math
import contextlib
import numpy as np
import concourse.bass as bass
import concourse.mybir as mybir
from concourse.bass_utils import run_bass_kernel_spmd

F32 = mybir.dt.float32
BF16 = mybir.dt.bfloat16
AF = mybir.ActivationFunctionType
ALU = mybir.AluOpType

D = 2048
DH = 128
NH = 16
CH = 2048
NQT = CH // 128
PAD = 1024
EPS = 1e-6
SCALE = 1.0 / math.sqrt(DH)
NEG = -30000.0
DIL = ((128, 1), (512, 4), (2048, 16))
REACH_DIL = (1, 2, 8)
REACH_NA = 3
ENGS = ("sp", "act", "pool", "pe", "dve")


class KB:
    def __init__(self, nc, sems):
        self.nc = nc
        self.sems = sems
        self.cnt = {k: 0 for k in sems}
        self.cur = None
        self.E = None
        self.waited = {}
        self.free_sems = [k for k in sems if k.startswith("r")]
        self.ring_sem = {}

    def I(self, eng, fn, sig=False):
        tok = None
        if sig:
            self.cnt[eng] += 1
            tok = (eng, self.cnt[eng])
        if self.cur == eng:
            ins = fn(self.E)
            if sig:
                ins.then_inc(self.sems[eng], 1)
        return tok

    def DMA(self, q, sem, out, in_):
        self.cnt[sem] += 16
        tok = (sem, self.cnt[sem])
        if self.cur == q:
            self.E.dma_start(out=out, in_=in_).then_inc(self.sems[sem], 16)
        return tok

    def W(self, eng, *toks):
        if self.cur != eng:
            return
        for t in toks:
            if t is None:
                continue
            if isinstance(t, list):
                self.W(eng, *t)
                continue
            s, v = t
            if self.waited.get((eng, s), 0) < v:
                self.E.wait_ge(self.sems[s], v)
                self.waited[(eng, s)] = v

    def state(self):
        return (dict(self.cnt),)

    def restore(self, st):
        self.cnt = dict(st[0])


class Ring:
    def __init__(self, kb, name, tiles, semnames):
        self.kb = kb
        self.t = tiles
        self.n = len(tiles)
        self.sem = semnames
        self.reset()

    def reset(self):
        self.use = 0
        self.free = [None] * self.n

    def nxt(self):
        i = self.use % self.n
        self.use += 1
        return i


def run_phase(nc, kb, tensors, nsem_rings, script):
    with contextlib.ExitStack() as st:
        T = {}
        for name, (shape, dt, kind) in tensors.items():
            if kind == "sb":
                T[name] = st.enter_context(nc.sbuf_tensor(name + "_%d" % kb.phase_id, shape, dt))
            else:
                T[name] = st.enter_context(nc.psum_tensor(name + "_%d" % kb.phase_id, shape, dt))
        block = st.enter_context(nc.Block())
        s0 = kb.state()

        def mk(engname):
            def f(e):
                kb.restore(s0)
                kb.cur = engname
                kb.E = e
                script(kb, T)
                kb.cur = None
            return f

        block.sync(mk("sp"))
        block.scalar(mk("act"))
        block.gpsimd(mk("pool"))
        block.tensor(mk("pe"))
        block.vector(mk("dve"))
    kb.phase_id += 1


def end_phase(kb, T, extra):
    toks = list(extra)
    toks.append(kb.I("act", lambda e: e.activation(out=T["scr"][:, 0:1], in_=T["scr"][:, 1:2], func=AF.Copy), sig=True))
    toks.append(kb.I("dve", lambda e: e.tensor_copy(out=T["scr"][:, 2:3], in_=T["scr"][:, 3:4]), sig=True))
    toks.append(kb.I("pool", lambda e: e.tensor_copy(out=T["scr"][:, 4:5], in_=T["scr"][:, 5:6]), sig=True))
    for s in kb.cnt:
        if s.startswith("r") and kb.cnt[s] > 0:
            toks.append((s, kb.cnt[s]))
    kb.W("pe", *toks)
    toks.append(kb.I("pe", lambda e: e.matmul(T["pm0"][:, 0:8], T["ones"][:, 0:128], T["ones"][:, 0:8], start=True, stop=True), sig=True))
    for s in kb.cnt:
        if s.startswith("r") and kb.cnt[s] > 0:
            toks.append((s, kb.cnt[s]))
    for e in ENGS:
        kb.W(e, *toks)


def build(NCH, LAYERS, DBG_STOP=None):
    T_ = NCH * CH
    TP = T_ + 2 * PAD
    nc = bass.Bass("TRN2", target_bir_lowering=False)

    def din(name, shape, dt=F32):
        return nc.dram_tensor(name, list(shape), dt, kind="ExternalInput")

    xT_in = din("xT", [16, 128, T_])
    memT_in = din("memT", [NCH, 16, 128, 256])
    gvec_in = din("gvec", [128, 4 * len(LAYERS) * 16])
    hvec_in = din("hvec", [128, len(LAYERS) * 16])
    flags_in = din("flags", [128, 32])
    ones_in = din("onesm", [128, 128])
    W = {}
    for li, L in enumerate(LAYERS):
        if L % 2 == 0:
            W[li, "qk"] = din("wqk%d" % li, [32, 128, 16 * 128])
            W[li, "v"] = din("wv%d" % li, [4, 128, 16 * 512])
            W[li, "bias"] = din("bias%d" % li, [NH, 128, 5 * 7 * 128])
        else:
            W[li, "qk"] = din("wqk%d" % li, [96, 128, 16 * 128])
            W[li, "v"] = din("wv%d" % li, [12, 128, 16 * 512])
            W[li, "bias"] = din("bias%d" % li, [NH, 128, 27 * 128])
        W[li, "o"] = din("wo%d" % li, [16, 128, 16 * 128])
        W[li, "xq"] = din("wxq%d" % li, [4, 128, 16 * 128])
        W[li, "xk"] = din("wxk%d" % li, [4, 128, 16 * 128])
        W[li, "xv"] = din("wxv%d" % li, [1, 128, 16 * 512])
        W[li, "xo"] = din("wxo%d" % li, [16, 128, 4 * 128])
        W[li, "up"] = din("wup%d" % li, [64, 128, 16 * 128])
        W[li, "dn"] = din("wdn%d" % li, [16, 128, 64 * 128])
    yT = nc.dram_tensor("yT", [16, 128, T_], F32, kind="ExternalOutput")
    XS = yT if DBG_STOP is not None else nc.dram_tensor("xscr", [16, 128, T_], F32)
    QT = [nc.dram_tensor("qt%d" % g, [NH, 128, T_], BF16) for g in range(3)]
    KT = [nc.dram_tensor("kt%d" % g, [NH, 128, TP], BF16) for g in range(3)]
    VV = [nc.dram_tensor("vv%d" % g, [NH, TP, 128], BF16) for g in range(3)]
    AT = nc.dram_tensor("at", [64, 128, CH], BF16)

    semnames = list(ENGS) + ["r%d" % i for i in range(64)]
    with contextlib.ExitStack() as st:
        sems = {k: st.enter_context(nc.semaphore("s_" + k)) for k in semnames}
        kb = KB(nc, sems)
        kb.phase_id = 0
        rsem = iter(["r%d" % i for i in range(64)])
        RS = {}

        def ringsem(name, n):
            if name not in RS:
                RS[name] = [next(rsem) for _ in range(n)]
            return RS[name]

        ones = st.enter_context(nc.sbuf_tensor("ones", [128, 128], BF16))
        gvec = st.enter_context(nc.sbuf_tensor("gvec_sb", [128, 4 * len(LAYERS) * 16], F32))
        hvec = st.enter_context(nc.sbuf_tensor("hvec_sb", [128, len(LAYERS) * 16], F32))
        flags = st.enter_context(nc.sbuf_tensor("flags_sb", [128, 32], F32))
        scr = st.enter_context(nc.sbuf_tensor("scr", [128, 8], F32))
        zer = st.enter_context(nc.sbuf_tensor("zer", [128, 2048], BF16))
        epsb = st.enter_context(nc.sbuf_tensor("epsb", [128, 2], F32))
        COMMON = {"epsb": epsb, "ones": ones, "gvec": gvec, "hvec": hvec, "flags": flags, "scr": scr, "zer": zer}

        def phase(tensors, script):
            tensors = dict(tensors)
            if "pm0" not in tensors:
                tensors["pm0"] = ([128, 512], F32, "ps")

            def s2(kb, T):
                T = dict(T)
                T.update(COMMON)
                script(kb, T)
            run_phase(nc, kb, tensors, 0, s2)

        def init_script(kb, T):
            s = ringsem("init", 1)[0]
            t = []
            t.append(kb.DMA("pool", s, T["ones"][:], ones_in[:, :]))
            t.append(kb.DMA("sp", s, T["gvec"][:], gvec_in[:, :]))
            t.append(kb.DMA("sp", s, T["hvec"][:], hvec_in[:, :]))
            t.append(kb.DMA("sp", s, T["flags"][:], flags_in[:, :]))
            kb.W("dve", t[1], t[2])
            kb.I("dve", lambda e: e.tensor_scalar(out=T["gvec"][:], in0=T["gvec"][:], scalar1=float(math.sqrt(D)), scalar2=None, op0=ALU.mult), sig=True)
            hk = T["hvec"][:].rearrange("p (a two) -> p a two", two=2)[:, :, 1]
            kb.I("dve", lambda e: e.tensor_scalar(out=hk, in0=hk, scalar1=float(math.sqrt(DH)), scalar2=None, op0=ALU.mult), sig=True)
            kb.W("pe", t[0])
            t1 = kb.I("dve", lambda e: e.memset(T["zer"][:], 0.0), sig=True)
            kb.I("dve", lambda e: e.memset(T["epsb"][:, 0:1], float(D * EPS)), sig=True)
            kb.I("dve", lambda e: e.memset(T["epsb"][:, 1:2], float(DH * EPS)), sig=True)
            t2 = kb.I("dve", lambda e: e.memset(T["scr"][:], 0.0), sig=True)
            kb.W("sp", t1)
            for g in range(3):
                for h in range(NH):
                    kb.DMA("sp", s, KT[g][h, :, 0:PAD], T["zer"][:, 0:PAD])
                    kb.DMA("sp", s, KT[g][h, :, PAD + T_:TP], T["zer"][:, 0:PAD])
                    kb.DMA("sp", s, VV[g][h, 0:PAD, :].rearrange("(a p) d -> p a d", p=128), T["zer"][:, 0:PAD].rearrange("p (a d) -> p a d", d=128))
                    kb.DMA("sp", s, VV[g][h, PAD + T_:TP, :].rearrange("(a p) d -> p a d", p=128), T["zer"][:, 0:PAD].rearrange("p (a d) -> p a d", d=128))
            end_phase(kb, T, [t2])

        phase({}, init_script)

        class Ctx:
            pass

        def mk_rings(kb, T, spec):
            R = {}
            for name, n in spec.items():
                R[name] = Ring(kb, name, [T["%s%d" % (name, i)] for i in range(n)], ringsem(name, n))
            return R

        def norm_chunk(kb, T, R, src, c, gcol, ntok=CH, tok0=None, dst=None, srcap=None):
            dst = T["big"] if dst is None else dst
            done = []
            nsub = ntok // 256
            for j in range(nsub):
                i = R["xs"].nxt()
                xs = R["xs"].t[i]
                kb.W("sp", R["xs"].free[i])
                if srcap is None:
                    a = src[:, :, c * CH + j * 256: c * CH + (j + 1) * 256].rearrange("k p t -> p k t")
                else:
                    a = srcap[:, :, j * 256:(j + 1) * 256].rearrange("k p t -> p k t")
                tl = kb.DMA("sp", R["xs"].sem[i], xs[:], a)
                kb.W("act", tl)
                qi = R["sq"].nxt()
                sq = R["sq"].t[qi]
                kb.W("act", R["sq"].free[qi])
                t1 = kb.I("act", lambda e: e.activation(out=sq[:], in_=xs[:], func=AF.Square), sig=True)
                pb = R["pn"].nxt()
                ps = R["pn"].t[pb]
                kb.W("pe", t1, R["pn"].free[pb])
                for kc in range(16):
                    t2 = kb.I("pe", lambda e, kc=kc: e.matmul(ps[:, 0:256], T["ones"][:, :], sq[:, kc, :], start=(kc == 0), stop=(kc == 15)), sig=(kc == 15))
                R["sq"].free[qi] = t2
                ri = R["rs"].nxt()
                rs = R["rs"].t[ri]
                kb.W("dve", t2, R["rs"].free[ri])
                kb.W("act", ("pe", kb.cnt["pe"]), R["rs"].free[ri])
                t3a = kb.I("act", lambda e: e.activation(out=rs[:, 0:256], in_=ps[:, 0:256], func=AF.Sqrt, bias=T["epsb"][:, 0:1], scale=1.0), sig=True)
                kb.W("dve", t3a)
                t3 = kb.I("dve", lambda e: e.reciprocal(out=rs[:, 0:256], in_=rs[:, 0:256]), sig=True)
                kb.W("dve", t3)
                R["pn"].free[pb] = t3
                kb.W("dve", t3, tl)
                last = []
                for kc in range(16):
                    eng = "dve"
                    tk = kb.I(eng, lambda e, kc=kc: e.scalar_tensor_tensor(
                        out=dst[:, kc, j * 256:(j + 1) * 256], in0=xs[:, kc, :], scalar=T["gvec"][:, gcol + kc:gcol + kc + 1],
                        in1=rs[:, 0:256], op0=ALU.mult, op1=ALU.mult), sig=(kc == 15))
                    if tk is not None:
                        last.append(tk)
                R["xs"].free[i] = last
                R["rs"].free[ri] = last
                done += last
            return done

        def proj_fm(kb, T, R, wd, blocks, KC, src, ntok, src_toks, epi, wview=None):
            ntt = ntok // 512
            pend = None
            for bi, b in enumerate(blocks):
                wi = R["wr"].nxt()
                wt = R["wr"].t[wi]
                kb.W("pool", R["wr"].free[wi])
                tw = kb.DMA("pool", R["wr"].sem[wi], wt[:, 0:KC * 128], wd[b, :, :])
                kb.W("pe", tw, src_toks)
                for tt in range(ntt):
                    pb = R["pm"].nxt()
                    ps = R["pm"].t[pb]
                    kb.W("pe", R["pm"].free[pb])
                    for kc in range(KC):
                        tm = kb.I("pe", lambda e, kc=kc, tt=tt: e.matmul(ps[:, :], wt[:, kc * 128:(kc + 1) * 128], src[:, kc, tt * 512:(tt + 1) * 512], start=(kc == 0), stop=(kc == KC - 1)), sig=(kc == KC - 1))
                    R["pm"].free[pb] = epi(bi, b, tt, ps, tm)
                R["wr"].free[wi] = tm
            return tm

        def epi_qknorm(kb, T, R, ps, tm, gain_ap, dst_ap):
            qi = R["sq2"].nxt()
            sq = R["sq2"].t[qi]
            kb.W("act", tm, R["sq2"].free[qi])
            t1 = kb.I("act", lambda e: e.activation(out=sq[:], in_=ps[:, :], func=AF.Square), sig=True)
            pb = R["pn"].nxt()
            ps2 = R["pn"].t[pb]
            kb.W("pe", t1, R["pn"].free[pb])
            t2 = kb.I("pe", lambda e: e.matmul(ps2[:, :], T["ones"][:, :], sq[:], start=True, stop=True), sig=True)
            R["sq2"].free[qi] = t2
            ri = R["rs"].nxt()
            rs = R["rs"].t[ri]
            kb.W("dve", t2, R["rs"].free[ri], tm)
            kb.W("act", ("pe", kb.cnt["pe"]), R["rs"].free[ri])
            t3a = kb.I("act", lambda e: e.activation(out=rs[:, :], in_=ps2[:, :], func=AF.Sqrt, bias=T["epsb"][:, 1:2], scale=1.0), sig=True)
            kb.W("dve", t3a)
            t3 = kb.I("dve", lambda e: e.reciprocal(out=rs[:, :], in_=rs[:, :]), sig=True)
            kb.W("dve", t3)
            R["pn"].free[pb] = t3
            si = R["stg"].nxt()
            sg = R["stg"].t[si]
            kb.W("dve", R["stg"].free[si], t3)
            t4 = kb.I("dve", lambda e: e.scalar_tensor_tensor(out=sg[:], in0=ps[:, :], scalar=gain_ap, in1=rs[:, :], op0=ALU.mult, op1=ALU.mult), sig=True)
            R["rs"].free[ri] = t4
            if dst_ap is not None:
                kb.W("sp", t4)
                R["stg"].free[si] = kb.DMA("sp", R["stg"].sem[si], dst_ap, sg[:])
            return t4, sg, si

        def epi_resid(kb, T, R, ps, tm, src_ap, dst_ap):
            xi = R["xr"].nxt()
            xr = R["xr"].t[xi]
            kb.W("sp", R["xr"].free[xi])
            tl = kb.DMA("sp", R["xr"].sem[xi], xr[:], src_ap)
            kb.W("dve", tl, tm)
            t1 = kb.I("dve", lambda e: e.tensor_tensor(out=xr[:], in0=ps[:, :], in1=xr[:], op=ALU.add), sig=True)
            kb.W("sp", t1)
            R["xr"].free[xi] = kb.DMA("sp", R["xr"].sem[xi], dst_ap, xr[:])
            return t1

        def proj_tm(kb, T, R, wd, g, src, ntiles, src_toks, epi):
            wi = R["wv"].nxt()
            wt = R["wv"].t[wi]
            kb.W("pool", R["wv"].free[wi])
            tw = kb.DMA("pool", R["wv"].sem[wi], wt[:], wd[g, :, :])
            kb.W("pe", tw, src_toks)
            for t16 in range(ntiles):
                pb = R["pm"].nxt()
                ps = R["pm"].t[pb]
                kb.W("pe", R["pm"].free[pb])
                for kc in range(16):
                    tm = kb.I("pe", lambda e, kc=kc, t16=t16: e.matmul(ps[:, :], src[:, kc, t16 * 128:(t16 + 1) * 128], wt[:, kc * 512:(kc + 1) * 512], start=(kc == 0), stop=(kc == 15)), sig=(kc == 15))
                R["pm"].free[pb] = epi(t16, ps, tm)
            R["wv"].free[wi] = tm

        def attention(kb, T, R, ngroups, slots, qtile, ksrc, vsrc, bias_fn, flag_fn, out_ap, out_tok_dep, scale=None):
            nb = (len(slots) + 3) // 4
            oi = R["po"].nxt()
            po = R["po"].t[oi]
            pd = R["pd"].t[oi]
            kb.W("pe", R["po"].free[oi])
            first = True
            for bi in range(nb):
                bs = slots[bi * 4:(bi + 1) * 4]
                n = len(bs)
                si = R["ps"].nxt()
                ps = R["ps"].t[si]
                kb.W("pe", R["ps"].free[si])
                for j, (g, dl) in enumerate(bs):
                    tS = kb.I("pe", lambda e, j=j, g=g, dl=dl: e.matmul(ps[:, j * 128:(j + 1) * 128], ksrc(g, dl), qtile(g), start=True, stop=True), sig=(j == n - 1))
                pi = R["pt"].nxt()
                pt = R["pt"].t[pi]
                if bias_fn is not None:
                    ti = R["tm"].nxt()
                    tmp = R["tm"].t[ti]
                    kb.W("dve", tS, R["tm"].free[ti])
                    tB = bias_fn(bi, bs, tmp, ps)
                    R["ps"].free[si] = tB
                    kb.W("act", tB, R["pt"].free[pi])
                    tE = kb.I("act", lambda e: e.activation(out=pt[:, 0:n * 128], in_=tmp[:, 0:n * 128], func=AF.Exp), sig=True)
                    R["tm"].free[ti] = tE
                else:
                    kb.W("act", tS, R["pt"].free[pi])
                    tE = kb.I("act", lambda e: e.activation(out=pt[:, 0:n * 128], in_=ps[:, 0:n * 128], func=AF.Exp, scale=float(scale)), sig=True)
                    R["ps"].free[si] = tE
                tF = flag_fn(bi, bs, pt, tE) if flag_fn is not None else None
                kb.W("pe", tE, tF)
                for j, (g, dl) in enumerate(bs):
                    lastall = (bi == nb - 1 and j == n - 1)
                    kb.I("pe", lambda e, j=j, g=g, dl=dl, first=first, lastall=lastall: e.matmul(po[:, 0:128], vsrc(g, dl), pt[:, j * 128:(j + 1) * 128], start=first, stop=lastall))
                    tP = kb.I("pe", lambda e, j=j, first=first, lastall=lastall: e.matmul(pd[:, 0:128], T["ones"][:, :], pt[:, j * 128:(j + 1) * 128], start=first, stop=lastall), sig=(j == n - 1))
                    first = False
                R["pt"].free[pi] = tP
            ri = R["rc"].nxt()
            rc = R["rc"].t[ri]
            kb.W("dve", tP, R["rc"].free[ri], out_tok_dep)
            t1 = kb.I("dve", lambda e: e.reciprocal(out=rc[:, :], in_=pd[:, 0:128]), sig=True)
            kb.W("dve", t1)
            t2 = kb.I("dve", lambda e: e.tensor_tensor(out=out_ap, in0=po[:, 0:128], in1=rc[:, :], op=ALU.mult), sig=True)
            R["rc"].free[ri] = t2
            R["po"].free[oi] = t2
            return t2

        xsrc = xT_in
        for li, L in enumerate(LAYERS):
            is_na = (L % 2 == 0)
            ng = 1 if is_na else 3
            gbase = li * 64
            hbase = li * 16
            last_layer = (li == len(LAYERS) - 1)

            def qkv_script(kb, T, li=li, is_na=is_na, ng=ng, gbase=gbase, hbase=hbase, xsrc=xsrc):
                R = mk_rings(kb, T, {"xs": 2, "sq": 1, "pn": 2, "rs": 2, "wr": 3, "pm": 4, "sq2": 2, "stg": 3, "wv": 2, "vst": 2})
                big = T["big"]
                prev = None
                for c in range(NCH):
                    for e_ in ("dve", "pool"):
                        kb.W(e_, prev)
                    toks = norm_chunk(kb, T, R, xsrc, c, gbase + 0)

                    def epi(bi, b, tt, ps, tm, c=c):
                        g = b // 32
                        isk = (b % 32) >= 16
                        h = b % 16
                        gain = T["hvec"][:, hbase + g * 2 + (1 if isk else 0): hbase + g * 2 + (1 if isk else 0) + 1]
                        if isk:
                            dst = KT[g][h, :, PAD + c * CH + tt * 512: PAD + c * CH + (tt + 1) * 512]
                        else:
                            dst = QT[g][h, :, c * CH + tt * 512: c * CH + (tt + 1) * 512]
                        t4, _, _ = epi_qknorm(kb, T, R, ps, tm, gain, dst)
                        return t4
                    tl1 = proj_fm(kb, T, R, W[li, "qk"], list(range(32 * ng)), 16, big, CH, toks, epi)

                    for vg in range(4 * ng):
                        def epiv(t16, ps, tm, vg=vg, c=c):
                            g = vg // 4
                            vi = R["vst"].nxt()
                            vs = R["vst"].t[vi]
                            kb.W("act", tm, R["vst"].free[vi])
                            t1 = kb.I("act", lambda e: e.activation(out=vs[:], in_=ps[:, :], func=AF.Copy), sig=True)
                            kb.W("sp", t1)
                            r0 = PAD + c * CH + t16 * 128
                            dst = VV[g][(vg % 4) * 4:(vg % 4) * 4 + 4, r0:r0 + 128, :].rearrange("h p d -> p h d")
                            R["vst"].free[vi] = kb.DMA("sp", R["vst"].sem[vi], dst, vs[:].rearrange("p (h d) -> p h d", d=128))
                            return t1
                        proj_tm(kb, T, R, W[li, "v"], vg, big, 16, toks, epiv)
                    prev = ("pe", kb.cnt["pe"])
                end_phase(kb, T, [])

            phase({
                "big": ([128, 16, CH], BF16, "sb"),
                "xs0": ([128, 16, 256], F32, "sb"), "xs1": ([128, 16, 256], F32, "sb"),
                "sq0": ([128, 16, 256], BF16, "sb"),
                "rs0": ([128, 512], F32, "sb"), "rs1": ([128, 512], F32, "sb"),
                "wr0": ([128, 2048], BF16, "sb"), "wr1": ([128, 2048], BF16, "sb"), "wr2": ([128, 2048], BF16, "sb"),
                "wv0": ([128, 8192], BF16, "sb"), "wv1": ([128, 8192], BF16, "sb"),
                "sq20": ([128, 512], BF16, "sb"), "sq21": ([128, 512], BF16, "sb"),
                "stg0": ([128, 512], BF16, "sb"), "stg1": ([128, 512], BF16, "sb"), "stg2": ([128, 512], BF16, "sb"),
                "vst0": ([128, 512], BF16, "sb"), "vst1": ([128, 512], BF16, "sb"),
                "pn0": ([128, 512], F32, "ps"), "pn1": ([128, 512], F32, "ps"),
                "pm0": ([128, 512], F32, "ps"), "pm1": ([128, 512], F32, "ps"), "pm2": ([128, 512], F32, "ps"), "pm3": ([128, 512], F32, "ps"),
            }, qkv_script)
            if DBG_STOP == (li, 1):
                break

            if is_na:
                slots = [(0, d) for d in range(-REACH_NA, REACH_NA + 1)]
                nbias = 5 * 7 * 128
            else:
                slots = [(g, d) for g in range(3) for d in range(-REACH_DIL[g], REACH_DIL[g] + 1)]
                nbias = 27 * 128
            reach = [REACH_NA] if is_na else list(REACH_DIL)
            kw = [CH + 2 * r * 128 for r in reach]
            koff = [sum(kw[:g]) for g in range(ng)]
            vt = [NQT + 2 * r for r in reach]
            voff = [sum(vt[:g]) for g in range(ng)]
            xdst = XS

            def att_script(kb, T, li=li, is_na=is_na, ng=ng, slots=slots, reach=reach, kw=kw, koff=koff, vt=vt, voff=voff, xsrc=xsrc, xdst=xdst, nbias=nbias):
                R = mk_rings(kb, T, {"qk": 2, "vb": 1, "ps": 2, "po": 2, "pd": 2, "pt": 2, "tm": 2, "rc": 2, "wr": 3, "pm": 2, "xr": 3})
                big = T["big"]
                prev_wo = None
                for c in range(NCH):
                    head_toks = []
                    for h in range(NH):
                        qi = R["qk"].nxt()
                        qk = R["qk"].t[qi]
                        kb.W("sp", R["qk"].free[qi])
                        tl = []
                        for g in range(ng):
                            tl.append(kb.DMA("sp", R["qk"].sem[qi], qk[:, g * CH:(g + 1) * CH], QT[g][h, :, c * CH:(c + 1) * CH]))
                            k0 = PAD + c * CH - reach[g] * 128
                            tl.append(kb.DMA("sp", R["qk"].sem[qi], qk[:, 3 * CH + koff[g]: 3 * CH + koff[g] + kw[g]], KT[g][h, :, k0:k0 + kw[g]]))
                        vi = R["vb"].nxt()
                        vb = T["vbuf"]
                        bb = T["bbuf"]
                        kb.W("sp", R["vb"].free[vi])
                        for g in range(ng):
                            r0 = PAD + c * CH - reach[g] * 128
                            tl.append(kb.DMA("sp", R["vb"].sem[vi], vb[:, voff[g]:voff[g] + vt[g], :], VV[g][h, r0:r0 + vt[g] * 128, :].rearrange("(a p) d -> p a d", p=128)))
                        tl.append(kb.DMA("sp", R["vb"].sem[vi], bb[:, 0:nbias], W[li, "bias"][h, :, :]))
                        kb.W("pe", tl)
                        kb.W("dve", tl)
                        kb.W("pool", tl)
                        lastq = None
                        for i in range(NQT):
                            def qtile(g, i=i):
                                return qk[:, g * CH + i * 128: g * CH + (i + 1) * 128]

                            def ksrc(g, dl, i=i):
                                o = 3 * CH + koff[g] + (i + dl + reach[g]) * 128
                                return qk[:, o:o + 128]

                            def vsrc(g, dl, i=i):
                                return vb[:, voff[g] + i + dl + reach[g], :]

                            def bias_fn(bi, bs, tmp, ps, i=i, c=c):
                                n = len(bs)
                                off = bi * 512
                                if is_na:
                                    cls = {0: 1, 1: 2, NQT - 2: 3, NQT - 1: 4}.get(i, 0)
                                    if cls == 0:
                                        return kb.I("dve", lambda e: e.tensor_tensor(out=tmp[:, 0:n * 128], in0=ps[:, 0:n * 128], in1=bb[:, off:off + n * 128], op=ALU.add), sig=True)
                                    top = cls in (1, 2)
                                    fcol = (0 if top else 8) + c
                                    tq = kb.I("dve", lambda e: e.scalar_tensor_tensor(out=tmp[:, 0:n * 128], in0=bb[:, off:off + n * 128], scalar=T["flags"][:, fcol + 4:fcol + 5], in1=ps[:, 0:n * 128], op0=ALU.mult, op1=ALU.add), sig=True)
                                    kb.W("dve", tq)
                                    return kb.I("dve", lambda e: e.scalar_tensor_tensor(out=tmp[:, 0:n * 128], in0=bb[:, cls * 896 + off:cls * 896 + off + n * 128], scalar=T["flags"][:, fcol:fcol + 1], in1=tmp[:, 0:n * 128], op0=ALU.mult, op1=ALU.add), sig=True)
                                else:
                                    o = slots.index(bs[0]) * 128
                                    return kb.I("dve", lambda e: e.tensor_tensor(out=tmp[:, 0:n * 128], in0=ps[:, 0:n * 128], in1=bb[:, o:o + n * 128], op=ALU.add), sig=True)

                            def flag_fn(bi, bs, pt, tE, i=i, c=c):
                                if is_na:
                                    return None
                                tk = None
                                first = True
                                for j, (g, dl) in enumerate(bs):
                                    if i + dl < 0 or i + dl >= NQT:
                                        if first:
                                            kb.W("pool", tE)
                                            first = False
                                        fcol = 16 + (c if i + dl < 0 else 4 + c)
                                        tk = kb.I("pool", lambda e, j=j, fcol=fcol: e.tensor_scalar(out=pt[:, j * 128:(j + 1) * 128], in0=pt[:, j * 128:(j + 1) * 128], scalar1=T["flags"][:, fcol:fcol + 1], scalar2=None, op0=ALU.mult), sig=True)
                                return tk

                            lastq = attention(kb, T, R, ng, slots, qtile, ksrc, vsrc, bias_fn, flag_fn,
                                              big[:, h, i * 128:(i + 1) * 128], prev_wo if (h == 0 and i == 0) else None)
                        pe_done = ("pe", kb.cnt["pe"])
                        R["qk"].free[qi] = [pe_done]
                        R["vb"].free[vi] = [pe_done, lastq]
                        head_toks.append(lastq)

                    def epi(bi, b, tt, ps, tm, c=c):
                        sl = slice(c * CH + tt * 512, c * CH + (tt + 1) * 512)
                        return epi_resid(kb, T, R, ps, tm, xsrc[b, :, sl], xdst[b, :, sl])
                    prev_wo = proj_fm(kb, T, R, W[li, "o"], list(range(16)), 16, big, CH, head_toks[-1], epi)
                end_phase(kb, T, [])

            phase({
                "big": ([128, 16, CH], BF16, "sb"),
                "qk0": ([128, 3 * CH + 8960], BF16, "sb"), "qk1": ([128, 3 * CH + 8960], BF16, "sb"),
                "vb0": ([128, 8], BF16, "sb"),
                "vbuf": ([128, 70, 128], BF16, "sb"), "bbuf": ([128, 5 * 7 * 128], F32, "sb"),
                "pt0": ([128, 1024], BF16, "sb"), "pt1": ([128, 1024], BF16, "sb"),
                "tm0": ([128, 1024], F32, "sb"), "tm1": ([128, 1024], F32, "sb"),
                "rc0": ([128, 128], F32, "sb"), "rc1": ([128, 128], F32, "sb"),
                "wr0": ([128, 2048], BF16, "sb"), "wr1": ([128, 2048], BF16, "sb"), "wr2": ([128, 2048], BF16, "sb"),
                "xr0": ([128, 512], F32, "sb"), "xr1": ([128, 512], F32, "sb"), "xr2": ([128, 512], F32, "sb"),
                "ps0": ([128, 512], F32, "ps"), "ps1": ([128, 512], F32, "ps"),
                "po0": ([128, 512], F32, "ps"), "po1": ([128, 512], F32, "ps"),
                "pd0": ([128, 512], F32, "ps"), "pd1": ([128, 512], F32, "ps"),
                "pm0": ([128, 512], F32, "ps"), "pm1": ([128, 512], F32, "ps"),
            }, att_script)
            xsrc = XS
            if DBG_STOP == (li, 2):
                break

            def cross_script(kb, T, li=li, gbase=gbase, hbase=hbase):
                R = mk_rings(kb, T, {"xs": 2, "sq": 1, "pn": 1, "rs": 2, "wr": 3, "pm": 2, "sq2": 2, "stg": 2, "wv": 1, "ps": 1, "po": 2, "pd": 2, "pt": 2, "rc": 2, "xr": 3})
                big = T["big"]
                mt = T["mt"]
                kx = T["kx"]
                vx = T["vx"]
                qx = T["qx"]
                ox = T["ox"]
                prev = None
                for c in range(NCH):
                    for e_ in ("dve", "pool", "act"):
                        kb.W(e_, prev)
                    tm_ = norm_chunk(kb, T, R, None, 0, gbase + 32, ntok=256, dst=mt, srcap=memT_in[c])

                    def epik(bi, b, tt, ps, tm):
                        raise RuntimeError
                    for b in range(4):
                        wi = R["wr"].nxt()
                        wt = R["wr"].t[wi]
                        kb.W("pool", R["wr"].free[wi])
                        tw = kb.DMA("pool", R["wr"].sem[wi], wt[:, :], W[li, "xk"][b, :, :])
                        kb.W("pe", tw, tm_)
                        pb = R["pm"].nxt()
                        ps = R["pm"].t[pb]
                        kb.W("pe", R["pm"].free[pb])
                        for kc in range(16):
                            tmm = kb.I("pe", lambda e, kc=kc: e.matmul(ps[:, 0:256], wt[:, kc * 128:(kc + 1) * 128], mt[:, kc, :], start=(kc == 0), stop=(kc == 15)), sig=(kc == 15))
                        R["wr"].free[wi] = tmm
                        qi = R["sq2"].nxt()
                        sq = R["sq2"].t[qi]
                        kb.W("act", tmm, R["sq2"].free[qi])
                        t1 = kb.I("act", lambda e: e.activation(out=sq[:, 0:256], in_=ps[:, 0:256], func=AF.Square), sig=True)
                        p2 = R["pn"].nxt()
                        ps2 = R["pn"].t[p2]
                        kb.W("pe", t1, R["pn"].free[p2])
                        t2 = kb.I("pe", lambda e: e.matmul(ps2[:, 0:256], T["ones"][:, :], sq[:, 0:256], start=True, stop=True), sig=True)
                        R["sq2"].free[qi] = t2
                        ri = R["rs"].nxt()
                        rs = R["rs"].t[ri]
                        kb.W("dve", t2, R["rs"].free[ri], tmm, prev)
                        kb.W("act", ("pe", kb.cnt["pe"]), R["rs"].free[ri])
                        t3a = kb.I("act", lambda e: e.activation(out=rs[:, 0:256], in_=ps2[:, 0:256], func=AF.Sqrt, bias=T["epsb"][:, 1:2], scale=1.0), sig=True)
                        kb.W("dve", t3a)
                        t3 = kb.I("dve", lambda e: e.reciprocal(out=rs[:, 0:256], in_=rs[:, 0:256]), sig=True)
                        kb.W("dve", t3)
                        R["pn"].free[p2] = t3
                        t4 = kb.I("dve", lambda e, b=b: e.scalar_tensor_tensor(out=kx[:, b, :], in0=ps[:, 0:256], scalar=T["hvec"][:, hbase + 7:hbase + 8], in1=rs[:, 0:256], op0=ALU.mult, op1=ALU.mult), sig=True)
                        R["rs"].free[ri] = t4
                        R["pm"].free[pb] = t4
                    tk_done = t4
                    def epiv(t16, ps, tm):
                        kb.W("act", tm)
                        return kb.I("act", lambda e: e.activation(out=vx[:, t16, :], in_=ps[:, :], func=AF.Copy), sig=True)
                    vtoks = []

                    def epiv2(t16, ps, tm):
                        t = epiv(t16, ps, tm)
                        vtoks.append(t)
                        return t
                    proj_tm(kb, T, R, W[li, "xv"], 0, mt, 2, tm_, epiv2)
                    toks = norm_chunk(kb, T, R, XS, c, gbase + 16)

                    def epiq(bi, b, tt, ps, tm):
                        qi = R["sq2"].nxt()
                        sq = R["sq2"].t[qi]
                        kb.W("act", tm, R["sq2"].free[qi])
                        t1 = kb.I("act", lambda e: e.activation(out=sq[:], in_=ps[:, :], func=AF.Square), sig=True)
                        p2 = R["pn"].nxt()
                        ps2 = R["pn"].t[p2]
                        kb.W("pe", t1, R["pn"].free[p2])
                        t2 = kb.I("pe", lambda e: e.matmul(ps2[:, :], T["ones"][:, :], sq[:], start=True, stop=True), sig=True)
                        R["sq2"].free[qi] = t2
                        ri = R["rs"].nxt()
                        rs = R["rs"].t[ri]
                        kb.W("dve", t2, R["rs"].free[ri], tm)
                        kb.W("act", ("pe", kb.cnt["pe"]), R["rs"].free[ri])
                        t3a = kb.I("act", lambda e: e.activation(out=rs[:, :], in_=ps2[:, :], func=AF.Sqrt, bias=T["epsb"][:, 1:2], scale=1.0), sig=True)
                        kb.W("dve", t3a)
                        t3 = kb.I("dve", lambda e: e.reciprocal(out=rs[:, :], in_=rs[:, :]), sig=True)
                        kb.W("dve", t3)
                        R["pn"].free[p2] = t3
                        t4 = kb.I("dve", lambda e: e.scalar_tensor_tensor(out=qx[:, b, tt * 512:(tt + 1) * 512], in0=ps[:, :], scalar=T["hvec"][:, hbase + 6:hbase + 7], in1=rs[:, :], op0=ALU.mult, op1=ALU.mult), sig=True)
                        R["rs"].free[ri] = t4
                        return t4
                    proj_fm(kb, T, R, W[li, "xq"], list(range(4)), 16, big, CH, toks, epiq)
                    qdone = ("dve", kb.cnt["dve"])
                    kb.W("pe", qdone, tk_done, vtoks)
                    lastq = None
                    for h in range(4):
                        for i in range(NQT):
                            lastq = attention(
                                kb, T, R, 1, [(0, 0), (0, 1)],
                                lambda g, i=i, h=h: qx[:, h, i * 128:(i + 1) * 128],
                                lambda g, dl, h=h: kx[:, h, dl * 128:(dl + 1) * 128],
                                lambda g, dl, h=h: vx[:, dl, h * 128:(h + 1) * 128],
                                None, None, ox[:, h, i * 128:(i + 1) * 128], prev if (h == 0 and i == 0) else None, scale=1.0)

                    def epi(bi, b, tt, ps, tm, c=c):
                        sl = slice(c * CH + tt * 512, c * CH + (tt + 1) * 512)
                        return epi_resid(kb, T, R, ps, tm, XS[b, :, sl], XS[b, :, sl])
                    prev = proj_fm(kb, T, R, W[li, "xo"], list(range(16)), 4, ox, CH, lastq, epi)
                end_phase(kb, T, [])

            phase({
                "big": ([128, 16, CH], BF16, "sb"),
                "mt": ([128, 16, 256], BF16, "sb"), "kx": ([128, 4, 256], BF16, "sb"), "vx": ([128, 2, 512], BF16, "sb"),
                "qx": ([128, 4, CH], BF16, "sb"), "ox": ([128, 4, CH], BF16, "sb"),
                "xs0": ([128, 16, 256], F32, "sb"), "xs1": ([128, 16, 256], F32, "sb"),
                "sq0": ([128, 16, 256], BF16, "sb"),
                "rs0": ([128, 512], F32, "sb"), "rs1": ([128, 512], F32, "sb"),
                "wr0": ([128, 2048], BF16, "sb"), "wr1": ([128, 2048], BF16, "sb"), "wr2": ([128, 2048], BF16, "sb"),
                "wv0": ([128, 8192], BF16, "sb"),
                "sq20": ([128, 512], BF16, "sb"), "sq21": ([128, 512], BF16, "sb"),
                "stg0": ([128, 8], BF16, "sb"), "stg1": ([128, 8], BF16, "sb"),
                "pt0": ([128, 1024], BF16, "sb"), "pt1": ([128, 1024], BF16, "sb"),
                "rc0": ([128, 128], F32, "sb"), "rc1": ([128, 128], F32, "sb"),
                "xr0": ([128, 512], F32, "sb"), "xr1": ([128, 512], F32, "sb"), "xr2": ([128, 512], F32, "sb"),
                "pn0": ([128, 512], F32, "ps"),
                "pm0": ([128, 512], F32, "ps"), "pm1": ([128, 512], F32, "ps"),
                "ps0": ([128, 512], F32, "ps"),
                "po0": ([128, 512], F32, "ps"), "po1": ([128, 512], F32, "ps"),
                "pd0": ([128, 512], F32, "ps"), "pd1": ([128, 512], F32, "ps"),
            }, cross_script)
            if DBG_STOP == (li, 3):
                break

            xout = yT if last_layer else XS

            def mlp_script(kb, T, li=li, gbase=gbase, xout=xout):
                R = mk_rings(kb, T, {"xs": 2, "sq": 1, "pn": 1, "rs": 2, "wr": 3, "pm": 4, "rl": 2, "stg": 3, "wd": 2, "xr": 3})
                big = T["big"]
                prev = None
                for c in range(NCH):
                    for e_ in ("dve", "pool", "sp"):
                        kb.W(e_, prev)
                    toks = norm_chunk(kb, T, R, XS, c, gbase + 48)
                    at_toks = []

                    def epi(bi, b, tt, ps, tm):
                        ri = R["rl"].nxt()
                        rl = R["rl"].t[ri]
                        kb.W("act", tm, R["rl"].free[ri])
                        t1 = kb.I("act", lambda e: e.activation(out=rl[:], in_=ps[:, :], func=AF.Relu), sig=True)
                        si = R["stg"].nxt()
                        sg = R["stg"].t[si]
                        kb.W("pool", t1, R["stg"].free[si])
                        t2 = kb.I("pool", lambda e: e.tensor_tensor(out=sg[:], in0=rl[:], in1=rl[:], op=ALU.mult), sig=True)
                        R["rl"].free[ri] = t2
                        kb.W("sp", t2)
                        td = kb.DMA("sp", R["stg"].sem[si], AT[b, :, tt * 512:(tt + 1) * 512], sg[:])
                        R["stg"].free[si] = td
                        at_toks.append(td)
                        return t1
                    tl = proj_fm(kb, T, R, W[li, "up"], list(range(64)), 16, big, CH, toks, epi)
                    kb.W("sp", at_toks[-12:], tl)
                    kb.W("sp", [(s, kb.cnt[s]) for s in R["stg"].sem])
                    for st4 in range(4):
                        kb.W("sp", ("pe", kb.cnt["pe"]))
                        a3 = big[:].rearrange("p a (b t) -> p (a b) t", t=512)
                        tla = []
                        for q4 in range(4):
                            tla.append(kb.DMA("sp", R["stg"].sem[0], a3[:, q4 * 16:(q4 + 1) * 16, :], AT[q4 * 16:(q4 + 1) * 16, :, st4 * 512:(st4 + 1) * 512].rearrange("k p t -> p k t")))
                        for b in range(16):
                            wi = R["wd"].nxt()
                            wt = R["wd"].t[wi]
                            kb.W("pool", R["wd"].free[wi])
                            tw = kb.DMA("pool", R["wd"].sem[wi], wt[:, :], W[li, "dn"][b, :, :])
                            kb.W("pe", tw, tla)
                            pb = R["pm"].nxt()
                            ps = R["pm"].t[pb]
                            kb.W("pe", R["pm"].free[pb])
                            for kc in range(64):
                                tmm = kb.I("pe", lambda e, kc=kc: e.matmul(ps[:, :], wt[:, kc * 128:(kc + 1) * 128], a3[:, kc, :], start=(kc == 0), stop=(kc == 63)), sig=(kc == 63))
                            R["wd"].free[wi] = tmm
                            sl = slice(c * CH + st4 * 512, c * CH + (st4 + 1) * 512)
                            R["pm"].free[pb] = epi_resid(kb, T, R, ps, tmm, XS[b, :, sl], xout[b, :, sl])
                    prev = ("pe", kb.cnt["pe"])
                end_phase(kb, T, [])

            phase({
                "big": ([128, 16, CH], BF16, "sb"),
                "xs0": ([128, 16, 256], F32, "sb"), "xs1": ([128, 16, 256], F32, "sb"),
                "sq0": ([128, 16, 256], BF16, "sb"),
                "rs0": ([128, 512], F32, "sb"), "rs1": ([128, 512], F32, "sb"),
                "wr0": ([128, 2048], BF16, "sb"), "wr1": ([128, 2048], BF16, "sb"), "wr2": ([128, 2048], BF16, "sb"),
                "wd0": ([128, 8192], BF16, "sb"), "wd1": ([128, 8192], BF16, "sb"),
                "rl0": ([128, 512], F32, "sb"), "rl1": ([128, 512], F32, "sb"),
                "stg0": ([128, 512], BF16, "sb"), "stg1": ([128, 512], BF16, "sb"), "stg2": ([128, 512], BF16, "sb"),
                "xr0": ([128, 512], F32, "sb"), "xr1": ([128, 512], F32, "sb"), "xr2": ([128, 512], F32, "sb"),
                "pn0": ([128, 512], F32, "ps"),
                "pm0": ([128, 512], F32, "ps"), "pm1": ([128, 512], F32, "ps"), "pm2": ([128, 512], F32, "ps"), "pm3": ([128, 512], F32, "ps"),
            }, mlp_script)
    return nc


def _blk(w, KC):
    K, M = w.shape
    return np.ascontiguousarray(w.reshape(KC, 128, M // 128, 128).transpose(2, 1, 0, 3).reshape(M // 128, 128, KC * 128))


def _blkv(w):
    K, M = w.shape
    return np.ascontiguousarray(w.reshape(16, 128, M // 512, 512).transpose(2, 1, 0, 3).reshape(M // 512, 128, 16 * 512))


def _t5_bucket(rel):
    nb = 16
    max_exact = 8
    ret = np.where(rel > 0, nb, 0)
    n = np.abs(rel)
    n_f = np.maximum(n, 1).astype(np.float32)
    large = max_exact + (np.log(n_f / np.float32(max_exact)) / np.float32(math.log(1024 / max_exact)) * np.float32(nb - max_exact)).astype(np.int32)
    large = np.minimum(large, nb - 1)
    return ret + np.where(n < max_exact, n, large)


def _dil_bias(t5_table):
    out = np.full((NH, 128, 27 * 128), NEG, np.float32)
    j = np.arange(128)[:, None]
    i = np.arange(128)[None, :]
    s = 0
    for g, (win, d) in enumerate(DIL):
        for dl in range(-REACH_DIL[g], REACH_DIL[g] + 1):
            rel = 128 * dl + j - i
            ok = (rel % d == 0) & (np.abs(rel) <= 64 * d)
            bk = _t5_bucket(rel)
            vals = t5_table[bk, g, :]
            blk = np.where(ok[None], vals.transpose(2, 0, 1), np.float32(NEG))
            out[:, :, s * 128:(s + 1) * 128] = blk
            s += 1
    return out


def _na_bias(rpb):
    out = np.full((NH, 128, 5, 7, 128), NEG, np.float32)
    R = 32
    kc = np.arange(64)
    cs = np.clip(kc - 8, 0, 48)
    col_ok = (kc[None, :] >= cs[:, None]) & (kc[None, :] < cs[:, None] + 16)
    dc = np.clip(kc[None, :] - kc[:, None], -15, 15) + 15
    for cls in range(5):
        qt = {0: 8, 1: 0, 2: 1, 3: 14, 4: 15}[cls]
        for si, dl in enumerate(range(-3, 4)):
            for a in range(2):
                r = 2 * qt + a
                if cls == 0:
                    rs = r - 4
                else:
                    rs = min(max(r - 4, 0), R - 8)
                for b in range(2):
                    kr = 2 * (qt + dl) + b
                    if kr < rs or kr >= rs + 8:
                        continue
                    dr = kr - r + 7
                    vals = rpb[:, dr, :][:, dc]
                    blk = np.where(col_ok[None], vals, np.float32(NEG)).transpose(0, 2, 1)
                    out[:, b * 64:(b + 1) * 64, cls, si, a * 64:(a + 1) * 64] = blk
    return np.ascontiguousarray(out.reshape(NH, 128, 5 * 7 * 128))


def _prep_weights(inp, LAYERS):
    Wd = {}
    f = np.float32
    for li, L in enumerate(LAYERS):
        l2 = L // 2
        if L % 2 == 0:
            w = inp["w_qkv_a"][l2]
            wq, wk, wv = w[:, 0:2048], w[:, 2048:4096], w[:, 4096:6144]
            Wd["wqk%d" % li] = _blk(np.concatenate([wq, wk], 1), 16)
            Wd["wv%d" % li] = _blkv(wv)
            Wd["bias%d" % li] = _na_bias(inp["rpb_a"][l2])
            Wd["wo%d" % li] = _blk(inp["w_o_a"][l2], 16)
        else:
            w = inp["w_qkv_b"][l2].reshape(2048, 3, 3, 2048)
            qk = np.concatenate([np.concatenate([w[:, g, 0], w[:, g, 1]], 1) for g in range(3)], 1)
            Wd["wqk%d" % li] = _blk(qk, 16)
            Wd["wv%d" % li] = _blkv(np.concatenate([w[:, g, 2] for g in range(3)], 1))
            Wd["bias%d" % li] = _dil_bias(inp["t5_table"])
            Wd["wo%d" % li] = _blk(inp["w_o_b"][l2], 16)
        Wd["wxq%d" % li] = _blk(inp["w_q_x"][L], 16)
        wkv = inp["w_kv_x"][L]
        Wd["wxk%d" % li] = _blk(wkv[:, 0:512], 16)
        Wd["wxv%d" % li] = _blkv(wkv[:, 512:1024])
        Wd["wxo%d" % li] = _blk(inp["w_o_x"][L], 4)
        Wd["wup%d" % li] = _blk(inp["w_up"][L], 16)
        Wd["wdn%d" % li] = _blk(inp["w_down"][L], 64)
    return Wd


def _vec_inputs(inp, LAYERS):
    gv = np.zeros((128, 4 * len(LAYERS) * 16), np.float32)
    hv = np.zeros((128, len(LAYERS) * 16), np.float32)
    for li, L in enumerate(LAYERS):
        for j, nm in enumerate(("g_mix", "g_cross", "g_mem", "g_mlp")):
            gv[:, li * 64 + j * 16: li * 64 + (j + 1) * 16] = inp[nm][L].reshape(16, 128).T
        l2 = L // 2
        if L % 2 == 0:
            hv[:, li * 16 + 0] = inp["q_norm_a"][l2]
            hv[:, li * 16 + 1] = inp["k_norm_a"][l2]
        else:
            for g in range(3):
                hv[:, li * 16 + 2 * g] = inp["q_norm_b"][l2, g]
                hv[:, li * 16 + 2 * g + 1] = inp["k_norm_b"][l2, g]
        hv[:, li * 16 + 6] = inp["q_norm_x"][L]
        hv[:, li * 16 + 7] = inp["k_norm_x"][L]
    return gv, hv


def _flags(kind, NCH):
    fl = np.zeros((128, 32), np.float32)
    for c in range(NCH):
        if kind == "prompt":
            top = 1.0 if c == 0 else 0.0
            bot = 1.0 if c == NCH - 1 else 0.0
        else:
            top = bot = 1.0
        fl[:, c] = top
        fl[:, 4 + c] = 1.0 - top
        fl[:, 8 + c] = bot
        fl[:, 12 + c] = 1.0 - bot
        fl[:, 16 + c] = 1.0 - top
        fl[:, 20 + c] = 1.0 - bot
    return fl


_CACHE = {}


def _run(inp, NCH, LAYERS, seqs, DBG_STOP=None, ncores=8):
    key = (NCH, tuple(LAYERS), DBG_STOP)
    if key not in _CACHE:
        _CACHE[key] = build(NCH, LAYERS, DBG_STOP)
    nc = _CACHE[key]
    Wd = _prep_weights(inp, LAYERS)
    gv, hv = _vec_inputs(inp, LAYERS)
    in_maps = []
    for (kind, xT, memT) in seqs:
        m = dict(Wd)
        m["xT"] = np.ascontiguousarray(xT.reshape(16, 128, NCH * CH))
        m["memT"] = np.ascontiguousarray(memT.reshape(NCH, 16, 128, 256))
        m["gvec"] = gv
        m["hvec"] = hv
        m["flags"] = _flags(kind, NCH)
        m["onesm"] = np.ones((128, 128), np.float32)
        in_maps.append(m)
    while len(in_maps) < ncores:
        in_maps.append(in_maps[-1])
    res = run_bass_kernel_spmd(nc, in_maps, core_ids=list(range(ncores)))
    return [r["yT"].reshape(2048, NCH * CH) for r in res.results]


def kernel(**inp):
    inp = {k: np.asarray(v) for k, v in inp.items()}
    LAYERS = [0, 1, 2, 3]
    NCH = 4
    xp = inp["x_prompt"][0]
    xs = inp["x_sample"].reshape(4 * 2048, 2048)
    memp = np.broadcast_to(inp["mem_prompt"][0].T[None], (4, 2048, 256))
    mems = inp["mem_sample"].transpose(0, 2, 1)
    seqs = [("prompt", np.ascontiguousarray(xp.T), np.ascontiguousarray(memp)),
            ("sample", np.ascontiguousarray(xs.T), np.ascontiguousarray(mems))]
    outs = _run(inp, NCH, LAYERS, seqs, ncores=2)
    y_prompt = np.ascontiguousarray(outs[0].T).reshape(1, 8192, 2048).astype(np.float32)
    y_sample = np.ascontiguousarray(outs[1].T).reshape(4, 2048, 2048).astype(np.float32)
    return (y_prompt, y_sample)
```

```python
import math
import contextlib
import numpy as np
import concourse.bass as bass
import concourse.mybir as mybir
from concourse.bass_utils import run_bass_kernel_spmd

F32 = mybir.dt.float32
BF16 = mybir.dt.bfloat16
AF = mybir.ActivationFunctionType
ALU = mybir.AluOpType

D = 2048
DH = 128
NH = 16
CH = 2048
NQT = CH // 128
PAD = 1024
EPS = 1e-6
SCALE = 1.0 / math.sqrt(DH)
NEG = -30000.0
DIL = ((128, 1), (512, 4), (2048, 16))
REACH_DIL = (1, 2, 8)
REACH_NA = 3
ENGS = ("sp", "act", "pool", "pe", "dve")


class KB:
    def __init__(self, nc, sems):
        self.nc = nc
        self.sems = sems
        self.cnt = {k: 0 for k in sems}
        self.cur = None
        self.E = None
        self.waited = {}
        self.free_sems = [k for k in sems if k.startswith("r")]
        self.ring_sem = {}

    def I(self, eng, fn, sig=False):
        tok = None
        if sig:
            self.cnt[eng] += 1
            tok = (eng, self.cnt[eng])
        if self.cur == eng:
            ins = fn(self.E)
            if sig:
                ins.then_inc(self.sems[eng], 1)
        return tok

    def DMA(self, q, sem, out, in_):
        self.cnt[sem] += 16
        tok = (sem, self.cnt[sem])
        if self.cur == q:
            self.E.dma_start(out=out, in_=in_).then_inc(self.sems[sem], 16)
        return tok

    def W(self, eng, *toks):
        if self.cur != eng:
            return
        for t in toks:
            if t is None:
                continue
            if isinstance(t, list):
                self.W(eng, *t)
                continue
            s, v = t
            if self.waited.get((eng, s), 0) < v:
                self.E.wait_ge(self.sems[s], v)
                self.waited[(eng, s)] = v

    def state(self):
        return (dict(self.cnt),)

    def restore(self, st):
        self.cnt = dict(st[0])


class Ring:
    def __init__(self, kb, name, tiles, semnames):
        self.kb = kb
        self.t = tiles
        self.n = len(tiles)
        self.sem = semnames
        self.reset()

    def reset(self):
        self.use = 0
        self.free = [None] * self.n

    def nxt(self):
        i = self.use % self.n
        self.use += 1
        return i


def run_phase(nc, kb, tensors, nsem_rings, script):
    with contextlib.ExitStack() as st:
        T = {}
        for name, (shape, dt, kind) in tensors.items():
            if kind == "sb":
                T[name] = st.enter_context(nc.sbuf_tensor(name + "_%d" % kb.phase_id, shape, dt))
            else:
                T[name] = st.enter_context(nc.psum_tensor(name + "_%d" % kb.phase_id, shape, dt))
        block = st.enter_context(nc.Block())
        s0 = kb.state()

        def mk(engname):
            def f(e):
                kb.restore(s0)
                kb.cur = engname
                kb.E = e
                script(kb, T)
                kb.cur = None
            return f

        block.sync(mk("sp"))
        block.scalar(mk("act"))
        block.gpsimd(mk("pool"))
        block.tensor(mk("pe"))
        block.vector(mk("dve"))
    kb.phase_id += 1


def end_phase(kb, T, extra):
    toks = list(extra)
    toks.append(kb.I("act", lambda e: e.activation(out=T["scr"][:, 0:1], in_=T["scr"][:, 1:2], func=AF.Copy), sig=True))
    toks.append(kb.I("dve", lambda e: e.tensor_copy(out=T["scr"][:, 2:3], in_=T["scr"][:, 3:4]), sig=True))
    toks.append(kb.I("pool", lambda e: e.tensor_copy(out=T["scr"][:, 4:5], in_=T["scr"][:, 5:6]), sig=True))
    for s in kb.cnt:
        if s.startswith("r") and kb.cnt[s] > 0:
            toks.append((s, kb.cnt[s]))
    kb.W("pe", *toks)
    toks.append(kb.I("pe", lambda e: e.matmul(T["pm0"][:, 0:8], T["ones"][:, 0:128], T["ones"][:, 0:8], start=True, stop=True), sig=True))
    for s in kb.cnt:
        if s.startswith("r") and kb.cnt[s] > 0:
            toks.append((s, kb.cnt[s]))
    for e in ENGS:
        kb.W(e, *toks)


def build(NCH, LAYERS, DBG_STOP=None):
    T_ = NCH * CH
    TP = T_ + 2 * PAD
    nc = bass.Bass("TRN2", target_bir_lowering=False)

    def din(name, shape, dt=F32):
        return nc.dram_tensor(name, list(shape), dt, kind="ExternalInput")

    xT_in = din("xT", [16, 128, T_])
    memT_in = din("memT", [NCH, 16, 128, 256])
    gvec_in = din("gvec", [128, 4 * len(LAYERS) * 16])
    hvec_in = din("hvec", [128, len(LAYERS) * 16])
    flags_in = din("flags", [128, 32])
    ones_in = din("onesm", [128, 128])
    W = {}
    for li, L in enumerate(LAYERS):
        if L % 2 == 0:
            W[li, "qk"] = din("wqk%d" % li, [32, 128, 16 * 128])
            W[li, "v"] = din("wv%d" % li, [4, 128, 16 * 512])
            W[li, "bias"] = din("bias%d" % li, [NH, 128, 5 * 7 * 128])
        else:
            W[li, "qk"] = din("wqk%d" % li, [96, 128, 16 * 128])
            W[li, "v"] = din("wv%d" % li, [12, 128, 16 * 512])
            W[li, "bias"] = din("bias%d" % li, [NH, 128, 27 * 128])
        W[li, "o"] = din("wo%d" % li, [16, 128, 16 * 128])
        W[li, "xq"] = din("wxq%d" % li, [4, 128, 16 * 128])
        W[li, "xk"] = din("wxk%d" % li, [4, 128, 16 * 128])
        W[li, "xv"] = din("wxv%d" % li, [1, 128, 16 * 512])
        W[li, "xo"] = din("wxo%d" % li, [16, 128, 4 * 128])
        W[li, "up"] = din("wup%d" % li, [64, 128, 16 * 128])
        W[li, "dn"] = din("wdn%d" % li, [16, 128, 64 * 128])
    yT = nc.dram_tensor("yT", [16, 128, T_], F32, kind="ExternalOutput")
    XS = yT if DBG_STOP is not None else nc.dram_tensor("xscr", [16, 128, T_], F32)
    QT = [nc.dram_tensor("qt%d" % g, [NH, 128, T_], BF16) for g in range(3)]
    KT = [nc.dram_tensor("kt%d" % g, [NH, 128, TP], BF16) for g in range(3)]
    VV = [nc.dram_tensor("vv%d" % g, [NH, TP, 128], BF16) for g in range(3)]
    AT = nc.dram_tensor("at", [64, 128, CH], BF16)

    semnames = list(ENGS) + ["r%d" % i for i in range(64)]
    with contextlib.ExitStack() as st:
        sems = {k: st.enter_context(nc.semaphore("s_" + k)) for k in semnames}
        kb = KB(nc, sems)
        kb.phase_id = 0
        rsem = iter(["r%d" % i for i in range(64)])
        RS = {}

        def ringsem(name, n):
            if name not in RS:
                RS[name] = [next(rsem) for _ in range(n)]
            return RS[name]

        ones = st.enter_context(nc.sbuf_tensor("ones", [128, 128], BF16))
        gvec = st.enter_context(nc.sbuf_tensor("gvec_sb", [128, 4 * len(LAYERS) * 16], F32))
        hvec = st.enter_context(nc.sbuf_tensor("hvec_sb", [128, len(LAYERS) * 16], F32))
        flags = st.enter_context(nc.sbuf_tensor("flags_sb", [128, 32], F32))
        scr = st.enter_context(nc.sbuf_tensor("scr", [128, 8], F32))
        zer = st.enter_context(nc.sbuf_tensor("zer", [128, 2048], BF16))
        epsb = st.enter_context(nc.sbuf_tensor("epsb", [128, 2], F32))
        COMMON = {"epsb": epsb, "ones": ones, "gvec": gvec, "hvec": hvec, "flags": flags, "scr": scr, "zer": zer}

        def phase(tensors, script):
            tensors = dict(tensors)
            if "pm0" not in tensors:
                tensors["pm0"] = ([128, 512], F32, "ps")

            def s2(kb, T):
                T = dict(T)
                T.update(COMMON)
                if "ps0" not in T and "ps1" in T:
                    T["ps0"] = T["pm0"]
                script(kb, T)
            run_phase(nc, kb, tensors, 0, s2)

        def init_script(kb, T):
            s = ringsem("init", 1)[0]
            t = []
            t.append(kb.DMA("pool", s, T["ones"][:], ones_in[:, :]))
            t.append(kb.DMA("sp", s, T["gvec"][:], gvec_in[:, :]))
            t.append(kb.DMA("sp", s, T["hvec"][:], hvec_in[:, :]))
            t.append(kb.DMA("sp", s, T["flags"][:], flags_in[:, :]))
            kb.W("dve", t[1], t[2])
            kb.I("dve", lambda e: e.tensor_scalar(out=T["gvec"][:], in0=T["gvec"][:], scalar1=float(math.sqrt(D)), scalar2=None, op0=ALU.mult), sig=True)
            hk = T["hvec"][:].rearrange("p (a two) -> p a two", two=2)[:, :, 1]
            kb.I("dve", lambda e: e.tensor_scalar(out=hk, in0=hk, scalar1=float(math.sqrt(DH)), scalar2=None, op0=ALU.mult), sig=True)
            kb.W("pe", t[0])
            t1 = kb.I("dve", lambda e: e.memset(T["zer"][:], 0.0), sig=True)
            kb.I("dve", lambda e: e.memset(T["epsb"][:, 0:1], float(D * EPS)), sig=True)
            kb.I("dve", lambda e: e.memset(T["epsb"][:, 1:2], float(DH * EPS)), sig=True)
            t2 = kb.I("dve", lambda e: e.memset(T["scr"][:], 0.0), sig=True)
            kb.W("sp", t1)
            for g in range(3):
                for h in range(NH):
                    kb.DMA("sp", s, KT[g][h, :, 0:PAD], T["zer"][:, 0:PAD])
                    kb.DMA("sp", s, KT[g][h, :, PAD + T_:TP], T["zer"][:, 0:PAD])
                    kb.DMA("sp", s, VV[g][h, 0:PAD, :].rearrange("(a p) d -> p a d", p=128), T["zer"][:, 0:PAD].rearrange("p (a d) -> p a d", d=128))
                    kb.DMA("sp", s, VV[g][h, PAD + T_:TP, :].rearrange("(a p) d -> p a d", p=128), T["zer"][:, 0:PAD].rearrange("p (a d) -> p a d", d=128))
            end_phase(kb, T, [t2])

        phase({}, init_script)

        class Ctx:
            pass

        def mk_rings(kb, T, spec):
            R = {}
            for name, n in spec.items():
                R[name] = Ring(kb, name, [T["%s%d" % (name, i)] for i in range(n)], ringsem(name, n))
            return R

        def norm_chunk(kb, T, R, src, c, gcol, ntok=CH, tok0=None, dst=None, srcap=None):
            dst = T["big"] if dst is None else dst
            done = []
            nsub = ntok // 256
            for j in range(nsub):
                i = R["xs"].nxt()
                xs = R["xs"].t[i]
                kb.W("sp", R["xs"].free[i])
                if srcap is None:
                    a = src[:, :, c * CH + j * 256: c * CH + (j + 1) * 256].rearrange("k p t -> p k t")
                else:
                    a = srcap[:, :, j * 256:(j + 1) * 256].rearrange("k p t -> p k t")
                tl = kb.DMA("sp", R["xs"].sem[i], xs[:], a)
                kb.W("act", tl)
                qi = R["sq"].nxt()
                sq = R["sq"].t[qi]
                kb.W("act", R["sq"].free[qi])
                t1 = kb.I("act", lambda e: e.activation(out=sq[:], in_=xs[:], func=AF.Square), sig=True)
                pb = R["pn"].nxt()
                ps = R["pn"].t[pb]
                kb.W("pe", t1, R["pn"].free[pb])
                for kc in range(16):
                    t2 = kb.I("pe", lambda e, kc=kc: e.matmul(ps[:, 0:256], T["ones"][:, :], sq[:, kc, :], start=(kc == 0), stop=(kc == 15)), sig=(kc == 15))
                R["sq"].free[qi] = t2
                ri = R["rs"].nxt()
                rs = R["rs"].t[ri]
                kb.W("dve", t2, R["rs"].free[ri])
                kb.W("act", ("pe", kb.cnt["pe"]), R["rs"].free[ri])
                t3a = kb.I("act", lambda e: e.activation(out=rs[:, 0:256], in_=ps[:, 0:256], func=AF.Sqrt, bias=T["epsb"][:, 0:1], scale=1.0), sig=True)
                kb.W("dve", t3a)
                t3 = kb.I("dve", lambda e: e.reciprocal(out=rs[:, 0:256], in_=rs[:, 0:256]), sig=True)
                kb.W("dve", t3)
                R["pn"].free[pb] = t3
                kb.W("dve", t3, tl)
                last = []
                for kc in range(16):
                    eng = "dve"
                    tk = kb.I(eng, lambda e, kc=kc: e.scalar_tensor_tensor(
                        out=dst[:, kc, j * 256:(j + 1) * 256], in0=xs[:, kc, :], scalar=T["gvec"][:, gcol + kc:gcol + kc + 1],
                        in1=rs[:, 0:256], op0=ALU.mult, op1=ALU.mult), sig=(kc == 15))
                    if tk is not None:
                        last.append(tk)
                R["xs"].free[i] = last
                R["rs"].free[ri] = last
                done += last
            return done

        def proj_fm(kb, T, R, wd, blocks, KC, src, ntok, src_toks, epi, wview=None):
            ntt = ntok // 512
            pend = None
            for bi, b in enumerate(blocks):
                wi = R["wr"].nxt()
                wt = R["wr"].t[wi]
                kb.W("pool", R["wr"].free[wi])
                tw = kb.DMA("pool", R["wr"].sem[wi], wt[:, 0:KC * 128], wd[b, :, :])
                kb.W("pe", tw, src_toks)
                for tt in range(ntt):
                    pb = R["pm"].nxt()
                    ps = R["pm"].t[pb]
                    kb.W("pe", R["pm"].free[pb])
                    for kc in range(KC):
                        tm = kb.I("pe", lambda e, kc=kc, tt=tt: e.matmul(ps[:, :], wt[:, kc * 128:(kc + 1) * 128], src[:, kc, tt * 512:(tt + 1) * 512], start=(kc == 0), stop=(kc == KC - 1)), sig=(kc == KC - 1))
                    R["pm"].free[pb] = epi(bi, b, tt, ps, tm)
                R["wr"].free[wi] = tm
            return tm

        def epi_qknorm(kb, T, R, ps, tm, gain_ap, dst_ap):
            qi = R["sq2"].nxt()
            sq = R["sq2"].t[qi]
            kb.W("act", tm, R["sq2"].free[qi])
            t1 = kb.I("act", lambda e: e.activation(out=sq[:], in_=ps[:, :], func=AF.Square), sig=True)
            pb = R["pn"].nxt()
            ps2 = R["pn"].t[pb]
            kb.W("pe", t1, R["pn"].free[pb])
            t2 = kb.I("pe", lambda e: e.matmul(ps2[:, :], T["ones"][:, :], sq[:], start=True, stop=True), sig=True)
            R["sq2"].free[qi] = t2
            ri = R["rs"].nxt()
            rs = R["rs"].t[ri]
            kb.W("dve", t2, R["rs"].free[ri], tm)
            kb.W("act", ("pe", kb.cnt["pe"]), R["rs"].free[ri])
            t3a = kb.I("act", lambda e: e.activation(out=rs[:, :], in_=ps2[:, :], func=AF.Sqrt, bias=T["epsb"][:, 1:2], scale=1.0), sig=True)
            kb.W("dve", t3a)
            t3 = kb.I("dve", lambda e: e.reciprocal(out=rs[:, :], in_=rs[:, :]), sig=True)
            kb.W("dve", t3)
            R["pn"].free[pb] = t3
            si = R["stg"].nxt()
            sg = R["stg"].t[si]
            kb.W("dve", R["stg"].free[si], t3)
            t4 = kb.I("dve", lambda e: e.scalar_tensor_tensor(out=sg[:], in0=ps[:, :], scalar=gain_ap, in1=rs[:, :], op0=ALU.mult, op1=ALU.mult), sig=True)
            R["rs"].free[ri] = t4
            if dst_ap is not None:
                kb.W("sp", t4)
                R["stg"].free[si] = kb.DMA("sp", R["stg"].sem[si], dst_ap, sg[:])
            return t4, sg, si

        def epi_resid(kb, T, R, ps, tm, src_ap, dst_ap):
            xi = R["xr"].nxt()
            xr = R["xr"].t[xi]
            kb.W("sp", R["xr"].free[xi])
            tl = kb.DMA("sp", R["xr"].sem[xi], xr[:], src_ap)
            kb.W("dve", tl, tm)
            t1 = kb.I("dve", lambda e: e.tensor_tensor(out=xr[:], in0=ps[:, :], in1=xr[:], op=ALU.add), sig=True)
            kb.W("sp", t1)
            R["xr"].free[xi] = kb.DMA("sp", R["xr"].sem[xi], dst_ap, xr[:])
            return t1

        def proj_tm(kb, T, R, wd, g, src, ntiles, src_toks, epi):
            wi = R["wv"].nxt()
            wt = R["wv"].t[wi]
            kb.W("pool", R["wv"].free[wi])
            tw = kb.DMA("pool", R["wv"].sem[wi], wt[:], wd[g, :, :])
            kb.W("pe", tw, src_toks)
            for t16 in range(ntiles):
                pb = R["pm"].nxt()
                ps = R["pm"].t[pb]
                kb.W("pe", R["pm"].free[pb])
                for kc in range(16):
                    tm = kb.I("pe", lambda e, kc=kc, t16=t16: e.matmul(ps[:, :], src[:, kc, t16 * 128:(t16 + 1) * 128], wt[:, kc * 512:(kc + 1) * 512], start=(kc == 0), stop=(kc == 15)), sig=(kc == 15))
                R["pm"].free[pb] = epi(t16, ps, tm)
            R["wv"].free[wi] = tm

        def attention_head(kb, T, R, nqt, slots, active, qtile, ksrc, vsrc, den_lhsT, bias_fn, out_ap, first_dep, scale=None, LA=3):
            units = []
            nb = (len(slots) + 3) // 4
            for i in range(nqt):
                ulist = []
                for bi in range(nb):
                    bs = slots[bi * 4:(bi + 1) * 4]
                    act = [active(i, g, dl) for (g, dl) in bs]
                    if any(act):
                        ulist.append([i, bi, bs, act, False, False])
                ulist[0][4] = True
                ulist[-1][5] = True
                units += ulist
            state = {}
            cur = {}
            res = {"last": None}

            def emit_S(u):
                i, bi, bs, act, fb, lb = units[u]
                n = len(bs)
                si = R["ps"].nxt()
                ps = R["ps"].t[si]
                kb.W("pe", R["ps"].free[si])
                lastj = max(j for j in range(n) if act[j])
                for j, (g, dl) in enumerate(bs):
                    if not act[j]:
                        continue
                    tS = kb.I("pe", lambda e, j=j, g=g, dl=dl: e.matmul(ps[:, j * 128:(j + 1) * 128], ksrc(g, dl, i), qtile(g, i), start=True, stop=True), sig=(j == lastj))
                pi = R["pt"].nxt()
                pt = R["pt"].t[pi]
                if bias_fn is not None:
                    ti = R["tm"].nxt()
                    tmp = R["tm"].t[ti]
                    kb.W("dve", tS, R["tm"].free[ti])
                    tB = bias_fn(i, bi, bs, tmp, ps)
                    R["ps"].free[si] = tB
                    kb.W("act", tB, R["pt"].free[pi])
                    tE = kb.I("act", lambda e: e.activation(out=pt[:, 0:n * 128], in_=tmp[:, 0:n * 128], func=AF.Exp), sig=True)
                    R["tm"].free[ti] = tE
                else:
                    kb.W("act", tS, R["pt"].free[pi])
                    tE = kb.I("act", lambda e: e.activation(out=pt[:, 0:n * 128], in_=ps[:, 0:n * 128], func=AF.Exp, scale=float(scale)), sig=True)
                    R["ps"].free[si] = tE
                state[u] = (pi, pt, tE)

            def emit_PV(u):
                i, bi, bs, act, fb, lb = units[u]
                n = len(bs)
                pi, pt, tE = state.pop(u)
                if fb:
                    oi = R["po"].nxt()
                    cur["oi"] = oi
                    kb.W("pe", R["po"].free[oi])
                oi = cur["oi"]
                po = R["po"].t[oi]
                pd = R["pd"].t[oi]
                kb.W("pe", tE)
                js = [j for j in range(n) if act[j]]
                for j in js:
                    g, dl = bs[j]
                    st_ = (fb and j == js[0])
                    sp_ = (lb and j == js[-1])
                    kb.I("pe", lambda e, j=j, g=g, dl=dl, st_=st_, sp_=sp_: e.matmul(po[:, 0:128], vsrc(g, dl, i), pt[:, j * 128:(j + 1) * 128], start=st_, stop=sp_))
                    tP = kb.I("pe", lambda e, j=j, g=g, dl=dl, st_=st_, sp_=sp_: e.matmul(pd[:, 0:128], den_lhsT(g, dl, i), pt[:, j * 128:(j + 1) * 128], start=st_, stop=sp_), sig=(j == js[-1]))
                R["pt"].free[pi] = tP
                if lb:
                    ri = R["rc"].nxt()
                    rc = R["rc"].t[ri]
                    kb.W("dve", tP, R["rc"].free[ri], first_dep if i == 0 else None)
                    t1 = kb.I("dve", lambda e: e.reciprocal(out=rc[:, :], in_=pd[:, 0:128]), sig=True)
                    kb.W("dve", t1)
                    t2 = kb.I("dve", lambda e: e.tensor_tensor(out=out_ap(i), in0=po[:, 0:128], in1=rc[:, :], op=ALU.mult), sig=True)
                    R["rc"].free[ri] = t2
                    R["po"].free[oi] = t2
                    res["last"] = t2

            nu = len(units)
            for u in range(min(LA, nu)):
                emit_S(u)
            for u in range(nu):
                if u + LA < nu:
                    emit_S(u + LA)
                emit_PV(u)
            return res["last"]

        xsrc = xT_in
        for li, L in enumerate(LAYERS):
            is_na = (L % 2 == 0)
            ng = 1 if is_na else 3
            gbase = li * 64
            hbase = li * 16
            last_layer = (li == len(LAYERS) - 1)

            def qkv_script(kb, T, li=li, is_na=is_na, ng=ng, gbase=gbase, hbase=hbase, xsrc=xsrc):
                R = mk_rings(kb, T, {"xs": 2, "sq": 1, "pn": 2, "rs": 2, "wr": 3, "pm": 4, "sq2": 2, "stg": 3, "wv": 2, "vst": 2})
                big = T["big"]
                prev = None
                for c in range(NCH):
                    for e_ in ("dve", "pool"):
                        kb.W(e_, prev)
                    toks = norm_chunk(kb, T, R, xsrc, c, gbase + 0)

                    def epi(bi, b, tt, ps, tm, c=c):
                        g = b // 32
                        isk = (b % 32) >= 16
                        h = b % 16
                        gain = T["hvec"][:, hbase + g * 2 + (1 if isk else 0): hbase + g * 2 + (1 if isk else 0) + 1]
                        if isk:
                            dst = KT[g][h, :, PAD + c * CH + tt * 512: PAD + c * CH + (tt + 1) * 512]
                        else:
                            dst = QT[g][h, :, c * CH + tt * 512: c * CH + (tt + 1) * 512]
                        t4, _, _ = epi_qknorm(kb, T, R, ps, tm, gain, dst)
                        return t4
                    tl1 = proj_fm(kb, T, R, W[li, "qk"], list(range(32 * ng)), 16, big, CH, toks, epi)

                    for vg in range(4 * ng):
                        def epiv(t16, ps, tm, vg=vg, c=c):
                            g = vg // 4
                            vi = R["vst"].nxt()
                            vs = R["vst"].t[vi]
                            kb.W("act", tm, R["vst"].free[vi])
                            t1 = kb.I("act", lambda e: e.activation(out=vs[:], in_=ps[:, :], func=AF.Copy), sig=True)
                            kb.W("sp", t1)
                            r0 = PAD + c * CH + t16 * 128
                            dst = VV[g][(vg % 4) * 4:(vg % 4) * 4 + 4, r0:r0 + 128, :].rearrange("h p d -> p h d")
                            R["vst"].free[vi] = kb.DMA("sp", R["vst"].sem[vi], dst, vs[:].rearrange("p (h d) -> p h d", d=128))
                            return t1
                        proj_tm(kb, T, R, W[li, "v"], vg, big, 16, toks, epiv)
                    prev = ("pe", kb.cnt["pe"])
                end_phase(kb, T, [])

            phase({
                "big": ([128, 16, CH], BF16, "sb"),
                "xs0": ([128, 16, 256], F32, "sb"), "xs1": ([128, 16, 256], F32, "sb"),
                "sq0": ([128, 16, 256], BF16, "sb"),
                "rs0": ([128, 512], F32, "sb"), "rs1": ([128, 512], F32, "sb"),
                "wr0": ([128, 2048], BF16, "sb"), "wr1": ([128, 2048], BF16, "sb"), "wr2": ([128, 2048], BF16, "sb"),
                "wv0": ([128, 8192], BF16, "sb"), "wv1": ([128, 8192], BF16, "sb"),
                "sq20": ([128, 512], BF16, "sb"), "sq21": ([128, 512], BF16, "sb"),
                "stg0": ([128, 512], BF16, "sb"), "stg1": ([128, 512], BF16, "sb"), "stg2": ([128, 512], BF16, "sb"),
                "vst0": ([128, 512], BF16, "sb"), "vst1": ([128, 512], BF16, "sb"),
                "pn0": ([128, 512], F32, "ps"), "pn1": ([128, 512], F32, "ps"),
                "pm0": ([128, 512], F32, "ps"), "pm1": ([128, 512], F32, "ps"), "pm2": ([128, 512], F32, "ps"), "pm3": ([128, 512], F32, "ps"),
            }, qkv_script)
            if DBG_STOP == (li, 1):
                break

            if is_na:
                slots = [(0, d) for d in range(-REACH_NA, REACH_NA + 1)]
                nbias = 5 * 7 * 128
            else:
                slots = [(g, d) for g in range(3) for d in range(-REACH_DIL[g], REACH_DIL[g] + 1)]
                nbias = 27 * 128
            reach = [REACH_NA] if is_na else list(REACH_DIL)
            kw = [CH + 2 * r * 128 for r in reach]
            koff = [sum(kw[:g]) for g in range(ng)]
            vt = [NQT + 2 * r for r in reach]
            voff = [sum(vt[:g]) for g in range(ng)]
            xdst = XS

            def att_script(kb, T, li=li, is_na=is_na, ng=ng, slots=slots, reach=reach, kw=kw, koff=koff, vt=vt, voff=voff, xsrc=xsrc, xdst=xdst, nbias=nbias):
                R = mk_rings(kb, T, {"qk": 2, "vb": 1, "ps": 4, "po": 2, "pd": 2, "pt": 4, "tm": 4, "rc": 2, "wr": 3, "xr": 3})
                R["pm"] = R["ps"]
                if not is_na:
                    tvo = None
                    for idx in range(8):
                        tvo = kb.I("dve", lambda e, idx=idx: e.tensor_scalar(out=T["vo"][:, idx, :], in0=T["ones"][:, :], scalar1=T["flags"][:, 16 + idx:17 + idx], scalar2=None, op0=ALU.mult), sig=True)
                    kb.W("pe", tvo)
                big = T["big"]
                prev_wo = None
                for c in range(NCH):
                    head_toks = []
                    for h in range(NH):
                        qi = R["qk"].nxt()
                        qk = R["qk"].t[qi]
                        kb.W("sp", R["qk"].free[qi])
                        tl = []
                        for g in range(ng):
                            tl.append(kb.DMA("sp", R["qk"].sem[qi], qk[:, g * CH:(g + 1) * CH], QT[g][h, :, c * CH:(c + 1) * CH]))
                            k0 = PAD + c * CH - reach[g] * 128
                            tl.append(kb.DMA("sp", R["qk"].sem[qi], qk[:, 3 * CH + koff[g]: 3 * CH + koff[g] + kw[g]], KT[g][h, :, k0:k0 + kw[g]]))
                        vi = R["vb"].nxt()
                        vb = T["vbuf"]
                        bb = T["bbuf"]
                        kb.W("sp", R["vb"].free[vi])
                        for g in range(ng):
                            r0 = PAD + c * CH - reach[g] * 128
                            tl.append(kb.DMA("sp", R["vb"].sem[vi], vb[:, voff[g]:voff[g] + vt[g], :], VV[g][h, r0:r0 + vt[g] * 128, :].rearrange("(a p) d -> p a d", p=128)))
                        tl.append(kb.DMA("sp", R["vb"].sem[vi], bb[:, 0:nbias], W[li, "bias"][h, :, :]))
                        kb.W("pe", tl)
                        kb.W("dve", tl)
                        kb.W("pool", tl)
                        tmask = []
                        if not is_na:
                            kb.W("pool", tl)
                            for g in range(ng):
                                if c > 0:
                                    tmask.append(kb.I("pool", lambda e, g=g: e.tensor_scalar(out=vb[:, voff[g]:voff[g] + reach[g], :], in0=vb[:, voff[g]:voff[g] + reach[g], :], scalar1=T["flags"][:, 16 + c:17 + c], scalar2=None, op0=ALU.mult), sig=True))
                                if c < NCH - 1:
                                    o_ = voff[g] + reach[g] + NQT
                                    tmask.append(kb.I("pool", lambda e, g=g, o_=o_: e.tensor_scalar(out=vb[:, o_:o_ + reach[g], :], in0=vb[:, o_:o_ + reach[g], :], scalar1=T["flags"][:, 20 + c:21 + c], scalar2=None, op0=ALU.mult), sig=True))
                            kb.W("pe", tmask)

                        def active(i, g, dl):
                            if c == 0 and i + dl < 0:
                                return False
                            if c == NCH - 1 and i + dl >= NQT:
                                return False
                            return True

                        def qtile(g, i):
                            return qk[:, g * CH + i * 128: g * CH + (i + 1) * 128]

                        def ksrc(g, dl, i):
                            o = 3 * CH + koff[g] + (i + dl + reach[g]) * 128
                            return qk[:, o:o + 128]

                        def vsrc(g, dl, i):
                            return vb[:, voff[g] + i + dl + reach[g], :]

                        def den_lhsT(g, dl, i):
                            if is_na or (0 <= i + dl < NQT):
                                return T["ones"][:, :]
                            return T["vo"][:, (c if i + dl < 0 else 4 + c), :]

                        def bias_fn(i, bi, bs, tmp, ps):
                            n = len(bs)
                            off = bi * 512
                            if is_na:
                                cls = {0: 1, 1: 2, NQT - 2: 3, NQT - 1: 4}.get(i, 0)
                                if cls == 0:
                                    return kb.I("dve", lambda e: e.tensor_tensor(out=tmp[:, 0:n * 128], in0=ps[:, 0:n * 128], in1=bb[:, off:off + n * 128], op=ALU.add), sig=True)
                                top = cls in (1, 2)
                                fcol = (0 if top else 8) + c
                                tq = kb.I("dve", lambda e: e.scalar_tensor_tensor(out=tmp[:, 0:n * 128], in0=bb[:, off:off + n * 128], scalar=T["flags"][:, fcol + 4:fcol + 5], in1=ps[:, 0:n * 128], op0=ALU.mult, op1=ALU.add), sig=True)
                                kb.W("dve", tq)
                                return kb.I("dve", lambda e: e.scalar_tensor_tensor(out=tmp[:, 0:n * 128], in0=bb[:, cls * 896 + off:cls * 896 + off + n * 128], scalar=T["flags"][:, fcol:fcol + 1], in1=tmp[:, 0:n * 128], op0=ALU.mult, op1=ALU.add), sig=True)
                            return kb.I("dve", lambda e: e.tensor_tensor(out=tmp[:, 0:n * 128], in0=ps[:, 0:n * 128], in1=bb[:, off:off + n * 128], op=ALU.add), sig=True)

                        lastq = attention_head(kb, T, R, NQT, slots, active, qtile, ksrc, vsrc, den_lhsT, bias_fn,
                                               lambda i: big[:, h, i * 128:(i + 1) * 128], prev_wo if h == 0 else None)
                        pe_done = ("pe", kb.cnt["pe"])
                        R["qk"].free[qi] = [pe_done]
                        R["vb"].free[vi] = [pe_done, lastq]
                        head_toks.append(lastq)

                    def epi(bi, b, tt, ps, tm, c=c):
                        sl = slice(c * CH + tt * 512, c * CH + (tt + 1) * 512)
                        return epi_resid(kb, T, R, ps, tm, xsrc[b, :, sl], xdst[b, :, sl])
                    prev_wo = proj_fm(kb, T, R, W[li, "o"], list(range(16)), 16, big, CH, head_toks[-1], epi)
                end_phase(kb, T, [])

            phase({
                "big": ([128, 16, CH], BF16, "sb"),
                "qk0": ([128, 3 * CH + 8960], BF16, "sb"), "qk1": ([128, 3 * CH + 8960], BF16, "sb"),
                "vb0": ([128, 8], BF16, "sb"),
                "vbuf": ([128, 70, 128], BF16, "sb"), "bbuf": ([128, 5 * 7 * 128], F32, "sb"),
                "pt0": ([128, 512], BF16, "sb"), "pt1": ([128, 512], BF16, "sb"), "pt2": ([128, 512], BF16, "sb"), "pt3": ([128, 512], BF16, "sb"),
                "tm0": ([128, 512], F32, "sb"), "tm1": ([128, 512], F32, "sb"), "tm2": ([128, 512], F32, "sb"), "tm3": ([128, 512], F32, "sb"),
                "vo": ([128, 8, 128], BF16, "sb"),
                "rc0": ([128, 128], F32, "sb"), "rc1": ([128, 128], F32, "sb"),
                "wr0": ([128, 2048], BF16, "sb"), "wr1": ([128, 2048], BF16, "sb"), "wr2": ([128, 2048], BF16, "sb"),
                "xr0": ([128, 512], F32, "sb"), "xr1": ([128, 512], F32, "sb"), "xr2": ([128, 512], F32, "sb"),
                "pm0": ([128, 512], F32, "ps"), "ps1": ([128, 512], F32, "ps"), "ps2": ([128, 512], F32, "ps"), "ps3": ([128, 512], F32, "ps"),
                "po0": ([128, 512], F32, "ps"), "po1": ([128, 512], F32, "ps"),
                "pd0": ([128, 512], F32, "ps"), "pd1": ([128, 512], F32, "ps"),
            }, att_script)
            xsrc = XS
            if DBG_STOP == (li, 2):
                break

            def cross_script(kb, T, li=li, gbase=gbase, hbase=hbase):
                R = mk_rings(kb, T, {"xs": 2, "sq": 1, "pn": 1, "rs": 2, "wr": 3, "pm": 2, "sq2": 2, "stg": 2, "wv": 1, "ps": 1, "po": 2, "pd": 2, "pt": 2, "rc": 2, "xr": 3})
                big = T["big"]
                mt = T["mt"]
                kx = T["kx"]
                vx = T["vx"]
                qx = T["qx"]
                ox = T["ox"]
                prev = None
                for c in range(NCH):
                    for e_ in ("dve", "pool", "act"):
                        kb.W(e_, prev)
                    tm_ = norm_chunk(kb, T, R, None, 0, gbase + 32, ntok=256, dst=mt, srcap=memT_in[c])

                    def epik(bi, b, tt, ps, tm):
                        raise RuntimeError
                    for b in range(4):
                        wi = R["wr"].nxt()
                        wt = R["wr"].t[wi]
                        kb.W("pool", R["wr"].free[wi])
                        tw = kb.DMA("pool", R["wr"].sem[wi], wt[:, :], W[li, "xk"][b, :, :])
                        kb.W("pe", tw, tm_)
                        pb = R["pm"].nxt()
                        ps = R["pm"].t[pb]
                        kb.W("pe", R["pm"].free[pb])
                        for kc in range(16):
                            tmm = kb.I("pe", lambda e, kc=kc: e.matmul(ps[:, 0:256], wt[:, kc * 128:(kc + 1) * 128], mt[:, kc, :], start=(kc == 0), stop=(kc == 15)), sig=(kc == 15))
                        R["wr"].free[wi] = tmm
                        qi = R["sq2"].nxt()
                        sq = R["sq2"].t[qi]
                        kb.W("act", tmm, R["sq2"].free[qi])
                        t1 = kb.I("act", lambda e: e.activation(out=sq[:, 0:256], in_=ps[:, 0:256], func=AF.Square), sig=True)
                        p2 = R["pn"].nxt()
                        ps2 = R["pn"].t[p2]
                        kb.W("pe", t1, R["pn"].free[p2])
                        t2 = kb.I("pe", lambda e: e.matmul(ps2[:, 0:256], T["ones"][:, :], sq[:, 0:256], start=True, stop=True), sig=True)
                        R["sq2"].free[qi] = t2
                        ri = R["rs"].nxt()
                        rs = R["rs"].t[ri]
                        kb.W("dve", t2, R["rs"].free[ri], tmm, prev)
                        kb.W("act", ("pe", kb.cnt["pe"]), R["rs"].free[ri])
                        t3a = kb.I("act", lambda e: e.activation(out=rs[:, 0:256], in_=ps2[:, 0:256], func=AF.Sqrt, bias=T["epsb"][:, 1:2], scale=1.0), sig=True)
                        kb.W("dve", t3a)
                        t3 = kb.I("dve", lambda e: e.reciprocal(out=rs[:, 0:256], in_=rs[:, 0:256]), sig=True)
                        kb.W("dve", t3)
                        R["pn"].free[p2] = t3
                        t4 = kb.I("dve", lambda e, b=b: e.scalar_tensor_tensor(out=kx[:, b, :], in0=ps[:, 0:256], scalar=T["hvec"][:, hbase + 7:hbase + 8], in1=rs[:, 0:256], op0=ALU.mult, op1=ALU.mult), sig=True)
                        R["rs"].free[ri] = t4
                        R["pm"].free[pb] = t4
                    tk_done = t4
                    def epiv(t16, ps, tm):
                        kb.W("act", tm)
                        return kb.I("act", lambda e: e.activation(out=vx[:, t16, :], in_=ps[:, :], func=AF.Copy), sig=True)
                    vtoks = []

                    def epiv2(t16, ps, tm):
                        t = epiv(t16, ps, tm)
                        vtoks.append(t)
                        return t
                    proj_tm(kb, T, R, W[li, "xv"], 0, mt, 2, tm_, epiv2)
                    toks = norm_chunk(kb, T, R, XS, c, gbase + 16)

                    def epiq(bi, b, tt, ps, tm):
                        qi = R["sq2"].nxt()
                        sq = R["sq2"].t[qi]
                        kb.W("act", tm, R["sq2"].free[qi])
                        t1 = kb.I("act", lambda e: e.activation(out=sq[:], in_=ps[:, :], func=AF.Square), sig=True)
                        p2 = R["pn"].nxt()
                        ps2 = R["pn"].t[p2]
                        kb.W("pe", t1, R["pn"].free[p2])
                        t2 = kb.I("pe", lambda e: e.matmul(ps2[:, :], T["ones"][:, :], sq[:], start=True, stop=True), sig=True)
                        R["sq2"].free[qi] = t2
                        ri = R["rs"].nxt()
                        rs = R["rs"].t[ri]
                        kb.W("dve", t2, R["rs"].free[ri], tm)
                        kb.W("act", ("pe", kb.cnt["pe"]), R["rs"].free[ri])
                        t3a = kb.I("act", lambda e: e.activation(out=rs[:, :], in_=ps2[:, :], func=AF.Sqrt, bias=T["epsb"][:, 1:2], scale=1.0), sig=True)
                        kb.W("dve", t3a)
                        t3 = kb.I("dve", lambda e: e.reciprocal(out=rs[:, :], in_=rs[:, :]), sig=True)
                        kb.W("dve", t3)
                        R["pn"].free[p2] = t3
                        t4 = kb.I("dve", lambda e: e.scalar_tensor_tensor(out=qx[:, b, tt * 512:(tt + 1) * 512], in0=ps[:, :], scalar=T["hvec"][:, hbase + 6:hbase + 7], in1=rs[:, :], op0=ALU.mult, op1=ALU.mult), sig=True)
                        R["rs"].free[ri] = t4
                        return t4
                    proj_fm(kb, T, R, W[li, "xq"], list(range(4)), 16, big, CH, toks, epiq)
                    qdone = ("dve", kb.cnt["dve"])
                    kb.W("pe", qdone, tk_done, vtoks)
                    lastq = None
                    for h in range(4):
                        lastq = attention_head(
                            kb, T, R, NQT, [(0, 0), (0, 1)], lambda i, g, dl: True,
                            lambda g, i, h=h: qx[:, h, i * 128:(i + 1) * 128],
                            lambda g, dl, i, h=h: kx[:, h, dl * 128:(dl + 1) * 128],
                            lambda g, dl, i, h=h: vx[:, dl, h * 128:(h + 1) * 128],
                            lambda g, dl, i: T["ones"][:, :], None,
                            lambda i, h=h: ox[:, h, i * 128:(i + 1) * 128], prev if h == 0 else None, scale=1.0, LA=1)

                    def epi(bi, b, tt, ps, tm, c=c):
                        sl = slice(c * CH + tt * 512, c * CH + (tt + 1) * 512)
                        return epi_resid(kb, T, R, ps, tm, XS[b, :, sl], XS[b, :, sl])
                    prev = proj_fm(kb, T, R, W[li, "xo"], list(range(16)), 4, ox, CH, lastq, epi)
                end_phase(kb, T, [])

            phase({
                "big": ([128, 16, CH], BF16, "sb"),
                "mt": ([128, 16, 256], BF16, "sb"), "kx": ([128, 4, 256], BF16, "sb"), "vx": ([128, 2, 512], BF16, "sb"),
                "qx": ([128, 4, CH], BF16, "sb"), "ox": ([128, 4, CH], BF16, "sb"),
                "xs0": ([128, 16, 256], F32, "sb"), "xs1": ([128, 16, 256], F32, "sb"),
                "sq0": ([128, 16, 256], BF16, "sb"),
                "rs0": ([128, 512], F32, "sb"), "rs1": ([128, 512], F32, "sb"),
                "wr0": ([128, 2048], BF16, "sb"), "wr1": ([128, 2048], BF16, "sb"), "wr2": ([128, 2048], BF16, "sb"),
                "wv0": ([128, 8192], BF16, "sb"),
                "sq20": ([128, 512], BF16, "sb"), "sq21": ([128, 512], BF16, "sb"),
                "stg0": ([128, 8], BF16, "sb"), "stg1": ([128, 8], BF16, "sb"),
                "pt0": ([128, 1024], BF16, "sb"), "pt1": ([128, 1024], BF16, "sb"),
                "rc0": ([128, 128], F32, "sb"), "rc1": ([128, 128], F32, "sb"),
                "xr0": ([128, 512], F32, "sb"), "xr1": ([128, 512], F32, "sb"), "xr2": ([128, 512], F32, "sb"),
                "pn0": ([128, 512], F32, "ps"),
                "pm0": ([128, 512], F32, "ps"), "pm1": ([128, 512], F32, "ps"),
                "ps0": ([128, 512], F32, "ps"),
                "po0": ([128, 512], F32, "ps"), "po1": ([128, 512], F32, "ps"),
                "pd0": ([128, 512], F32, "ps"), "pd1": ([128, 512], F32, "ps"),
            }, cross_script)
            if DBG_STOP == (li, 3):
                break

            xout = yT if last_layer else XS

            def mlp_script(kb, T, li=li, gbase=gbase, xout=xout):
                R = mk_rings(kb, T, {"xs": 2, "sq": 1, "pn": 1, "rs": 2, "wr": 3, "pm": 4, "rl": 2, "stg": 3, "wd": 2, "xr": 3})
                big = T["big"]
                prev = None
                for c in range(NCH):
                    for e_ in ("dve", "pool", "sp"):
                        kb.W(e_, prev)
                    toks = norm_chunk(kb, T, R, XS, c, gbase + 48)
                    at_toks = []

                    def epi(bi, b, tt, ps, tm):
                        ri = R["rl"].nxt()
                        rl = R["rl"].t[ri]
                        kb.W("act", tm, R["rl"].free[ri])
                        t1 = kb.I("act", lambda e: e.activation(out=rl[:], in_=ps[:, :], func=AF.Relu), sig=True)
                        si = R["stg"].nxt()
                        sg = R["stg"].t[si]
                        kb.W("pool", t1, R["stg"].free[si])
                        t2 = kb.I("pool", lambda e: e.tensor_tensor(out=sg[:], in0=rl[:], in1=rl[:], op=ALU.mult), sig=True)
                        R["rl"].free[ri] = t2
                        kb.W("sp", t2)
                        td = kb.DMA("sp", R["stg"].sem[si], AT[b, :, tt * 512:(tt + 1) * 512], sg[:])
                        R["stg"].free[si] = td
                        at_toks.append(td)
                        return t1
                    tl = proj_fm(kb, T, R, W[li, "up"], list(range(64)), 16, big, CH, toks, epi)
                    kb.W("sp", at_toks[-12:], tl)
                    kb.W("sp", [(s, kb.cnt[s]) for s in R["stg"].sem])
                    for st4 in range(4):
                        kb.W("sp", ("pe", kb.cnt["pe"]))
                        a3 = big[:].rearrange("p a (b t) -> p (a b) t", t=512)
                        tla = []
                        for q4 in range(4):
                            tla.append(kb.DMA("sp", R["stg"].sem[0], a3[:, q4 * 16:(q4 + 1) * 16, :], AT[q4 * 16:(q4 + 1) * 16, :, st4 * 512:(st4 + 1) * 512].rearrange("k p t -> p k t")))
                        for b in range(16):
                            wi = R["wd"].nxt()
                            wt = R["wd"].t[wi]
                            kb.W("pool", R["wd"].free[wi])
                            tw = kb.DMA("pool", R["wd"].sem[wi], wt[:, :], W[li, "dn"][b, :, :])
                            kb.W("pe", tw, tla)
                            pb = R["pm"].nxt()
                            ps = R["pm"].t[pb]
                            kb.W("pe", R["pm"].free[pb])
                            for kc in range(64):
                                tmm = kb.I("pe", lambda e, kc=kc: e.matmul(ps[:, :], wt[:, kc * 128:(kc + 1) * 128], a3[:, kc, :], start=(kc == 0), stop=(kc == 63)), sig=(kc == 63))
                            R["wd"].free[wi] = tmm
                            sl = slice(c * CH + st4 * 512, c * CH + (st4 + 1) * 512)
                            R["pm"].free[pb] = epi_resid(kb, T, R, ps, tmm, XS[b, :, sl], xout[b, :, sl])
                    prev = ("pe", kb.cnt["pe"])
                end_phase(kb, T, [])

            phase({
                "big": ([128, 16, CH], BF16, "sb"),
                "xs0": ([128, 16, 256], F32, "sb"), "xs1": ([128, 16, 256], F32, "sb"),
                "sq0": ([128, 16, 256], BF16, "sb"),
                "rs0": ([128, 512], F32, "sb"), "rs1": ([128, 512], F32, "sb"),
                "wr0": ([128, 2048], BF16, "sb"), "wr1": ([128, 2048], BF16, "sb"), "wr2": ([128, 2048], BF16, "sb"),
                "wd0": ([128, 8192], BF16, "sb"), "wd1": ([128, 8192], BF16, "sb"),
                "rl0": ([128, 512], F32, "sb"), "rl1": ([128, 512], F32, "sb"),
                "stg0": ([128, 512], BF16, "sb"), "stg1": ([128, 512], BF16, "sb"), "stg2": ([128, 512], BF16, "sb"),
                "xr0": ([128, 512], F32, "sb"), "xr1": ([128, 512], F32, "sb"), "xr2": ([128, 512], F32, "sb"),
                "pn0": ([128, 512], F32, "ps"),
                "pm0": ([128, 512], F32, "ps"), "pm1": ([128, 512], F32, "ps"), "pm2": ([128, 512], F32, "ps"), "pm3": ([128, 512], F32, "ps"),
            }, mlp_script)
    return nc


def _blk(w, KC):
    K, M = w.shape
    return np.ascontiguousarray(w.reshape(KC, 128, M // 128, 128).transpose(2, 1, 0, 3).reshape(M // 128, 128, KC * 128))


def _blkv(w):
    K, M = w.shape
    return np.ascontiguousarray(w.reshape(16, 128, M // 512, 512).transpose(2, 1, 0, 3).reshape(M // 512, 128, 16 * 512))


def _t5_bucket(rel):
    nb = 16
    max_exact = 8
    ret = np.where(rel > 0, nb, 0)
    n = np.abs(rel)
    n_f = np.maximum(n, 1).astype(np.float32)
    large = max_exact + (np.log(n_f / np.float32(max_exact)) / np.float32(math.log(1024 / max_exact)) * np.float32(nb - max_exact)).astype(np.int32)
    large = np.minimum(large, nb - 1)
    return ret + np.where(n < max_exact, n, large)


def _dil_bias(t5_table):
    out = np.full((NH, 128, 27 * 128), NEG, np.float32)
    j = np.arange(128)[:, None]
    i = np.arange(128)[None, :]
    s = 0
    for g, (win, d) in enumerate(DIL):
        for dl in range(-REACH_DIL[g], REACH_DIL[g] + 1):
            rel = 128 * dl + j - i
            ok = (rel % d == 0) & (np.abs(rel) <= 64 * d)
            bk = _t5_bucket(rel)
            vals = t5_table[bk, g, :]
            blk = np.where(ok[None], vals.transpose(2, 0, 1), np.float32(NEG))
            out[:, :, s * 128:(s + 1) * 128] = blk
            s += 1
    return out


def _na_bias(rpb):
    out = np.full((NH, 128, 5, 7, 128), NEG, np.float32)
    R = 32
    kc = np.arange(64)
    cs = np.clip(kc - 8, 0, 48)
    col_ok = (kc[None, :] >= cs[:, None]) & (kc[None, :] < cs[:, None] + 16)
    dc = np.clip(kc[None, :] - kc[:, None], -15, 15) + 15
    for cls in range(5):
        qt = {0: 8, 1: 0, 2: 1, 3: 14, 4: 15}[cls]
        for si, dl in enumerate(range(-3, 4)):
            for a in range(2):
                r = 2 * qt + a
                if cls == 0:
                    rs = r - 4
                else:
                    rs = min(max(r - 4, 0), R - 8)
                for b in range(2):
                    kr = 2 * (qt + dl) + b
                    if kr < rs or kr >= rs + 8:
                        continue
                    dr = kr - r + 7
                    vals = rpb[:, dr, :][:, dc]
                    blk = np.where(col_ok[None], vals, np.float32(NEG)).transpose(0, 2, 1)
                    out[:, b * 64:(b + 1) * 64, cls, si, a * 64:(a + 1) * 64] = blk
    return np.ascontiguousarray(out.reshape(NH, 128, 5 * 7 * 128))


def _prep_weights(inp, LAYERS):
    Wd = {}
    f = np.float32
    for li, L in enumerate(LAYERS):
        l2 = L // 2
        if L % 2 == 0:
            w = inp["w_qkv_a"][l2]
            wq, wk, wv = w[:, 0:2048], w[:, 2048:4096], w[:, 4096:6144]
            Wd["wqk%d" % li] = _blk(np.concatenate([wq, wk], 1), 16)
            Wd["wv%d" % li] = _blkv(wv)
            Wd["bias%d" % li] = _na_bias(inp["rpb_a"][l2])
            Wd["wo%d" % li] = _blk(inp["w_o_a"][l2], 16)
        else:
            w = inp["w_qkv_b"][l2].reshape(2048, 3, 3, 2048)
            qk = np.concatenate([np.concatenate([w[:, g, 0], w[:, g, 1]], 1) for g in range(3)], 1)
            Wd["wqk%d" % li] = _blk(qk, 16)
            Wd["wv%d" % li] = _blkv(np.concatenate([w[:, g, 2] for g in range(3)], 1))
            Wd["bias%d" % li] = _dil_bias(inp["t5_table"])
            Wd["wo%d" % li] = _blk(inp["w_o_b"][l2], 16)
        Wd["wxq%d" % li] = _blk(inp["w_q_x"][L], 16)
        wkv = inp["w_kv_x"][L]
        Wd["wxk%d" % li] = _blk(wkv[:, 0:512], 16)
        Wd["wxv%d" % li] = _blkv(wkv[:, 512:1024])
        Wd["wxo%d" % li] = _blk(inp["w_o_x"][L], 4)
        Wd["wup%d" % li] = _blk(inp["w_up"][L], 16)
        Wd["wdn%d" % li] = _blk(inp["w_down"][L], 64)
    return Wd


def _vec_inputs(inp, LAYERS):
    gv = np.zeros((128, 4 * len(LAYERS) * 16), np.float32)
    hv = np.zeros((128, len(LAYERS) * 16), np.float32)
    for li, L in enumerate(LAYERS):
        for j, nm in enumerate(("g_mix", "g_cross", "g_mem", "g_mlp")):
            gv[:, li * 64 + j * 16: li * 64 + (j + 1) * 16] = inp[nm][L].reshape(16, 128).T
        l2 = L // 2
        if L % 2 == 0:
            hv[:, li * 16 + 0] = inp["q_norm_a"][l2]
            hv[:, li * 16 + 1] = inp["k_norm_a"][l2]
        else:
            for g in range(3):
                hv[:, li * 16 + 2 * g] = inp["q_norm_b"][l2, g]
                hv[:, li * 16 + 2 * g + 1] = inp["k_norm_b"][l2, g]
        hv[:, li * 16 + 6] = inp["q_norm_x"][L]
        hv[:, li * 16 + 7] = inp["k_norm_x"][L]
    return gv, hv


def _flags(kind, NCH):
    fl = np.zeros((128, 32), np.float32)
    for c in range(NCH):
        if kind == "prompt":
            top = 1.0 if c == 0 else 0.0
            bot = 1.0 if c == NCH - 1 else 0.0
        else:
            top = bot = 1.0
        fl[:, c] = top
        fl[:, 4 + c] = 1.0 - top
        fl[:, 8 + c] = bot
        fl[:, 12 + c] = 1.0 - bot
        fl[:, 16 + c] = 1.0 - top
        fl[:, 20 + c] = 1.0 - bot
    return fl


_CACHE = {}


def _run(inp, NCH, LAYERS, seqs, DBG_STOP=None, ncores=8):
    key = (NCH, tuple(LAYERS), DBG_STOP)
    if key not in _CACHE:
        _CACHE[key] = build(NCH, LAYERS, DBG_STOP)
    nc = _CACHE[key]
    Wd = _prep_weights(inp, LAYERS)
    gv, hv = _vec_inputs(inp, LAYERS)
    in_maps = []
    for (kind, xT, memT) in seqs:
        m = dict(Wd)
        m["xT"] = np.ascontiguousarray(xT.reshape(16, 128, NCH * CH))
        m["memT"] = np.ascontiguousarray(memT.reshape(NCH, 16, 128, 256))
        m["gvec"] = gv
        m["hvec"] = hv
        m["flags"] = _flags(kind, NCH)
        m["onesm"] = np.ones((128, 128), np.float32)
        in_maps.append(m)
    while len(in_maps) < ncores:
        in_maps.append(in_maps[-1])
    res = run_bass_kernel_spmd(nc, in_maps, core_ids=list(range(ncores)))
    return [r["yT"].reshape(2048, NCH * CH) for r in res.results]


def kernel(**inp):
    inp = {k: np.asarray(v) for k, v in inp.items()}
    LAYERS = [0, 1, 2, 3]
    NCH = 4
    xp = inp["x_prompt"][0]
    xs = inp["x_sample"].reshape(4 * 2048, 2048)
    memp = np.broadcast_to(inp["mem_prompt"][0].T[None], (4, 2048, 256))
    mems = inp["mem_sample"].transpose(0, 2, 1)
    seqs = [("prompt", np.ascontiguousarray(xp.T), np.ascontiguousarray(memp)),
            ("sample", np.ascontiguousarray(xs.T), np.ascontiguousarray(mems))]
    outs = _run(inp, NCH, LAYERS, seqs, ncores=2)
    y_prompt = np.ascontiguousarray(outs[0].T).reshape(1, 8192, 2048).astype(np.float32)
    y_sample = np.ascontiguousarray(outs[1].T).reshape(4, 2048, 2048).astype(np.float32)
    return (y_prompt, y_sample)
```

```python
import math
import contextlib
import numpy as np
import concourse.bass as bass
import concourse.mybir as mybir
from concourse.bass_utils import run_bass_kernel_spmd

F32 = mybir.dt.float32
BF16 = mybir.dt.bfloat16
AF = mybir.ActivationFunctionType
ALU = mybir.AluOpType

D = 2048
DH = 128
NH = 16
CH = 2048
NQT = CH // 128
PAD = 1024
EPS = 1e-6
SCALE = 1.0 / math.sqrt(DH)
NEG = -30000.0
DIL = ((128, 1), (512, 4), (2048, 16))
REACH_DIL = (1, 2, 8)
REACH_NA = 3
ENGS = ("sp", "act", "pool", "pe", "dve")


class KB:
    def __init__(self, nc, sems):
        self.nc = nc
        self.sems = sems
        self.cnt = {k: 0 for k in sems}
        self.cur = None
        self.E = None
        self.waited = {}
        self.free_sems = [k for k in sems if k.startswith("r")]
        self.ring_sem = {}

    def I(self, eng, fn, sig=False):
        tok = None
        if sig:
            self.cnt[eng] += 1
            tok = (eng, self.cnt[eng])
        if self.cur == eng:
            ins = fn(self.E)
            if sig:
                ins.then_inc(self.sems[eng], 1)
        return tok

    def DMA(self, q, sem, out, in_):
        self.cnt[sem] += 16
        tok = (sem, self.cnt[sem])
        if self.cur == q:
            self.E.dma_start(out=out, in_=in_).then_inc(self.sems[sem], 16)
        return tok

    def W(self, eng, *toks):
        if self.cur != eng:
            return
        for t in toks:
            if t is None:
                continue
            if isinstance(t, list):
                self.W(eng, *t)
                continue
            s, v = t
            if self.waited.get((eng, s), 0) < v:
                self.E.wait_ge(self.sems[s], v)
                self.waited[(eng, s)] = v

    def state(self):
        return (dict(self.cnt),)

    def restore(self, st):
        self.cnt = dict(st[0])


class Ring:
    def __init__(self, kb, name, tiles, semnames):
        self.kb = kb
        self.t = tiles
        self.n = len(tiles)
        self.sem = semnames
        self.reset()

    def reset(self):
        self.use = 0
        self.free = [None] * self.n

    def nxt(self):
        i = self.use % self.n
        self.use += 1
        return i


def run_phase(nc, kb, tensors, nsem_rings, script):
    with contextlib.ExitStack() as st:
        T = {}
        for name, (shape, dt, kind) in tensors.items():
            if kind == "sb":
                T[name] = st.enter_context(nc.sbuf_tensor(name + "_%d" % kb.phase_id, shape, dt))
            else:
                T[name] = st.enter_context(nc.psum_tensor(name + "_%d" % kb.phase_id, shape, dt))
        block = st.enter_context(nc.Block())
        s0 = kb.state()

        def mk(engname):
            def f(e):
                kb.restore(s0)
                kb.cur = engname
                kb.E = e
                script(kb, T)
                kb.cur = None
            return f

        block.sync(mk("sp"))
        block.scalar(mk("act"))
        block.gpsimd(mk("pool"))
        block.tensor(mk("pe"))
        block.vector(mk("dve"))
    kb.phase_id += 1


def end_phase(kb, T, extra):
    toks = list(extra)
    toks.append(kb.I("act", lambda e: e.activation(out=T["scr"][:, 0:1], in_=T["scr"][:, 1:2], func=AF.Copy), sig=True))
    toks.append(kb.I("dve", lambda e: e.tensor_copy(out=T["scr"][:, 2:3], in_=T["scr"][:, 3:4]), sig=True))
    toks.append(kb.I("pool", lambda e: e.tensor_copy(out=T["scr"][:, 4:5], in_=T["scr"][:, 5:6]), sig=True))
    for s in kb.cnt:
        if s.startswith("r") and kb.cnt[s] > 0:
            toks.append((s, kb.cnt[s]))
    kb.W("pe", *toks)
    toks.append(kb.I("pe", lambda e: e.matmul(T["pm0"][:, 0:8], T["ones"][:, 0:128], T["ones"][:, 0:8], start=True, stop=True), sig=True))
    for s in kb.cnt:
        if s.startswith("r") and kb.cnt[s] > 0:
            toks.append((s, kb.cnt[s]))
    for e in ENGS:
        kb.W(e, *toks)


def build(NCH, LAYERS, DBG_STOP=None):
    T_ = NCH * CH
    TP = T_ + 2 * PAD
    nc = bass.Bass("TRN2", target_bir_lowering=False)

    def din(name, shape, dt=F32):
        return nc.dram_tensor(name, list(shape), dt, kind="ExternalInput")

    xT_in = din("xT", [16, 128, T_])
    memT_in = din("memT", [NCH, 16, 128, 256])
    gvec_in = din("gvec", [128, 4 * len(LAYERS) * 16])
    hvec_in = din("hvec", [128, len(LAYERS) * 16])
    flags_in = din("flags", [128, 32])
    ones_in = din("onesm", [128, 128])
    W = {}
    for li, L in enumerate(LAYERS):
        if L % 2 == 0:
            W[li, "qk"] = din("wqk%d" % li, [32, 128, 16 * 128])
            W[li, "v"] = din("wv%d" % li, [4, 128, 16 * 512])
            W[li, "bias"] = din("bias%d" % li, [NH, 128, 5 * 7 * 128])
        else:
            W[li, "qk"] = din("wqk%d" % li, [96, 128, 16 * 128])
            W[li, "v"] = din("wv%d" % li, [12, 128, 16 * 512])
            W[li, "bias"] = din("bias%d" % li, [NH, 128, 27 * 128])
        W[li, "o"] = din("wo%d" % li, [16, 128, 16 * 128])
        W[li, "xq"] = din("wxq%d" % li, [4, 128, 16 * 128])
        W[li, "xk"] = din("wxk%d" % li, [4, 128, 16 * 128])
        W[li, "xv"] = din("wxv%d" % li, [1, 128, 16 * 512])
        W[li, "xo"] = din("wxo%d" % li, [16, 128, 4 * 128])
        W[li, "up"] = din("wup%d" % li, [64, 128, 16 * 128])
        W[li, "dn"] = din("wdn%d" % li, [16, 128, 64 * 128])
    yT = nc.dram_tensor("yT", [16, 128, T_], F32, kind="ExternalOutput")
    XS = yT if DBG_STOP is not None else nc.dram_tensor("xscr", [16, 128, T_], F32)
    QT = [nc.dram_tensor("qt%d" % g, [NH, 128, T_], BF16) for g in range(3)]
    KT = [nc.dram_tensor("kt%d" % g, [NH, 128, TP], BF16) for g in range(3)]
    VV = [nc.dram_tensor("vv%d" % g, [NH, TP, 128], BF16) for g in range(3)]
    AT = nc.dram_tensor("at", [64, 128, T_], BF16)
    WDB = nc.dram_tensor("wdb", [16, 128, 8192], BF16)

    semnames = list(ENGS) + ["r%d" % i for i in range(64)]
    with contextlib.ExitStack() as st:
        sems = {k: st.enter_context(nc.semaphore("s_" + k)) for k in semnames}
        kb = KB(nc, sems)
        kb.phase_id = 0
        rsem = iter(["r%d" % i for i in range(64)])
        RS = {}

        def ringsem(name, n):
            if name not in RS:
                RS[name] = [next(rsem) for _ in range(n)]
            return RS[name]

        ones = st.enter_context(nc.sbuf_tensor("ones", [128, 128], BF16))
        gvec = st.enter_context(nc.sbuf_tensor("gvec_sb", [128, 4 * len(LAYERS) * 16], F32))
        hvec = st.enter_context(nc.sbuf_tensor("hvec_sb", [128, len(LAYERS) * 16], F32))
        flags = st.enter_context(nc.sbuf_tensor("flags_sb", [128, 32], F32))
        scr = st.enter_context(nc.sbuf_tensor("scr", [128, 8], F32))
        zer = st.enter_context(nc.sbuf_tensor("zer", [128, 2048], BF16))
        epsb = st.enter_context(nc.sbuf_tensor("epsb", [128, 2], F32))
        COMMON = {"epsb": epsb, "ones": ones, "gvec": gvec, "hvec": hvec, "flags": flags, "scr": scr, "zer": zer}

        def phase(tensors, script):
            tensors = dict(tensors)
            if "pm0" not in tensors:
                tensors["pm0"] = ([128, 512], F32, "ps")

            def s2(kb, T):
                T = dict(T)
                T.update(COMMON)
                if "ps0" not in T and "ps1" in T:
                    T["ps0"] = T["pm0"]
                script(kb, T)
            run_phase(nc, kb, tensors, 0, s2)

        def init_script(kb, T):
            s = ringsem("init", 1)[0]
            t = []
            t.append(kb.DMA("pool", s, T["ones"][:], ones_in[:, :]))
            t.append(kb.DMA("sp", s, T["gvec"][:], gvec_in[:, :]))
            t.append(kb.DMA("sp", s, T["hvec"][:], hvec_in[:, :]))
            t.append(kb.DMA("sp", s, T["flags"][:], flags_in[:, :]))
            kb.W("dve", t[1], t[2])
            kb.I("dve", lambda e: e.tensor_scalar(out=T["gvec"][:], in0=T["gvec"][:], scalar1=float(math.sqrt(D)), scalar2=None, op0=ALU.mult), sig=True)
            hk = T["hvec"][:].rearrange("p (a two) -> p a two", two=2)[:, :, 1]
            kb.I("dve", lambda e: e.tensor_scalar(out=hk, in0=hk, scalar1=float(math.sqrt(DH)), scalar2=None, op0=ALU.mult), sig=True)
            kb.W("pe", t[0])
            t1 = kb.I("dve", lambda e: e.memset(T["zer"][:], 0.0), sig=True)
            kb.I("dve", lambda e: e.memset(T["epsb"][:, 0:1], float(D * EPS)), sig=True)
            kb.I("dve", lambda e: e.memset(T["epsb"][:, 1:2], float(DH * EPS)), sig=True)
            t2 = kb.I("dve", lambda e: e.memset(T["scr"][:], 0.0), sig=True)
            kb.W("sp", t1)
            for g in range(3):
                for h in range(NH):
                    kb.DMA("sp", s, KT[g][h, :, 0:PAD], T["zer"][:, 0:PAD])
                    kb.DMA("sp", s, KT[g][h, :, PAD + T_:TP], T["zer"][:, 0:PAD])
                    kb.DMA("sp", s, VV[g][h, 0:PAD, :].rearrange("(a p) d -> p a d", p=128), T["zer"][:, 0:PAD].rearrange("p (a d) -> p a d", d=128))
                    kb.DMA("sp", s, VV[g][h, PAD + T_:TP, :].rearrange("(a p) d -> p a d", p=128), T["zer"][:, 0:PAD].rearrange("p (a d) -> p a d", d=128))
            end_phase(kb, T, [t2])

        phase({}, init_script)

        class Ctx:
            pass

        def mk_rings(kb, T, spec):
            R = {}
            for name, n in spec.items():
                R[name] = Ring(kb, name, [T["%s%d" % (name, i)] for i in range(n)], ringsem(name, n))
            return R

        def norm_chunk(kb, T, R, src, c, gcol, ntok=CH, tok0=None, dst=None, srcap=None):
            dst = T["big"] if dst is None else dst
            done = []
            nsub = ntok // 256
            for j in range(nsub):
                i = R["xs"].nxt()
                xs = R["xs"].t[i]
                kb.W("sp", R["xs"].free[i])
                if srcap is None:
                    a = src[:, :, c * CH + j * 256: c * CH + (j + 1) * 256].rearrange("k p t -> p k t")
                else:
                    a = srcap[:, :, j * 256:(j + 1) * 256].rearrange("k p t -> p k t")
                tl = kb.DMA("sp", R["xs"].sem[i], xs[:], a)
                kb.W("act", tl)
                qi = R["sq"].nxt()
                sq = R["sq"].t[qi]
                kb.W("act", R["sq"].free[qi])
                t1 = kb.I("act", lambda e: e.activation(out=sq[:], in_=xs[:], func=AF.Square), sig=True)
                pb = R["pn"].nxt()
                ps = R["pn"].t[pb]
                kb.W("pe", t1, R["pn"].free[pb])
                for kc in range(16):
                    t2 = kb.I("pe", lambda e, kc=kc: e.matmul(ps[:, 0:256], T["ones"][:, :], sq[:, kc, :], start=(kc == 0), stop=(kc == 15)), sig=(kc == 15))
                R["sq"].free[qi] = t2
                ri = R["rs"].nxt()
                rs = R["rs"].t[ri]
                kb.W("dve", t2, R["rs"].free[ri])
                kb.W("act", ("pe", kb.cnt["pe"]), R["rs"].free[ri])
                t3a = kb.I("act", lambda e: e.activation(out=rs[:, 0:256], in_=ps[:, 0:256], func=AF.Sqrt, bias=T["epsb"][:, 0:1], scale=1.0), sig=True)
                kb.W("dve", t3a)
                t3 = kb.I("dve", lambda e: e.reciprocal(out=rs[:, 0:256], in_=rs[:, 0:256]), sig=True)
                kb.W("dve", t3)
                R["pn"].free[pb] = t3
                kb.W("dve", t3, tl)
                last = []
                for kc in range(16):
                    eng = "dve"
                    tk = kb.I(eng, lambda e, kc=kc: e.scalar_tensor_tensor(
                        out=dst[:, kc, j * 256:(j + 1) * 256], in0=xs[:, kc, :], scalar=T["gvec"][:, gcol + kc:gcol + kc + 1],
                        in1=rs[:, 0:256], op0=ALU.mult, op1=ALU.mult), sig=(kc == 15))
                    if tk is not None:
                        last.append(tk)
                R["xs"].free[i] = last
                R["rs"].free[ri] = last
                done += last
            return done

        def proj_fm(kb, T, R, wd, blocks, KC, src, ntok, src_toks, epi, wview=None):
            ntt = ntok // 512
            pend = None
            for bi, b in enumerate(blocks):
                wi = R["wr"].nxt()
                wt = R["wr"].t[wi]
                kb.W("pool", R["wr"].free[wi])
                tw = kb.DMA("pool", R["wr"].sem[wi], wt[:, 0:KC * 128], wd[b, :, :])
                kb.W("pe", tw, src_toks)
                for tt in range(ntt):
                    pb = R["pm"].nxt()
                    ps = R["pm"].t[pb]
                    kb.W("pe", R["pm"].free[pb])
                    for kc in range(KC):
                        tm = kb.I("pe", lambda e, kc=kc, tt=tt: e.matmul(ps[:, :], wt[:, kc * 128:(kc + 1) * 128], src[:, kc, tt * 512:(tt + 1) * 512], start=(kc == 0), stop=(kc == KC - 1)), sig=(kc == KC - 1))
                    r = epi(bi, b, tt, ps, tm)
                    if pend is not None:
                        R["pm"].free[pend[0]] = pend[1]()
                        pend = None
                    if callable(r):
                        pend = (pb, r)
                    else:
                        R["pm"].free[pb] = r
                R["wr"].free[wi] = tm
            if pend is not None:
                R["pm"].free[pend[0]] = pend[1]()
            return tm

        def epi_qknorm(kb, T, R, ps, tm, gain_ap, dst_ap):
            qi = R["sq2"].nxt()
            sq = R["sq2"].t[qi]
            kb.W("act", tm, R["sq2"].free[qi])
            t1 = kb.I("act", lambda e: e.activation(out=sq[:], in_=ps[:, :], func=AF.Square), sig=True)

            def cont():
                pb = R["pn"].nxt()
                ps2 = R["pn"].t[pb]
                kb.W("pe", t1, R["pn"].free[pb])
                t2 = kb.I("pe", lambda e: e.matmul(ps2[:, :], T["ones"][:, :], sq[:], start=True, stop=True), sig=True)
                R["sq2"].free[qi] = t2
                ri = R["rs"].nxt()
                rs = R["rs"].t[ri]
                kb.W("act", t2, R["rs"].free[ri])
                t3a = kb.I("act", lambda e: e.activation(out=rs[:, :], in_=ps2[:, :], func=AF.Sqrt, bias=T["epsb"][:, 1:2], scale=1.0), sig=True)
                kb.W("dve", t3a, tm)
                t3 = kb.I("dve", lambda e: e.reciprocal(out=rs[:, :], in_=rs[:, :]), sig=True)
                kb.W("dve", t3)
                R["pn"].free[pb] = t3
                si = R["stg"].nxt()
                sg = R["stg"].t[si]
                kb.W("dve", R["stg"].free[si])
                t4 = kb.I("dve", lambda e: e.scalar_tensor_tensor(out=sg[:], in0=ps[:, :], scalar=gain_ap, in1=rs[:, :], op0=ALU.mult, op1=ALU.mult), sig=True)
                R["rs"].free[ri] = t4
                if dst_ap is not None:
                    kb.W("sp", t4)
                    R["stg"].free[si] = kb.DMA("sp", R["stg"].sem[si], dst_ap, sg[:])
                return t4
            return cont

        def epi_resid(kb, T, R, ps, tm, src_ap, dst_ap):
            xi = R["xr"].nxt()
            xr = R["xr"].t[xi]
            kb.W("sp", R["xr"].free[xi])
            tl = kb.DMA("sp", R["xr"].sem[xi], xr[:], src_ap)
            kb.W("dve", tl, tm)
            t1 = kb.I("dve", lambda e: e.tensor_tensor(out=xr[:], in0=ps[:, :], in1=xr[:], op=ALU.add), sig=True)
            kb.W("sp", t1)
            R["xr"].free[xi] = kb.DMA("sp", R["xr"].sem[xi], dst_ap, xr[:])
            return t1

        def proj_tm(kb, T, R, wd, g, src, ntiles, src_toks, epi):
            wi = R["wv"].nxt()
            wt = R["wv"].t[wi]
            kb.W("pool", R["wv"].free[wi])
            tw = kb.DMA("pool", R["wv"].sem[wi], wt[:], wd[g, :, :])
            kb.W("pe", tw, src_toks)
            for t16 in range(ntiles):
                pb = R["pm"].nxt()
                ps = R["pm"].t[pb]
                kb.W("pe", R["pm"].free[pb])
                for kc in range(16):
                    tm = kb.I("pe", lambda e, kc=kc, t16=t16: e.matmul(ps[:, :], src[:, kc, t16 * 128:(t16 + 1) * 128], wt[:, kc * 512:(kc + 1) * 512], start=(kc == 0), stop=(kc == 15)), sig=(kc == 15))
                R["pm"].free[pb] = epi(t16, ps, tm)
            R["wv"].free[wi] = tm

        def attention_head(kb, T, R, nqt, slots, active, qtile, ksrc, vsrc, den_lhsT, bias_fn, out_ap, first_dep, scale=None, LA=3):
            units = []
            nb = (len(slots) + 3) // 4
            for i in range(nqt):
                ulist = []
                for bi in range(nb):
                    bs = slots[bi * 4:(bi + 1) * 4]
                    act = [active(i, g, dl) for (g, dl) in bs]
                    if any(act):
                        ulist.append([i, bi, bs, act, False, False])
                ulist[0][4] = True
                ulist[-1][5] = True
                units += ulist
            state = {}
            cur = {}
            res = {"last": None}

            def emit_S(u):
                i, bi, bs, act, fb, lb = units[u]
                n = len(bs)
                si = R["ps"].nxt()
                ps = R["ps"].t[si]
                kb.W("pe", R["ps"].free[si])
                lastj = max(j for j in range(n) if act[j])
                for j, (g, dl) in enumerate(bs):
                    if not act[j]:
                        continue
                    tS = kb.I("pe", lambda e, j=j, g=g, dl=dl: e.matmul(ps[:, j * 128:(j + 1) * 128], ksrc(g, dl, i), qtile(g, i), start=True, stop=True), sig=(j == lastj))
                pi = R["pt"].nxt()
                pt = R["pt"].t[pi]
                if bias_fn is not None:
                    ti = R["tm"].nxt()
                    tmp = R["tm"].t[ti]
                    kb.W("dve", tS, R["tm"].free[ti])
                    tB = bias_fn(i, bi, bs, tmp, ps)
                    R["ps"].free[si] = tB
                    kb.W("act", tB, R["pt"].free[pi])
                    tE = kb.I("act", lambda e: e.activation(out=pt[:, 0:n * 128], in_=tmp[:, 0:n * 128], func=AF.Exp), sig=True)
                    R["tm"].free[ti] = tE
                else:
                    kb.W("act", tS, R["pt"].free[pi])
                    tE = kb.I("act", lambda e: e.activation(out=pt[:, 0:n * 128], in_=ps[:, 0:n * 128], func=AF.Exp, scale=float(scale)), sig=True)
                    R["ps"].free[si] = tE
                state[u] = (pi, pt, tE)

            def emit_PV(u):
                i, bi, bs, act, fb, lb = units[u]
                n = len(bs)
                pi, pt, tE = state.pop(u)
                if fb:
                    oi = R["po"].nxt()
                    cur["oi"] = oi
                    kb.W("pe", R["po"].free[oi])
                oi = cur["oi"]
                po = R["po"].t[oi]
                pd = R["pd"].t[oi]
                kb.W("pe", tE)
                js = [j for j in range(n) if act[j]]
                for j in js:
                    g, dl = bs[j]
                    st_ = (fb and j == js[0])
                    sp_ = (lb and j == js[-1])
                    kb.I("pe", lambda e, j=j, g=g, dl=dl, st_=st_, sp_=sp_: e.matmul(po[:, 0:128], vsrc(g, dl, i), pt[:, j * 128:(j + 1) * 128], start=st_, stop=sp_))
                    tP = kb.I("pe", lambda e, j=j, g=g, dl=dl, st_=st_, sp_=sp_: e.matmul(pd[:, 0:128], den_lhsT(g, dl, i), pt[:, j * 128:(j + 1) * 128], start=st_, stop=sp_), sig=(j == js[-1]))
                R["pt"].free[pi] = tP
                if lb:
                    ri = R["rc"].nxt()
                    rc = R["rc"].t[ri]
                    kb.W("dve", tP, R["rc"].free[ri], first_dep if i == 0 else None)
                    t1 = kb.I("dve", lambda e: e.reciprocal(out=rc[:, :], in_=pd[:, 0:128]), sig=True)
                    kb.W("dve", t1)
                    t2 = kb.I("dve", lambda e: e.tensor_tensor(out=out_ap(i), in0=po[:, 0:128], in1=rc[:, :], op=ALU.mult), sig=True)
                    R["rc"].free[ri] = t2
                    R["po"].free[oi] = t2
                    res["last"] = t2

            nu = len(units)
            for u in range(min(LA, nu)):
                emit_S(u)
            for u in range(nu):
                if u + LA < nu:
                    emit_S(u + LA)
                emit_PV(u)
            return res["last"]

        xsrc = xT_in
        for li, L in enumerate(LAYERS):
            is_na = (L % 2 == 0)
            ng = 1 if is_na else 3
            gbase = li * 64
            hbase = li * 16
            last_layer = (li == len(LAYERS) - 1)

            def qkv_script(kb, T, li=li, is_na=is_na, ng=ng, gbase=gbase, hbase=hbase, xsrc=xsrc):
                R = mk_rings(kb, T, {"xs": 2, "sq": 1, "pn": 2, "rs": 2, "wr": 3, "pm": 4, "sq2": 2, "stg": 3, "wv": 2, "vst": 2})
                big = T["big"]
                prev = None
                for c in range(NCH):
                    for e_ in ("dve", "pool"):
                        kb.W(e_, prev)
                    toks = norm_chunk(kb, T, R, xsrc, c, gbase + 0)

                    def epi(bi, b, tt, ps, tm, c=c):
                        g = b // 32
                        isk = (b % 32) >= 16
                        h = b % 16
                        gain = T["hvec"][:, hbase + g * 2 + (1 if isk else 0): hbase + g * 2 + (1 if isk else 0) + 1]
                        if isk:
                            dst = KT[g][h, :, PAD + c * CH + tt * 512: PAD + c * CH + (tt + 1) * 512]
                        else:
                            dst = QT[g][h, :, c * CH + tt * 512: c * CH + (tt + 1) * 512]
                        return epi_qknorm(kb, T, R, ps, tm, gain, dst)
                    tl1 = proj_fm(kb, T, R, W[li, "qk"], list(range(32 * ng)), 16, big, CH, toks, epi)

                    for vg in range(4 * ng):
                        def epiv(t16, ps, tm, vg=vg, c=c):
                            g = vg // 4
                            vi = R["vst"].nxt()
                            vs = R["vst"].t[vi]
                            kb.W("act", tm, R["vst"].free[vi])
                            t1 = kb.I("act", lambda e: e.activation(out=vs[:], in_=ps[:, :], func=AF.Copy), sig=True)
                            kb.W("sp", t1)
                            r0 = PAD + c * CH + t16 * 128
                            dst = VV[g][(vg % 4) * 4:(vg % 4) * 4 + 4, r0:r0 + 128, :].rearrange("h p d -> p h d")
                            R["vst"].free[vi] = kb.DMA("sp", R["vst"].sem[vi], dst, vs[:].rearrange("p (h d) -> p h d", d=128))
                            return t1
                        proj_tm(kb, T, R, W[li, "v"], vg, big, 16, toks, epiv)
                    prev = ("pe", kb.cnt["pe"])
                end_phase(kb, T, [])

            phase({
                "big": ([128, 16, CH], BF16, "sb"),
                "xs0": ([128, 16, 256], F32, "sb"), "xs1": ([128, 16, 256], F32, "sb"),
                "sq0": ([128, 16, 256], BF16, "sb"),
                "rs0": ([128, 512], F32, "sb"), "rs1": ([128, 512], F32, "sb"),
                "wr0": ([128, 2048], BF16, "sb"), "wr1": ([128, 2048], BF16, "sb"), "wr2": ([128, 2048], BF16, "sb"),
                "wv0": ([128, 8192], BF16, "sb"), "wv1": ([128, 8192], BF16, "sb"),
                "sq20": ([128, 512], BF16, "sb"), "sq21": ([128, 512], BF16, "sb"),
                "stg0": ([128, 512], BF16, "sb"), "stg1": ([128, 512], BF16, "sb"), "stg2": ([128, 512], BF16, "sb"),
                "vst0": ([128, 512], BF16, "sb"), "vst1": ([128, 512], BF16, "sb"),
                "pn0": ([128, 512], F32, "ps"), "pn1": ([128, 512], F32, "ps"),
                "pm0": ([128, 512], F32, "ps"), "pm1": ([128, 512], F32, "ps"), "pm2": ([128, 512], F32, "ps"), "pm3": ([128, 512], F32, "ps"),
            }, qkv_script)
            if DBG_STOP == (li, 1):
                break

            if is_na:
                slots = [(0, d) for d in range(-REACH_NA, REACH_NA + 1)]
                nbias = 5 * 7 * 128
            else:
                slots = [(g, d) for g in range(3) for d in range(-REACH_DIL[g], REACH_DIL[g] + 1)]
                nbias = 27 * 128
            reach = [REACH_NA] if is_na else list(REACH_DIL)
            kw = [CH + 2 * r * 128 for r in reach]
            koff = [sum(kw[:g]) for g in range(ng)]
            vt = [NQT + 2 * r for r in reach]
            voff = [sum(vt[:g]) for g in range(ng)]
            xdst = XS

            def att_script(kb, T, li=li, is_na=is_na, ng=ng, slots=slots, reach=reach, kw=kw, koff=koff, vt=vt, voff=voff, xsrc=xsrc, xdst=xdst, nbias=nbias):
                R = mk_rings(kb, T, {"qk": 2, "vb": 1, "ps": 4, "po": 2, "pd": 2, "pt": 4, "tm": 4, "rc": 2, "wr": 3, "xr": 3})
                R["pm"] = R["ps"]
                if not is_na:
                    tvo = None
                    for idx in range(8):
                        tvo = kb.I("dve", lambda e, idx=idx: e.tensor_scalar(out=T["vo"][:, idx, :], in0=T["ones"][:, :], scalar1=T["flags"][:, 16 + idx:17 + idx], scalar2=None, op0=ALU.mult), sig=True)
                    kb.W("pe", tvo)
                big = T["big"]
                prev_wo = None
                for c in range(NCH):
                    head_toks = []
                    for h in range(NH):
                        qi = R["qk"].nxt()
                        qk = R["qk"].t[qi]
                        kb.W("sp", R["qk"].free[qi])
                        tl = []
                        for g in range(ng):
                            tl.append(kb.DMA("sp", R["qk"].sem[qi], qk[:, g * CH:(g + 1) * CH], QT[g][h, :, c * CH:(c + 1) * CH]))
                            k0 = PAD + c * CH - reach[g] * 128
                            tl.append(kb.DMA("sp", R["qk"].sem[qi], qk[:, 3 * CH + koff[g]: 3 * CH + koff[g] + kw[g]], KT[g][h, :, k0:k0 + kw[g]]))
                        vi = R["vb"].nxt()
                        vb = T["vbuf"]
                        bb = T["bbuf"]
                        kb.W("sp", R["vb"].free[vi])
                        for g in range(ng):
                            r0 = PAD + c * CH - reach[g] * 128
                            tl.append(kb.DMA("sp", R["vb"].sem[vi], vb[:, voff[g]:voff[g] + vt[g], :], VV[g][h, r0:r0 + vt[g] * 128, :].rearrange("(a p) d -> p a d", p=128)))
                        tl.append(kb.DMA("sp", R["vb"].sem[vi], bb[:, 0:nbias], W[li, "bias"][h, :, :]))
                        kb.W("pe", tl)
                        kb.W("dve", tl)
                        kb.W("pool", tl)
                        tmask = []
                        if not is_na:
                            kb.W("pool", tl)
                            for g in range(ng):
                                if c > 0:
                                    tmask.append(kb.I("pool", lambda e, g=g: e.tensor_scalar(out=vb[:, voff[g]:voff[g] + reach[g], :], in0=vb[:, voff[g]:voff[g] + reach[g], :], scalar1=T["flags"][:, 16 + c:17 + c], scalar2=None, op0=ALU.mult), sig=True))
                                if c < NCH - 1:
                                    o_ = voff[g] + reach[g] + NQT
                                    tmask.append(kb.I("pool", lambda e, g=g, o_=o_: e.tensor_scalar(out=vb[:, o_:o_ + reach[g], :], in0=vb[:, o_:o_ + reach[g], :], scalar1=T["flags"][:, 20 + c:21 + c], scalar2=None, op0=ALU.mult), sig=True))
                            kb.W("pe", tmask)

                        def active(i, g, dl):
                            if c == 0 and i + dl < 0:
                                return False
                            if c == NCH - 1 and i + dl >= NQT:
                                return False
                            return True

                        def qtile(g, i):
                            return qk[:, g * CH + i * 128: g * CH + (i + 1) * 128]

                        def ksrc(g, dl, i):
                            o = 3 * CH + koff[g] + (i + dl + reach[g]) * 128
                            return qk[:, o:o + 128]

                        def vsrc(g, dl, i):
                            return vb[:, voff[g] + i + dl + reach[g], :]

                        def den_lhsT(g, dl, i):
                            if is_na or (0 <= i + dl < NQT):
                                return T["ones"][:, :]
                            return T["vo"][:, (c if i + dl < 0 else 4 + c), :]

                        def bias_fn(i, bi, bs, tmp, ps):
                            n = len(bs)
                            off = bi * 512
                            if is_na:
                                cls = {0: 1, 1: 2, NQT - 2: 3, NQT - 1: 4}.get(i, 0)
                                if cls == 0:
                                    return kb.I("dve", lambda e: e.tensor_tensor(out=tmp[:, 0:n * 128], in0=ps[:, 0:n * 128], in1=bb[:, off:off + n * 128], op=ALU.add), sig=True)
                                top = cls in (1, 2)
                                fcol = (0 if top else 8) + c
                                tq = kb.I("dve", lambda e: e.scalar_tensor_tensor(out=tmp[:, 0:n * 128], in0=bb[:, off:off + n * 128], scalar=T["flags"][:, fcol + 4:fcol + 5], in1=ps[:, 0:n * 128], op0=ALU.mult, op1=ALU.add), sig=True)
                                kb.W("dve", tq)
                                return kb.I("dve", lambda e: e.scalar_tensor_tensor(out=tmp[:, 0:n * 128], in0=bb[:, cls * 896 + off:cls * 896 + off + n * 128], scalar=T["flags"][:, fcol:fcol + 1], in1=tmp[:, 0:n * 128], op0=ALU.mult, op1=ALU.add), sig=True)
                            return kb.I("dve", lambda e: e.tensor_tensor(out=tmp[:, 0:n * 128], in0=ps[:, 0:n * 128], in1=bb[:, off:off + n * 128], op=ALU.add), sig=True)

                        lastq = attention_head(kb, T, R, NQT, slots, active, qtile, ksrc, vsrc, den_lhsT, bias_fn,
                                               lambda i: big[:, h, i * 128:(i + 1) * 128], prev_wo if h == 0 else None)
                        pe_done = ("pe", kb.cnt["pe"])
                        R["qk"].free[qi] = [pe_done]
                        R["vb"].free[vi] = [pe_done, lastq]
                        head_toks.append(lastq)

                    def epi(bi, b, tt, ps, tm, c=c):
                        sl = slice(c * CH + tt * 512, c * CH + (tt + 1) * 512)
                        return epi_resid(kb, T, R, ps, tm, xsrc[b, :, sl], xdst[b, :, sl])
                    prev_wo = proj_fm(kb, T, R, W[li, "o"], list(range(16)), 16, big, CH, head_toks[-1], epi)
                end_phase(kb, T, [])

            phase({
                "big": ([128, 16, CH], BF16, "sb"),
                "qk0": ([128, 3 * CH + 8960], BF16, "sb"), "qk1": ([128, 3 * CH + 8960], BF16, "sb"),
                "vb0": ([128, 8], BF16, "sb"),
                "vbuf": ([128, 70, 128], BF16, "sb"), "bbuf": ([128, 5 * 7 * 128], F32, "sb"),
                "pt0": ([128, 512], BF16, "sb"), "pt1": ([128, 512], BF16, "sb"), "pt2": ([128, 512], BF16, "sb"), "pt3": ([128, 512], BF16, "sb"),
                "tm0": ([128, 512], F32, "sb"), "tm1": ([128, 512], F32, "sb"), "tm2": ([128, 512], F32, "sb"), "tm3": ([128, 512], F32, "sb"),
                "vo": ([128, 8, 128], BF16, "sb"),
                "rc0": ([128, 128], F32, "sb"), "rc1": ([128, 128], F32, "sb"),
                "wr0": ([128, 2048], BF16, "sb"), "wr1": ([128, 2048], BF16, "sb"), "wr2": ([128, 2048], BF16, "sb"),
                "xr0": ([128, 512], F32, "sb"), "xr1": ([128, 512], F32, "sb"), "xr2": ([128, 512], F32, "sb"),
                "pm0": ([128, 512], F32, "ps"), "ps1": ([128, 512], F32, "ps"), "ps2": ([128, 512], F32, "ps"), "ps3": ([128, 512], F32, "ps"),
                "po0": ([128, 512], F32, "ps"), "po1": ([128, 512], F32, "ps"),
                "pd0": ([128, 512], F32, "ps"), "pd1": ([128, 512], F32, "ps"),
            }, att_script)
            xsrc = XS
            if DBG_STOP == (li, 2):
                break

            def cross_script(kb, T, li=li, gbase=gbase, hbase=hbase):
                R = mk_rings(kb, T, {"xs": 2, "sq": 1, "pn": 1, "rs": 2, "wr": 3, "sq2": 2, "stg": 2, "wv": 1, "ps": 3, "po": 2, "pd": 2, "pt": 3, "rc": 2, "xr": 3})
                R["pm"] = R["ps"]
                big = T["big"]
                mt = T["mt"]
                kx = T["kx"]
                vx = T["vx"]
                qx = T["qx"]
                ox = T["ox"]
                prev = None
                for c in range(NCH):
                    for e_ in ("dve", "pool", "act"):
                        kb.W(e_, prev)
                    tm_ = norm_chunk(kb, T, R, None, 0, gbase + 32, ntok=256, dst=mt, srcap=memT_in[c])

                    def epik(bi, b, tt, ps, tm):
                        raise RuntimeError
                    for b in range(4):
                        wi = R["wr"].nxt()
                        wt = R["wr"].t[wi]
                        kb.W("pool", R["wr"].free[wi])
                        tw = kb.DMA("pool", R["wr"].sem[wi], wt[:, :], W[li, "xk"][b, :, :])
                        kb.W("pe", tw, tm_)
                        pb = R["pm"].nxt()
                        ps = R["pm"].t[pb]
                        kb.W("pe", R["pm"].free[pb])
                        for kc in range(16):
                            tmm = kb.I("pe", lambda e, kc=kc: e.matmul(ps[:, 0:256], wt[:, kc * 128:(kc + 1) * 128], mt[:, kc, :], start=(kc == 0), stop=(kc == 15)), sig=(kc == 15))
                        R["wr"].free[wi] = tmm
                        qi = R["sq2"].nxt()
                        sq = R["sq2"].t[qi]
                        kb.W("act", tmm, R["sq2"].free[qi])
                        t1 = kb.I("act", lambda e: e.activation(out=sq[:, 0:256], in_=ps[:, 0:256], func=AF.Square), sig=True)
                        p2 = R["pn"].nxt()
                        ps2 = R["pn"].t[p2]
                        kb.W("pe", t1, R["pn"].free[p2])
                        t2 = kb.I("pe", lambda e: e.matmul(ps2[:, 0:256], T["ones"][:, :], sq[:, 0:256], start=True, stop=True), sig=True)
                        R["sq2"].free[qi] = t2
                        ri = R["rs"].nxt()
                        rs = R["rs"].t[ri]
                        kb.W("dve", t2, R["rs"].free[ri], tmm, prev)
                        kb.W("act", ("pe", kb.cnt["pe"]), R["rs"].free[ri])
                        t3a = kb.I("act", lambda e: e.activation(out=rs[:, 0:256], in_=ps2[:, 0:256], func=AF.Sqrt, bias=T["epsb"][:, 1:2], scale=1.0), sig=True)
                        kb.W("dve", t3a)
                        t3 = kb.I("dve", lambda e: e.reciprocal(out=rs[:, 0:256], in_=rs[:, 0:256]), sig=True)
                        kb.W("dve", t3)
                        R["pn"].free[p2] = t3
                        t4 = kb.I("dve", lambda e, b=b: e.scalar_tensor_tensor(out=kx[:, b, :], in0=ps[:, 0:256], scalar=T["hvec"][:, hbase + 7:hbase + 8], in1=rs[:, 0:256], op0=ALU.mult, op1=ALU.mult), sig=True)
                        R["rs"].free[ri] = t4
                        R["pm"].free[pb] = t4
                    tk_done = t4
                    def epiv(t16, ps, tm):
                        kb.W("act", tm)
                        return kb.I("act", lambda e: e.activation(out=vx[:, t16, :], in_=ps[:, :], func=AF.Copy), sig=True)
                    vtoks = []

                    def epiv2(t16, ps, tm):
                        t = epiv(t16, ps, tm)
                        vtoks.append(t)
                        return t
                    proj_tm(kb, T, R, W[li, "xv"], 0, mt, 2, tm_, epiv2)
                    toks = norm_chunk(kb, T, R, XS, c, gbase + 16)

                    def epiq(bi, b, tt, ps, tm):
                        qi = R["sq2"].nxt()
                        sq = R["sq2"].t[qi]
                        kb.W("act", tm, R["sq2"].free[qi])
                        t1 = kb.I("act", lambda e: e.activation(out=sq[:], in_=ps[:, :], func=AF.Square), sig=True)
                        p2 = R["pn"].nxt()
                        ps2 = R["pn"].t[p2]
                        kb.W("pe", t1, R["pn"].free[p2])
                        t2 = kb.I("pe", lambda e: e.matmul(ps2[:, :], T["ones"][:, :], sq[:], start=True, stop=True), sig=True)
                        R["sq2"].free[qi] = t2
                        ri = R["rs"].nxt()
                        rs = R["rs"].t[ri]
                        kb.W("dve", t2, R["rs"].free[ri], tm)
                        kb.W("act", ("pe", kb.cnt["pe"]), R["rs"].free[ri])
                        t3a = kb.I("act", lambda e: e.activation(out=rs[:, :], in_=ps2[:, :], func=AF.Sqrt, bias=T["epsb"][:, 1:2], scale=1.0), sig=True)
                        kb.W("dve", t3a)
                        t3 = kb.I("dve", lambda e: e.reciprocal(out=rs[:, :], in_=rs[:, :]), sig=True)
                        kb.W("dve", t3)
                        R["pn"].free[p2] = t3
                        t4 = kb.I("dve", lambda e: e.scalar_tensor_tensor(out=qx[:, b, tt * 512:(tt + 1) * 512], in0=ps[:, :], scalar=T["hvec"][:, hbase + 6:hbase + 7], in1=rs[:, :], op0=ALU.mult, op1=ALU.mult), sig=True)
                        R["rs"].free[ri] = t4
                        return t4
                    proj_fm(kb, T, R, W[li, "xq"], list(range(4)), 16, big, CH, toks, epiq)
                    qdone = ("dve", kb.cnt["dve"])
                    kb.W("pe", qdone, tk_done, vtoks)
                    lastq = None
                    for h in range(4):
                        lastq = attention_head(
                            kb, T, R, NQT, [(0, 0), (0, 1)], lambda i, g, dl: True,
                            lambda g, i, h=h: qx[:, h, i * 128:(i + 1) * 128],
                            lambda g, dl, i, h=h: kx[:, h, dl * 128:(dl + 1) * 128],
                            lambda g, dl, i, h=h: vx[:, dl, h * 128:(h + 1) * 128],
                            lambda g, dl, i: T["ones"][:, :], None,
                            lambda i, h=h: ox[:, h, i * 128:(i + 1) * 128], prev if h == 0 else None, scale=1.0, LA=2)

                    def epi(bi, b, tt, ps, tm, c=c):
                        sl = slice(c * CH + tt * 512, c * CH + (tt + 1) * 512)
                        return epi_resid(kb, T, R, ps, tm, XS[b, :, sl], XS[b, :, sl])
                    prev = proj_fm(kb, T, R, W[li, "xo"], list(range(16)), 4, ox, CH, lastq, epi)
                end_phase(kb, T, [])

            phase({
                "big": ([128, 16, CH], BF16, "sb"),
                "mt": ([128, 16, 256], BF16, "sb"), "kx": ([128, 4, 256], BF16, "sb"), "vx": ([128, 2, 512], BF16, "sb"),
                "qx": ([128, 4, CH], BF16, "sb"), "ox": ([128, 4, CH], BF16, "sb"),
                "xs0": ([128, 16, 256], F32, "sb"), "xs1": ([128, 16, 256], F32, "sb"),
                "sq0": ([128, 16, 256], BF16, "sb"),
                "rs0": ([128, 512], F32, "sb"), "rs1": ([128, 512], F32, "sb"),
                "wr0": ([128, 2048], BF16, "sb"), "wr1": ([128, 2048], BF16, "sb"), "wr2": ([128, 2048], BF16, "sb"),
                "wv0": ([128, 8192], BF16, "sb"),
                "sq20": ([128, 512], BF16, "sb"), "sq21": ([128, 512], BF16, "sb"),
                "stg0": ([128, 8], BF16, "sb"), "stg1": ([128, 8], BF16, "sb"),
                "pt0": ([128, 512], BF16, "sb"), "pt1": ([128, 512], BF16, "sb"), "pt2": ([128, 512], BF16, "sb"),
                "rc0": ([128, 128], F32, "sb"), "rc1": ([128, 128], F32, "sb"),
                "xr0": ([128, 512], F32, "sb"), "xr1": ([128, 512], F32, "sb"), "xr2": ([128, 512], F32, "sb"),
                "pn0": ([128, 512], F32, "ps"),
                "pm0": ([128, 512], F32, "ps"), "ps1": ([128, 512], F32, "ps"), "ps2": ([128, 512], F32, "ps"),
                "po0": ([128, 512], F32, "ps"), "po1": ([128, 512], F32, "ps"),
                "pd0": ([128, 512], F32, "ps"), "pd1": ([128, 512], F32, "ps"),
            }, cross_script)
            if DBG_STOP == (li, 3):
                break

            xout = yT if last_layer else XS

            def mlpup_script(kb, T, li=li, gbase=gbase):
                R = mk_rings(kb, T, {"xs": 2, "sq": 1, "pn": 1, "rs": 2, "wr": 3, "pm": 4, "rl": 2, "stg": 3})
                big = T["big"]
                prev = None
                for c in range(NCH):
                    for e_ in ("dve", "pool", "sp"):
                        kb.W(e_, prev)
                    toks = norm_chunk(kb, T, R, XS, c, gbase + 48)

                    def epi(bi, b, tt, ps, tm, c=c):
                        ri = R["rl"].nxt()
                        rl = R["rl"].t[ri]
                        kb.W("act", tm, R["rl"].free[ri])
                        t1 = kb.I("act", lambda e: e.activation(out=rl[:], in_=ps[:, :], func=AF.Relu), sig=True)
                        si = R["stg"].nxt()
                        sg = R["stg"].t[si]
                        kb.W("pool", t1, R["stg"].free[si])
                        t2 = kb.I("pool", lambda e: e.tensor_tensor(out=sg[:], in0=rl[:], in1=rl[:], op=ALU.mult), sig=True)
                        R["rl"].free[ri] = t2
                        kb.W("sp", t2)
                        R["stg"].free[si] = kb.DMA("sp", R["stg"].sem[si], AT[b, :, c * CH + tt * 512: c * CH + (tt + 1) * 512], sg[:])
                        return t1
                    proj_fm(kb, T, R, W[li, "up"], list(range(64)), 16, big, CH, toks, epi)
                    prev = ("pe", kb.cnt["pe"])
                end_phase(kb, T, [])

            phase({
                "big": ([128, 16, CH], BF16, "sb"),
                "xs0": ([128, 16, 256], F32, "sb"), "xs1": ([128, 16, 256], F32, "sb"),
                "sq0": ([128, 16, 256], BF16, "sb"),
                "rs0": ([128, 512], F32, "sb"), "rs1": ([128, 512], F32, "sb"),
                "wr0": ([128, 2048], BF16, "sb"), "wr1": ([128, 2048], BF16, "sb"), "wr2": ([128, 2048], BF16, "sb"),
                "rl0": ([128, 512], F32, "sb"), "rl1": ([128, 512], F32, "sb"),
                "stg0": ([128, 512], BF16, "sb"), "stg1": ([128, 512], BF16, "sb"), "stg2": ([128, 512], BF16, "sb"),
                "pn0": ([128, 512], F32, "ps"),
                "pm0": ([128, 512], F32, "ps"), "pm1": ([128, 512], F32, "ps"), "pm2": ([128, 512], F32, "ps"), "pm3": ([128, 512], F32, "ps"),
            }, mlpup_script)

            def mlpdn_script(kb, T, li=li, xout=xout):
                R = mk_rings(kb, T, {"wd": 2, "pm": 4, "xr": 3, "ab": 1, "wb": 2})
                ab = T["abig"]
                twb = [None] * 16
                for s8 in range(T_ // 1024):
                    kb.W("sp", ("pe", kb.cnt["pe"]))
                    tla = []
                    for q4 in range(4):
                        tla.append(kb.DMA("sp", R["ab"].sem[0], ab[:, q4 * 16:(q4 + 1) * 16, :], AT[q4 * 16:(q4 + 1) * 16, :, s8 * 1024:(s8 + 1) * 1024].rearrange("k p t -> p k t")))
                    for b in range(16):
                        wi = R["wd"].nxt()
                        wt = R["wd"].t[wi]
                        if s8 == 0:
                            kb.W("pool", R["wd"].free[wi])
                            tw = kb.DMA("pool", R["wd"].sem[wi], wt[:, :], W[li, "dn"][b, :, :])
                            kb.W("act", tw)
                            twb[b] = kb.DMA("act", R["wb"].sem[wi], WDB[b, :, :], wt[:, :])
                        else:
                            kb.W("act", R["wd"].free[wi], [(sm, kb.cnt[sm]) for sm in R["wb"].sem])
                            tw = kb.DMA("act", R["wd"].sem[wi], wt[:, :], WDB[b, :, :])
                        kb.W("pe", tw, tla)
                        for tt in range(2):
                            pb = R["pm"].nxt()
                            ps = R["pm"].t[pb]
                            kb.W("pe", R["pm"].free[pb])
                            for kc in range(64):
                                tmm = kb.I("pe", lambda e, kc=kc, tt=tt: e.matmul(ps[:, :], wt[:, kc * 128:(kc + 1) * 128], ab[:, kc, tt * 512:(tt + 1) * 512], start=(kc == 0), stop=(kc == 63)), sig=(kc == 63))
                            sl = slice(s8 * 1024 + tt * 512, s8 * 1024 + (tt + 1) * 512)
                            R["pm"].free[pb] = epi_resid(kb, T, R, ps, tmm, XS[b, :, sl], xout[b, :, sl])
                        R["wd"].free[wi] = [tmm, twb[b]] if s8 == 0 else [tmm]
                end_phase(kb, T, [])

            phase({
                "abig": ([128, 64, 1024], BF16, "sb"),
                "wd0": ([128, 8192], BF16, "sb"), "wd1": ([128, 8192], BF16, "sb"),
                "ab0": ([128, 8], BF16, "sb"), "wb0": ([128, 8], BF16, "sb"), "wb1": ([128, 8], BF16, "sb"),
                "xr0": ([128, 512], F32, "sb"), "xr1": ([128, 512], F32, "sb"), "xr2": ([128, 512], F32, "sb"),
                "pm0": ([128, 512], F32, "ps"), "pm1": ([128, 512], F32, "ps"), "pm2": ([128, 512], F32, "ps"), "pm3": ([128, 512], F32, "ps"),
            }, mlpdn_script)
    return nc


def _blk(w, KC):
    K, M = w.shape
    return np.ascontiguousarray(w.reshape(KC, 128, M // 128, 128).transpose(2, 1, 0, 3).reshape(M // 128, 128, KC * 128))


def _blkv(w):
    K, M = w.shape
    return np.ascontiguousarray(w.reshape(16, 128, M // 512, 512).transpose(2, 1, 0, 3).reshape(M // 512, 128, 16 * 512))


def _t5_bucket(rel):
    nb = 16
    max_exact = 8
    ret = np.where(rel > 0, nb, 0)
    n = np.abs(rel)
    n_f = np.maximum(n, 1).astype(np.float32)
    large = max_exact + (np.log(n_f / np.float32(max_exact)) / np.float32(math.log(1024 / max_exact)) * np.float32(nb - max_exact)).astype(np.int32)
    large = np.minimum(large, nb - 1)
    return ret + np.where(n < max_exact, n, large)


def _dil_bias(t5_table):
    out = np.full((NH, 128, 27 * 128), NEG, np.float32)
    j = np.arange(128)[:, None]
    i = np.arange(128)[None, :]
    s = 0
    for g, (win, d) in enumerate(DIL):
        for dl in range(-REACH_DIL[g], REACH_DIL[g] + 1):
            rel = 128 * dl + j - i
            ok = (rel % d == 0) & (np.abs(rel) <= 64 * d)
            bk = _t5_bucket(rel)
            vals = t5_table[bk, g, :]
            blk = np.where(ok[None], vals.transpose(2, 0, 1), np.float32(NEG))
            out[:, :, s * 128:(s + 1) * 128] = blk
            s += 1
    return out


def _na_bias(rpb):
    out = np.full((NH, 128, 5, 7, 128), NEG, np.float32)
    R = 32
    kc = np.arange(64)
    cs = np.clip(kc - 8, 0, 48)
    col_ok = (kc[None, :] >= cs[:, None]) & (kc[None, :] < cs[:, None] + 16)
    dc = np.clip(kc[None, :] - kc[:, None], -15, 15) + 15
    for cls in range(5):
        qt = {0: 8, 1: 0, 2: 1, 3: 14, 4: 15}[cls]
        for si, dl in enumerate(range(-3, 4)):
            for a in range(2):
                r = 2 * qt + a
                if cls == 0:
                    rs = r - 4
                else:
                    rs = min(max(r - 4, 0), R - 8)
                for b in range(2):
                    kr = 2 * (qt + dl) + b
                    if kr < rs or kr >= rs + 8:
                        continue
                    dr = kr - r + 7
                    vals = rpb[:, dr, :][:, dc]
                    blk = np.where(col_ok[None], vals, np.float32(NEG)).transpose(0, 2, 1)
                    out[:, b * 64:(b + 1) * 64, cls, si, a * 64:(a + 1) * 64] = blk
    return np.ascontiguousarray(out.reshape(NH, 128, 5 * 7 * 128))


def _prep_weights(inp, LAYERS):
    Wd = {}
    f = np.float32
    for li, L in enumerate(LAYERS):
        l2 = L // 2
        if L % 2 == 0:
            w = inp["w_qkv_a"][l2]
            wq, wk, wv = w[:, 0:2048], w[:, 2048:4096], w[:, 4096:6144]
            Wd["wqk%d" % li] = _blk(np.concatenate([wq, wk], 1), 16)
            Wd["wv%d" % li] = _blkv(wv)
            Wd["bias%d" % li] = _na_bias(inp["rpb_a"][l2])
            Wd["wo%d" % li] = _blk(inp["w_o_a"][l2], 16)
        else:
            w = inp["w_qkv_b"][l2].reshape(2048, 3, 3, 2048)
            qk = np.concatenate([np.concatenate([w[:, g, 0], w[:, g, 1]], 1) for g in range(3)], 1)
            Wd["wqk%d" % li] = _blk(qk, 16)
            Wd["wv%d" % li] = _blkv(np.concatenate([w[:, g, 2] for g in range(3)], 1))
            Wd["bias%d" % li] = _dil_bias(inp["t5_table"])
            Wd["wo%d" % li] = _blk(inp["w_o_b"][l2], 16)
        Wd["wxq%d" % li] = _blk(inp["w_q_x"][L], 16)
        wkv = inp["w_kv_x"][L]
        Wd["wxk%d" % li] = _blk(wkv[:, 0:512], 16)
        Wd["wxv%d" % li] = _blkv(wkv[:, 512:1024])
        Wd["wxo%d" % li] = _blk(inp["w_o_x"][L], 4)
        Wd["wup%d" % li] = _blk(inp["w_up"][L], 16)
        Wd["wdn%d" % li] = _blk(inp["w_down"][L], 64)
    return Wd


def _vec_inputs(inp, LAYERS):
    gv = np.zeros((128, 4 * len(LAYERS) * 16), np.float32)
    hv = np.zeros((128, len(LAYERS) * 16), np.float32)
    for li, L in enumerate(LAYERS):
        for j, nm in enumerate(("g_mix", "g_cross", "g_mem", "g_mlp")):
            gv[:, li * 64 + j * 16: li * 64 + (j + 1) * 16] = inp[nm][L].reshape(16, 128).T
        l2 = L // 2
        if L % 2 == 0:
            hv[:, li * 16 + 0] = inp["q_norm_a"][l2]
            hv[:, li * 16 + 1] = inp["k_norm_a"][l2]
        else:
            for g in range(3):
                hv[:, li * 16 + 2 * g] = inp["q_norm_b"][l2, g]
                hv[:, li * 16 + 2 * g + 1] = inp["k_norm_b"][l2, g]
        hv[:, li * 16 + 6] = inp["q_norm_x"][L]
        hv[:, li * 16 + 7] = inp["k_norm_x"][L]
    return gv, hv


def _flags(kind, NCH):
    fl = np.zeros((128, 32), np.float32)
    for c in range(NCH):
        if kind == "prompt":
            top = 1.0 if c == 0 else 0.0
            bot = 1.0 if c == NCH - 1 else 0.0
        else:
            top = bot = 1.0
        fl[:, c] = top
        fl[:, 4 + c] = 1.0 - top
        fl[:, 8 + c] = bot
        fl[:, 12 + c] = 1.0 - bot
        fl[:, 16 + c] = 1.0 - top
        fl[:, 20 + c] = 1.0 - bot
    return fl


_CACHE = {}


def _run(inp, NCH, LAYERS, seqs, DBG_STOP=None, ncores=8):
    key = (NCH, tuple(LAYERS), DBG_STOP)
    if key not in _CACHE:
        _CACHE[key] = build(NCH, LAYERS, DBG_STOP)
    nc = _CACHE[key]
    Wd = _prep_weights(inp, LAYERS)
    gv, hv = _vec_inputs(inp, LAYERS)
    in_maps = []
    for (kind, xT, memT) in seqs:
        m = dict(Wd)
        m["xT"] = np.ascontiguousarray(xT.reshape(16, 128, NCH * CH))
        m["memT"] = np.ascontiguousarray(memT.reshape(NCH, 16, 128, 256))
        m["gvec"] = gv
        m["hvec"] = hv
        m["flags"] = _flags(kind, NCH)
        m["onesm"] = np.ones((128, 128), np.float32)
        in_maps.append(m)
    while len(in_maps) < ncores:
        in_maps.append(in_maps[-1])
    res = run_bass_kernel_spmd(nc, in_maps, core_ids=list(range(ncores)))
    return [r["yT"].reshape(2048, NCH * CH) for r in res.results]


def kernel(**inp):
    inp = {k: np.asarray(v) for k, v in inp.items()}
    LAYERS = [0, 1, 2, 3]
    NCH = 4
    xp = inp["x_prompt"][0]
    xs = inp["x_sample"].reshape(4 * 2048, 2048)
    memp = np.broadcast_to(inp["mem_prompt"][0].T[None], (4, 2048, 256))
    mems = inp["mem_sample"].transpose(0, 2, 1)
    seqs = [("prompt", np.ascontiguousarray(xp.T), np.ascontiguousarray(memp)),
            ("sample", np.ascontiguousarray(xs.T), np.ascontiguousarray(mems))]
    outs = _run(inp, NCH, LAYERS, seqs, ncores=2)
    y_prompt = np.ascontiguousarray(outs[0].T).reshape(1, 8192, 2048).astype(np.float32)
    y_sample = np.ascontiguousarray(outs[1].T).reshape(4, 2048, 2048).astype(np.float32)
    return (y_prompt, y_sample)
```

```python
import math
import contextlib
import numpy as np
import concourse.bass as bass
import concourse.mybir as mybir
from concourse.bass_utils import run_bass_kernel_spmd

F32 = mybir.dt.float32
BF16 = mybir.dt.bfloat16
AF = mybir.ActivationFunctionType
ALU = mybir.AluOpType

D = 2048
DH = 128
NH = 16
CH = 2048
NQT = CH // 128
PAD = 1024
EPS = 1e-6
SCALE = 1.0 / math.sqrt(DH)
NEG = -30000.0
DIL = ((128, 1), (512, 4), (2048, 16))
REACH_DIL = (1, 2, 8)
REACH_NA = 3
ENGS = ("sp", "act", "pool", "pe", "dve")


class KB:
    def __init__(self, nc, sems):
        self.nc = nc
        self.sems = sems
        self.cnt = {k: 0 for k in sems}
        self.cur = None
        self.E = None
        self.waited = {}
        self.free_sems = [k for k in sems if k.startswith("r")]
        self.ring_sem = {}

    def I(self, eng, fn, sig=False):
        tok = None
        if sig:
            self.cnt[eng] += 1
            tok = (eng, self.cnt[eng])
        if self.cur == eng:
            ins = fn(self.E)
            if sig:
                ins.then_inc(self.sems[eng], 1)
        return tok

    def DMA(self, q, sem, out, in_):
        self.cnt[sem] += 16
        tok = (sem, self.cnt[sem])
        if self.cur == q:
            self.E.dma_start(out=out, in_=in_).then_inc(self.sems[sem], 16)
        return tok

    def W(self, eng, *toks):
        if self.cur != eng:
            return
        for t in toks:
            if t is None:
                continue
            if isinstance(t, list):
                self.W(eng, *t)
                continue
            s, v = t
            if self.waited.get((eng, s), 0) < v:
                self.E.wait_ge(self.sems[s], v)
                self.waited[(eng, s)] = v

    def state(self):
        return (dict(self.cnt),)

    def restore(self, st):
        self.cnt = dict(st[0])


class Ring:
    def __init__(self, kb, name, tiles, semnames):
        self.kb = kb
        self.t = tiles
        self.n = len(tiles)
        self.sem = semnames
        self.reset()

    def reset(self):
        self.use = 0
        self.free = [None] * self.n

    def nxt(self):
        i = self.use % self.n
        self.use += 1
        return i


def run_phase(nc, kb, tensors, nsem_rings, script):
    with contextlib.ExitStack() as st:
        T = {}
        for name, (shape, dt, kind) in tensors.items():
            if kind == "sb":
                T[name] = st.enter_context(nc.sbuf_tensor(name + "_%d" % kb.phase_id, shape, dt))
            else:
                T[name] = st.enter_context(nc.psum_tensor(name + "_%d" % kb.phase_id, shape, dt))
        block = st.enter_context(nc.Block())
        s0 = kb.state()

        def mk(engname):
            def f(e):
                kb.restore(s0)
                kb.cur = engname
                kb.E = e
                script(kb, T)
                kb.cur = None
            return f

        block.sync(mk("sp"))
        block.scalar(mk("act"))
        block.gpsimd(mk("pool"))
        block.tensor(mk("pe"))
        block.vector(mk("dve"))
    kb.phase_id += 1


def end_phase(kb, T, extra):
    toks = list(extra)
    toks.append(kb.I("act", lambda e: e.activation(out=T["scr"][:, 0:1], in_=T["scr"][:, 1:2], func=AF.Copy), sig=True))
    toks.append(kb.I("dve", lambda e: e.tensor_copy(out=T["scr"][:, 2:3], in_=T["scr"][:, 3:4]), sig=True))
    toks.append(kb.I("pool", lambda e: e.tensor_copy(out=T["scr"][:, 4:5], in_=T["scr"][:, 5:6]), sig=True))
    for s in kb.cnt:
        if s.startswith("r") and kb.cnt[s] > 0:
            toks.append((s, kb.cnt[s]))
    kb.W("pe", *toks)
    toks.append(kb.I("pe", lambda e: e.matmul(T["pm0"][:, 0:8], T["ones"][:, 0:128], T["ones"][:, 0:8], start=True, stop=True), sig=True))
    for s in kb.cnt:
        if s.startswith("r") and kb.cnt[s] > 0:
            toks.append((s, kb.cnt[s]))
    for e in ENGS:
        kb.W(e, *toks)


def build(NCH, LAYERS, DBG_STOP=None):
    T_ = NCH * CH
    TP = T_ + 2 * PAD
    nc = bass.Bass("TRN2", target_bir_lowering=False)

    def din(name, shape, dt=F32):
        return nc.dram_tensor(name, list(shape), dt, kind="ExternalInput")

    xT_in = din("xT", [16, 128, T_])
    memT_in = din("memT", [NCH, 16, 128, 256])
    gvec_in = din("gvec", [128, 4 * len(LAYERS) * 16])
    hvec_in = din("hvec", [128, len(LAYERS) * 16])
    flags_in = din("flags", [128, 32])
    ones_in = din("onesm", [128, 128])
    W = {}
    for li, L in enumerate(LAYERS):
        if L % 2 == 0:
            W[li, "qk"] = din("wqk%d" % li, [32, 128, 16 * 128])
            W[li, "v"] = din("wv%d" % li, [4, 128, 16 * 512])
            W[li, "bias"] = din("bias%d" % li, [NH, 128, 5 * 7 * 128])
        else:
            W[li, "qk"] = din("wqk%d" % li, [96, 128, 16 * 128])
            W[li, "v"] = din("wv%d" % li, [12, 128, 16 * 512])
            W[li, "bias"] = din("bias%d" % li, [NH, 128, 27 * 128])
        W[li, "o"] = din("wo%d" % li, [16, 128, 16 * 128])
        W[li, "xq"] = din("wxq%d" % li, [4, 128, 16 * 128])
        W[li, "xk"] = din("wxk%d" % li, [4, 128, 16 * 128])
        W[li, "xv"] = din("wxv%d" % li, [1, 128, 16 * 512])
        W[li, "xo"] = din("wxo%d" % li, [16, 128, 4 * 128])
        W[li, "up"] = din("wup%d" % li, [64, 128, 16 * 128])
        W[li, "dn"] = din("wdn%d" % li, [16, 128, 64 * 128])
    yT = nc.dram_tensor("yT", [16, 128, T_], F32, kind="ExternalOutput")
    XS = yT if DBG_STOP is not None else nc.dram_tensor("xscr", [16, 128, T_], F32)
    QT = [nc.dram_tensor("qt%d" % g, [NH, 128, T_], BF16) for g in range(3)]
    KT = [nc.dram_tensor("kt%d" % g, [NH, 128, TP], BF16) for g in range(3)]
    VV = [nc.dram_tensor("vv%d" % g, [NH, TP, 128], BF16) for g in range(3)]
    AT = nc.dram_tensor("at", [64, 128, T_], BF16)
    WDB = nc.dram_tensor("wdb", [16, 128, 8192], BF16)

    semnames = list(ENGS) + ["r%d" % i for i in range(64)]
    with contextlib.ExitStack() as st:
        sems = {k: st.enter_context(nc.semaphore("s_" + k)) for k in semnames}
        kb = KB(nc, sems)
        kb.phase_id = 0
        rsem = iter(["r%d" % i for i in range(64)])
        RS = {}

        def ringsem(name, n):
            if name not in RS:
                RS[name] = []
            while len(RS[name]) < n:
                RS[name].append(next(rsem))
            return RS[name]

        ones = st.enter_context(nc.sbuf_tensor("ones", [128, 128], BF16))
        gvec = st.enter_context(nc.sbuf_tensor("gvec_sb", [128, 4 * len(LAYERS) * 16], F32))
        hvec = st.enter_context(nc.sbuf_tensor("hvec_sb", [128, len(LAYERS) * 16], F32))
        flags = st.enter_context(nc.sbuf_tensor("flags_sb", [128, 32], F32))
        scr = st.enter_context(nc.sbuf_tensor("scr", [128, 8], F32))
        zer = st.enter_context(nc.sbuf_tensor("zer", [128, 1024], BF16))
        epsb = st.enter_context(nc.sbuf_tensor("epsb", [128, 2], F32))
        COMMON = {"epsb": epsb, "ones": ones, "gvec": gvec, "hvec": hvec, "flags": flags, "scr": scr, "zer": zer}

        def phase(tensors, script):
            tensors = dict(tensors)
            if "pm0" not in tensors:
                tensors["pm0"] = ([128, 512], F32, "ps")

            def s2(kb, T):
                T = dict(T)
                T.update(COMMON)
                if "ps0" not in T and "ps1" in T:
                    T["ps0"] = T["pm0"]
                script(kb, T)
            run_phase(nc, kb, tensors, 0, s2)

        def init_script(kb, T):
            s = ringsem("init", 1)[0]
            t = []
            t.append(kb.DMA("pool", s, T["ones"][:], ones_in[:, :]))
            t.append(kb.DMA("sp", s, T["gvec"][:], gvec_in[:, :]))
            t.append(kb.DMA("sp", s, T["hvec"][:], hvec_in[:, :]))
            t.append(kb.DMA("sp", s, T["flags"][:], flags_in[:, :]))
            kb.W("dve", t[1], t[2])
            kb.I("dve", lambda e: e.tensor_scalar(out=T["gvec"][:], in0=T["gvec"][:], scalar1=float(math.sqrt(D)), scalar2=None, op0=ALU.mult), sig=True)
            hk = T["hvec"][:].rearrange("p (a two) -> p a two", two=2)[:, :, 1]
            kb.I("dve", lambda e: e.tensor_scalar(out=hk, in0=hk, scalar1=float(math.sqrt(DH)), scalar2=None, op0=ALU.mult), sig=True)
            kb.W("pe", t[0])
            t1 = kb.I("dve", lambda e: e.memset(T["zer"][:], 0.0), sig=True)
            kb.I("dve", lambda e: e.memset(T["epsb"][:, 0:1], float(D * EPS)), sig=True)
            kb.I("dve", lambda e: e.memset(T["epsb"][:, 1:2], float(DH * EPS)), sig=True)
            t2 = kb.I("dve", lambda e: e.memset(T["scr"][:], 0.0), sig=True)
            kb.W("sp", t1)
            for g in range(3):
                for h in range(NH):
                    kb.DMA("sp", s, KT[g][h, :, 0:PAD], T["zer"][:, 0:PAD])
                    kb.DMA("sp", s, KT[g][h, :, PAD + T_:TP], T["zer"][:, 0:PAD])
                    kb.DMA("sp", s, VV[g][h, 0:PAD, :].rearrange("(a p) d -> p a d", p=128), T["zer"][:, 0:PAD].rearrange("p (a d) -> p a d", d=128))
                    kb.DMA("sp", s, VV[g][h, PAD + T_:TP, :].rearrange("(a p) d -> p a d", p=128), T["zer"][:, 0:PAD].rearrange("p (a d) -> p a d", d=128))
            end_phase(kb, T, [t2])

        phase({}, init_script)

        class Ctx:
            pass

        def mk_rings(kb, T, spec):
            R = {}
            for name, n in spec.items():
                R[name] = Ring(kb, name, [T["%s%d" % (name, i)] for i in range(n)], ringsem(name, n))
            return R

        def norm_chunk(kb, T, R, src, c, gcol, ntok=CH, tok0=None, dst=None, srcap=None):
            dst = T["big"] if dst is None else dst
            done = []
            nsub = ntok // 256
            for j in range(nsub):
                i = R["xs"].nxt()
                xs = R["xs"].t[i]
                kb.W("sp", R["xs"].free[i])
                if srcap is None:
                    a = src[:, :, c * CH + j * 256: c * CH + (j + 1) * 256].rearrange("k p t -> p k t")
                else:
                    a = srcap[:, :, j * 256:(j + 1) * 256].rearrange("k p t -> p k t")
                tl = kb.DMA("sp", R["xs"].sem[i], xs[:], a)
                kb.W("act", tl)
                qi = R["sq"].nxt()
                sq = R["sq"].t[qi]
                kb.W("act", R["sq"].free[qi])
                t1 = kb.I("act", lambda e: e.activation(out=sq[:], in_=xs[:], func=AF.Square), sig=True)
                pb = R["pn"].nxt()
                ps = R["pn"].t[pb]
                kb.W("pe", t1, R["pn"].free[pb])
                for kc in range(16):
                    t2 = kb.I("pe", lambda e, kc=kc: e.matmul(ps[:, 0:256], T["ones"][:, :], sq[:, kc, :], start=(kc == 0), stop=(kc == 15)), sig=(kc == 15))
                R["sq"].free[qi] = t2
                ri = R["rs"].nxt()
                rs = R["rs"].t[ri]
                kb.W("dve", t2, R["rs"].free[ri])
                kb.W("act", ("pe", kb.cnt["pe"]), R["rs"].free[ri])
                t3a = kb.I("act", lambda e: e.activation(out=rs[:, 0:256], in_=ps[:, 0:256], func=AF.Sqrt, bias=T["epsb"][:, 0:1], scale=1.0), sig=True)
                kb.W("dve", t3a)
                t3 = kb.I("dve", lambda e: e.reciprocal(out=rs[:, 0:256], in_=rs[:, 0:256]), sig=True)
                kb.W("dve", t3)
                R["pn"].free[pb] = t3
                kb.W("dve", t3, tl)
                last = []
                for kc in range(16):
                    eng = "dve"
                    tk = kb.I(eng, lambda e, kc=kc: e.scalar_tensor_tensor(
                        out=dst[:, kc, j * 256:(j + 1) * 256], in0=xs[:, kc, :], scalar=T["gvec"][:, gcol + kc:gcol + kc + 1],
                        in1=rs[:, 0:256], op0=ALU.mult, op1=ALU.mult), sig=(kc == 15))
                    if tk is not None:
                        last.append(tk)
                R["xs"].free[i] = last
                R["rs"].free[ri] = last
                done += last
            return done

        def proj_fm(kb, T, R, wd, blocks, KC, src, ntok, src_toks, epi, wview=None):
            ntt = ntok // 512
            pend = None
            for bi, b in enumerate(blocks):
                wi = R["wr"].nxt()
                wt = R["wr"].t[wi]
                kb.W("pool", R["wr"].free[wi])
                tw = kb.DMA("pool", R["wr"].sem[wi], wt[:, 0:KC * 128], wd[b, :, :])
                kb.W("pe", tw, src_toks)
                for tt in range(ntt):
                    pb = R["pm"].nxt()
                    ps = R["pm"].t[pb]
                    kb.W("pe", R["pm"].free[pb])
                    for kc in range(KC):
                        tm = kb.I("pe", lambda e, kc=kc, tt=tt: e.matmul(ps[:, :], wt[:, kc * 128:(kc + 1) * 128], src[:, kc, tt * 512:(tt + 1) * 512], start=(kc == 0), stop=(kc == KC - 1)), sig=(kc == KC - 1))
                    r = epi(bi, b, tt, ps, tm)
                    if pend is not None:
                        R["pm"].free[pend[0]] = pend[1]()
                        pend = None
                    if callable(r):
                        pend = (pb, r)
                    else:
                        R["pm"].free[pb] = r
                R["wr"].free[wi] = tm
            if pend is not None:
                R["pm"].free[pend[0]] = pend[1]()
            return tm

        def epi_qknorm(kb, T, R, ps, tm, gain_ap, dst_ap):
            qi = R["sq2"].nxt()
            sq = R["sq2"].t[qi]
            kb.W("act", tm, R["sq2"].free[qi])
            t1 = kb.I("act", lambda e: e.activation(out=sq[:], in_=ps[:, :], func=AF.Square), sig=True)

            def cont():
                pb = R["pn"].nxt()
                ps2 = R["pn"].t[pb]
                kb.W("pe", t1, R["pn"].free[pb])
                t2 = kb.I("pe", lambda e: e.matmul(ps2[:, :], T["ones"][:, :], sq[:], start=True, stop=True), sig=True)
                R["sq2"].free[qi] = t2
                ri = R["rs"].nxt()
                rs = R["rs"].t[ri]
                kb.W("act", t2, R["rs"].free[ri])
                t3a = kb.I("act", lambda e: e.activation(out=rs[:, :], in_=ps2[:, :], func=AF.Sqrt, bias=T["epsb"][:, 1:2], scale=1.0), sig=True)
                kb.W("dve", t3a, tm)
                t3 = kb.I("dve", lambda e: e.reciprocal(out=rs[:, :], in_=rs[:, :]), sig=True)
                kb.W("dve", t3)
                R["pn"].free[pb] = t3
                si = R["stg"].nxt()
                sg = R["stg"].t[si]
                kb.W("dve", R["stg"].free[si])
                t4 = kb.I("dve", lambda e: e.scalar_tensor_tensor(out=sg[:], in0=ps[:, :], scalar=gain_ap, in1=rs[:, :], op0=ALU.mult, op1=ALU.mult), sig=True)
                R["rs"].free[ri] = t4
                if dst_ap is not None:
                    kb.W("sp", t4)
                    R["stg"].free[si] = kb.DMA("sp", R["stg"].sem[si], dst_ap, sg[:])
                return t4
            return cont

        def epi_resid(kb, T, R, ps, tm, src_ap, dst_ap):
            xi = R["xr"].nxt()
            xr = R["xr"].t[xi]
            kb.W("sp", R["xr"].free[xi])
            tl = kb.DMA("sp", R["xr"].sem[xi], xr[:], src_ap)
            kb.W("dve", tl, tm)
            t1 = kb.I("dve", lambda e: e.tensor_tensor(out=xr[:], in0=ps[:, :], in1=xr[:], op=ALU.add), sig=True)
            kb.W("sp", t1)
            R["xr"].free[xi] = kb.DMA("sp", R["xr"].sem[xi], dst_ap, xr[:])
            return t1

        def proj_tm(kb, T, R, wd, g, src, ntiles, src_toks, epi):
            wi = R["wv"].nxt()
            wt = R["wv"].t[wi]
            kb.W("pool", R["wv"].free[wi])
            tw = kb.DMA("pool", R["wv"].sem[wi], wt[:], wd[g, :, :])
            kb.W("pe", tw, src_toks)
            for t16 in range(ntiles):
                pb = R["pm"].nxt()
                ps = R["pm"].t[pb]
                kb.W("pe", R["pm"].free[pb])
                for kc in range(16):
                    tm = kb.I("pe", lambda e, kc=kc, t16=t16: e.matmul(ps[:, :], src[:, kc, t16 * 128:(t16 + 1) * 128], wt[:, kc * 512:(kc + 1) * 512], start=(kc == 0), stop=(kc == 15)), sig=(kc == 15))
                R["pm"].free[pb] = epi(t16, ps, tm)
            R["wv"].free[wi] = tm

        def attention_head(kb, T, R, nqt, slots, active, qtile, ksrc, vsrc, den_lhsT, bias_fn, out_ap, first_dep, scale=None, LA=3):
            units = []
            nb = (len(slots) + 3) // 4
            for i in range(nqt):
                ulist = []
                for bi in range(nb):
                    bs = slots[bi * 4:(bi + 1) * 4]
                    act = [active(i, g, dl) for (g, dl) in bs]
                    if any(act):
                        ulist.append([i, bi, bs, act, False, False])
                ulist[0][4] = True
                ulist[-1][5] = True
                units += ulist
            state = {}
            cur = {}
            res = {"last": None}

            def emit_S(u):
                i, bi, bs, act, fb, lb = units[u]
                n = len(bs)
                si = R["ps"].nxt()
                ps = R["ps"].t[si]
                kb.W("pe", R["ps"].free[si])
                lastj = max(j for j in range(n) if act[j])
                for j, (g, dl) in enumerate(bs):
                    if not act[j]:
                        continue
                    tS = kb.I("pe", lambda e, j=j, g=g, dl=dl: e.matmul(ps[:, j * 128:(j + 1) * 128], ksrc(g, dl, i), qtile(g, i), start=True, stop=True), sig=(j == lastj))
                pi = R["pt"].nxt()
                pt = R["pt"].t[pi]
                if bias_fn is not None:
                    ti = R["tm"].nxt()
                    tmp = R["tm"].t[ti]
                    kb.W("dve", tS, R["tm"].free[ti])
                    tB = bias_fn(i, bi, bs, tmp, ps)
                    R["ps"].free[si] = tB
                    kb.W("act", tB, R["pt"].free[pi])
                    tE = kb.I("act", lambda e: e.activation(out=pt[:, 0:n * 128], in_=tmp[:, 0:n * 128], func=AF.Exp), sig=True)
                    R["tm"].free[ti] = tE
                else:
                    kb.W("act", tS, R["pt"].free[pi])
                    tE = kb.I("act", lambda e: e.activation(out=pt[:, 0:n * 128], in_=ps[:, 0:n * 128], func=AF.Exp, scale=float(scale)), sig=True)
                    R["ps"].free[si] = tE
                state[u] = (pi, pt, tE)

            def emit_PV(u):
                i, bi, bs, act, fb, lb = units[u]
                n = len(bs)
                pi, pt, tE = state.pop(u)
                if fb:
                    oi = R["po"].nxt()
                    cur["oi"] = oi
                    kb.W("pe", R["po"].free[oi])
                oi = cur["oi"]
                po = R["po"].t[oi]
                pd = R["pd"].t[oi]
                kb.W("pe", tE)
                js = [j for j in range(n) if act[j]]
                for j in js:
                    g, dl = bs[j]
                    st_ = (fb and j == js[0])
                    sp_ = (lb and j == js[-1])
                    kb.I("pe", lambda e, j=j, g=g, dl=dl, st_=st_, sp_=sp_: e.matmul(po[:, 0:128], vsrc(g, dl, i), pt[:, j * 128:(j + 1) * 128], start=st_, stop=sp_))
                    tP = kb.I("pe", lambda e, j=j, g=g, dl=dl, st_=st_, sp_=sp_: e.matmul(pd[:, 0:128], den_lhsT(g, dl, i), pt[:, j * 128:(j + 1) * 128], start=st_, stop=sp_), sig=(j == js[-1]))
                R["pt"].free[pi] = tP
                if lb:
                    ri = R["rc"].nxt()
                    rc = R["rc"].t[ri]
                    kb.W("dve", tP, R["rc"].free[ri], first_dep if i == 0 else None)
                    t1 = kb.I("dve", lambda e: e.reciprocal(out=rc[:, :], in_=pd[:, 0:128]), sig=True)
                    kb.W("dve", t1)
                    t2 = kb.I("dve", lambda e: e.tensor_tensor(out=out_ap(i), in0=po[:, 0:128], in1=rc[:, :], op=ALU.mult), sig=True)
                    R["rc"].free[ri] = t2
                    R["po"].free[oi] = t2
                    res["last"] = t2

            nu = len(units)
            for u in range(min(LA, nu)):
                emit_S(u)
            for u in range(nu):
                if u + LA < nu:
                    emit_S(u + LA)
                emit_PV(u)
            return res["last"]

        xsrc = xT_in
        for li, L in enumerate(LAYERS):
            is_na = (L % 2 == 0)
            ng = 1 if is_na else 3
            gbase = li * 64
            hbase = li * 16
            last_layer = (li == len(LAYERS) - 1)

            def qkv_script(kb, T, li=li, is_na=is_na, ng=ng, gbase=gbase, hbase=hbase, xsrc=xsrc):
                R = mk_rings(kb, T, {"xs": 2, "sq": 1, "pn": 2, "rs": 2, "wr": 3, "pm": 4, "sq2": 2, "stg": 3, "wv": 2, "vst": 2})
                big = T["big"]
                prev = None
                for c in range(NCH):
                    for e_ in ("dve", "pool"):
                        kb.W(e_, prev)
                    toks = norm_chunk(kb, T, R, xsrc, c, gbase + 0)

                    def epi(bi, b, tt, ps, tm, c=c):
                        g = b // 32
                        isk = (b % 32) >= 16
                        h = b % 16
                        gain = T["hvec"][:, hbase + g * 2 + (1 if isk else 0): hbase + g * 2 + (1 if isk else 0) + 1]
                        if isk:
                            dst = KT[g][h, :, PAD + c * CH + tt * 512: PAD + c * CH + (tt + 1) * 512]
                        else:
                            dst = QT[g][h, :, c * CH + tt * 512: c * CH + (tt + 1) * 512]
                        return epi_qknorm(kb, T, R, ps, tm, gain, dst)
                    tl1 = proj_fm(kb, T, R, W[li, "qk"], list(range(32 * ng)), 16, big, CH, toks, epi)

                    for vg in range(4 * ng):
                        def epiv(t16, ps, tm, vg=vg, c=c):
                            g = vg // 4
                            vi = R["vst"].nxt()
                            vs = R["vst"].t[vi]
                            kb.W("act", tm, R["vst"].free[vi])
                            t1 = kb.I("act", lambda e: e.activation(out=vs[:], in_=ps[:, :], func=AF.Copy), sig=True)
                            kb.W("sp", t1)
                            r0 = PAD + c * CH + t16 * 128
                            dst = VV[g][(vg % 4) * 4:(vg % 4) * 4 + 4, r0:r0 + 128, :].rearrange("h p d -> p h d")
                            R["vst"].free[vi] = kb.DMA("sp", R["vst"].sem[vi], dst, vs[:].rearrange("p (h d) -> p h d", d=128))
                            return t1
                        proj_tm(kb, T, R, W[li, "v"], vg, big, 16, toks, epiv)
                    prev = ("pe", kb.cnt["pe"])
                end_phase(kb, T, [])

            phase({
                "big": ([128, 16, CH], BF16, "sb"),
                "xs0": ([128, 16, 256], F32, "sb"), "xs1": ([128, 16, 256], F32, "sb"),
                "sq0": ([128, 16, 256], BF16, "sb"),
                "rs0": ([128, 512], F32, "sb"), "rs1": ([128, 512], F32, "sb"),
                "wr0": ([128, 2048], BF16, "sb"), "wr1": ([128, 2048], BF16, "sb"), "wr2": ([128, 2048], BF16, "sb"),
                "wv0": ([128, 8192], BF16, "sb"), "wv1": ([128, 8192], BF16, "sb"),
                "sq20": ([128, 512], BF16, "sb"), "sq21": ([128, 512], BF16, "sb"),
                "stg0": ([128, 512], BF16, "sb"), "stg1": ([128, 512], BF16, "sb"), "stg2": ([128, 512], BF16, "sb"),
                "vst0": ([128, 512], BF16, "sb"), "vst1": ([128, 512], BF16, "sb"),
                "pn0": ([128, 512], F32, "ps"), "pn1": ([128, 512], F32, "ps"),
                "pm0": ([128, 512], F32, "ps"), "pm1": ([128, 512], F32, "ps"), "pm2": ([128, 512], F32, "ps"), "pm3": ([128, 512], F32, "ps"),
            }, qkv_script)
            if DBG_STOP == (li, 1):
                break

            if is_na:
                slots = [(0, d) for d in range(-REACH_NA, REACH_NA + 1)]
                nbias = 5 * 7 * 128
            else:
                slots = [(g, d) for g in range(3) for d in range(-REACH_DIL[g], REACH_DIL[g] + 1)]
                nbias = 27 * 128
            reach = [REACH_NA] if is_na else list(REACH_DIL)
            kw = [CH + 2 * r * 128 for r in reach]
            koff = [sum(kw[:g]) for g in range(ng)]
            vt = [NQT + 2 * r for r in reach]
            voff = [sum(vt[:g]) for g in range(ng)]
            xdst = XS

            def att_script(kb, T, li=li, is_na=is_na, ng=ng, slots=slots, reach=reach, kw=kw, koff=koff, vt=vt, voff=voff, xsrc=xsrc, xdst=xdst, nbias=nbias):
                R = mk_rings(kb, T, {"qk": 2, "vb": 2, "bb": 2, "ps": 4, "po": 2, "pd": 2, "pt": 4, "tm": 4, "rc": 2, "wr": 2, "xr": 2})
                R["pm"] = R["ps"]
                if not is_na:
                    tvo = None
                    for idx in range(8):
                        tvo = kb.I("dve", lambda e, idx=idx: e.tensor_scalar(out=T["vo"][:, idx, :], in0=T["ones"][:, :], scalar1=T["flags"][:, 16 + idx:17 + idx], scalar2=None, op0=ALU.mult), sig=True)
                    kb.W("pe", tvo)
                big = T["big"]
                prev_wo = None
                for c in range(NCH):
                    head_toks = []
                    for h in range(NH):
                        qi = R["qk"].nxt()
                        qk = R["qk"].t[qi]
                        kb.W("sp", R["qk"].free[qi])
                        tl = []
                        for g in range(ng):
                            tl.append(kb.DMA("sp", R["qk"].sem[qi], qk[:, g * CH:(g + 1) * CH], QT[g][h, :, c * CH:(c + 1) * CH]))
                            k0 = PAD + c * CH - reach[g] * 128
                            tl.append(kb.DMA("sp", R["qk"].sem[qi], qk[:, 3 * CH + koff[g]: 3 * CH + koff[g] + kw[g]], KT[g][h, :, k0:k0 + kw[g]]))
                        vi = R["vb"].nxt()
                        vb = R["vb"].t[vi]
                        bi_ = R["bb"].nxt()
                        bb = R["bb"].t[bi_]
                        kb.W("sp", R["vb"].free[vi])
                        kb.W("pool", R["bb"].free[bi_])
                        for g in range(ng):
                            r0 = PAD + c * CH - reach[g] * 128
                            tl.append(kb.DMA("sp", R["vb"].sem[vi], vb[:, voff[g]:voff[g] + vt[g], :], VV[g][h, r0:r0 + vt[g] * 128, :].rearrange("(a p) d -> p a d", p=128)))
                        tl.append(kb.DMA("pool", R["bb"].sem[bi_], bb[:, 0:nbias], W[li, "bias"][h, :, :]))
                        kb.W("pe", tl)
                        kb.W("dve", tl)
                        kb.W("pool", tl)
                        tmask = []
                        if not is_na:
                            kb.W("pool", tl)
                            for g in range(ng):
                                if c > 0:
                                    tmask.append(kb.I("pool", lambda e, g=g: e.tensor_scalar(out=vb[:, voff[g]:voff[g] + reach[g], :], in0=vb[:, voff[g]:voff[g] + reach[g], :], scalar1=T["flags"][:, 16 + c:17 + c], scalar2=None, op0=ALU.mult), sig=True))
                                if c < NCH - 1:
                                    o_ = voff[g] + reach[g] + NQT
                                    tmask.append(kb.I("pool", lambda e, g=g, o_=o_: e.tensor_scalar(out=vb[:, o_:o_ + reach[g], :], in0=vb[:, o_:o_ + reach[g], :], scalar1=T["flags"][:, 20 + c:21 + c], scalar2=None, op0=ALU.mult), sig=True))
                            kb.W("pe", tmask)

                        def active(i, g, dl):
                            if c == 0 and i + dl < 0:
                                return False
                            if c == NCH - 1 and i + dl >= NQT:
                                return False
                            return True

                        def qtile(g, i):
                            return qk[:, g * CH + i * 128: g * CH + (i + 1) * 128]

                        def ksrc(g, dl, i):
                            o = 3 * CH + koff[g] + (i + dl + reach[g]) * 128
                            return qk[:, o:o + 128]

                        def vsrc(g, dl, i):
                            return vb[:, voff[g] + i + dl + reach[g], :]

                        def den_lhsT(g, dl, i):
                            if is_na or (0 <= i + dl < NQT):
                                return T["ones"][:, :]
                            return T["vo"][:, (c if i + dl < 0 else 4 + c), :]

                        def bias_fn(i, bi, bs, tmp, ps):
                            n = len(bs)
                            off = bi * 512
                            if is_na:
                                cls = {0: 1, 1: 2, NQT - 2: 3, NQT - 1: 4}.get(i, 0)
                                if cls == 0:
                                    return kb.I("dve", lambda e: e.tensor_tensor(out=tmp[:, 0:n * 128], in0=ps[:, 0:n * 128], in1=bb[:, off:off + n * 128], op=ALU.add), sig=True)
                                top = cls in (1, 2)
                                fcol = (0 if top else 8) + c
                                tq = kb.I("dve", lambda e: e.scalar_tensor_tensor(out=tmp[:, 0:n * 128], in0=bb[:, off:off + n * 128], scalar=T["flags"][:, fcol + 4:fcol + 5], in1=ps[:, 0:n * 128], op0=ALU.mult, op1=ALU.add), sig=True)
                                kb.W("dve", tq)
                                return kb.I("dve", lambda e: e.scalar_tensor_tensor(out=tmp[:, 0:n * 128], in0=bb[:, cls * 896 + off:cls * 896 + off + n * 128], scalar=T["flags"][:, fcol:fcol + 1], in1=tmp[:, 0:n * 128], op0=ALU.mult, op1=ALU.add), sig=True)
                            return kb.I("dve", lambda e: e.tensor_tensor(out=tmp[:, 0:n * 128], in0=ps[:, 0:n * 128], in1=bb[:, off:off + n * 128], op=ALU.add), sig=True)

                        lastq = attention_head(kb, T, R, NQT, slots, active, qtile, ksrc, vsrc, den_lhsT, bias_fn,
                                               lambda i: big[:, h, i * 128:(i + 1) * 128], prev_wo if h == 0 else None)
                        pe_done = ("pe", kb.cnt["pe"])
                        R["qk"].free[qi] = [pe_done]
                        R["vb"].free[vi] = [pe_done]
                        R["bb"].free[bi_] = [lastq]
                        head_toks.append(lastq)

                    def epi(bi, b, tt, ps, tm, c=c):
                        sl = slice(c * CH + tt * 512, c * CH + (tt + 1) * 512)
                        return epi_resid(kb, T, R, ps, tm, xsrc[b, :, sl], xdst[b, :, sl])
                    prev_wo = proj_fm(kb, T, R, W[li, "o"], list(range(16)), 16, big, CH, head_toks[-1], epi)
                end_phase(kb, T, [])

            phase({
                "big": ([128, 16, CH], BF16, "sb"),
                "qk0": ([128, 3 * CH + 8960], BF16, "sb"), "qk1": ([128, 3 * CH + 8960], BF16, "sb"),
                "vb0": ([128, 70, 128], BF16, "sb"), "vb1": ([128, 70, 128], BF16, "sb"),
                "bb0": ([128, 5 * 7 * 128], BF16, "sb"), "bb1": ([128, 5 * 7 * 128], BF16, "sb"),
                "pt0": ([128, 512], BF16, "sb"), "pt1": ([128, 512], BF16, "sb"), "pt2": ([128, 512], BF16, "sb"), "pt3": ([128, 512], BF16, "sb"),
                "tm0": ([128, 512], F32, "sb"), "tm1": ([128, 512], F32, "sb"), "tm2": ([128, 512], F32, "sb"), "tm3": ([128, 512], F32, "sb"),
                "vo": ([128, 8, 128], BF16, "sb"),
                "rc0": ([128, 128], F32, "sb"), "rc1": ([128, 128], F32, "sb"),
                "wr0": ([128, 2048], BF16, "sb"), "wr1": ([128, 2048], BF16, "sb"),
                "xr0": ([128, 512], F32, "sb"), "xr1": ([128, 512], F32, "sb"),
                "pm0": ([128, 512], F32, "ps"), "ps1": ([128, 512], F32, "ps"), "ps2": ([128, 512], F32, "ps"), "ps3": ([128, 512], F32, "ps"),
                "po0": ([128, 512], F32, "ps"), "po1": ([128, 512], F32, "ps"),
                "pd0": ([128, 512], F32, "ps"), "pd1": ([128, 512], F32, "ps"),
            }, att_script)
            xsrc = XS
            if DBG_STOP == (li, 2):
                break

            def cross_script(kb, T, li=li, gbase=gbase, hbase=hbase):
                R = mk_rings(kb, T, {"xs": 2, "sq": 1, "pn": 1, "rs": 2, "wr": 3, "sq2": 2, "stg": 2, "wv": 1, "ps": 3, "po": 2, "pd": 2, "pt": 3, "rc": 2, "xr": 3})
                R["pm"] = R["ps"]
                big = T["big"]
                mt = T["mt"]
                kx = T["kx"]
                vx = T["vx"]
                qx = T["qx"]
                ox = T["ox"]
                prev = None
                for c in range(NCH):
                    for e_ in ("dve", "pool", "act"):
                        kb.W(e_, prev)
                    tm_ = norm_chunk(kb, T, R, None, 0, gbase + 32, ntok=256, dst=mt, srcap=memT_in[c])

                    def epik(bi, b, tt, ps, tm):
                        raise RuntimeError
                    for b in range(4):
                        wi = R["wr"].nxt()
                        wt = R["wr"].t[wi]
                        kb.W("pool", R["wr"].free[wi])
                        tw = kb.DMA("pool", R["wr"].sem[wi], wt[:, :], W[li, "xk"][b, :, :])
                        kb.W("pe", tw, tm_)
                        pb = R["pm"].nxt()
                        ps = R["pm"].t[pb]
                        kb.W("pe", R["pm"].free[pb])
                        for kc in range(16):
                            tmm = kb.I("pe", lambda e, kc=kc: e.matmul(ps[:, 0:256], wt[:, kc * 128:(kc + 1) * 128], mt[:, kc, :], start=(kc == 0), stop=(kc == 15)), sig=(kc == 15))
                        R["wr"].free[wi] = tmm
                        qi = R["sq2"].nxt()
                        sq = R["sq2"].t[qi]
                        kb.W("act", tmm, R["sq2"].free[qi])
                        t1 = kb.I("act", lambda e: e.activation(out=sq[:, 0:256], in_=ps[:, 0:256], func=AF.Square), sig=True)
                        p2 = R["pn"].nxt()
                        ps2 = R["pn"].t[p2]
                        kb.W("pe", t1, R["pn"].free[p2])
                        t2 = kb.I("pe", lambda e: e.matmul(ps2[:, 0:256], T["ones"][:, :], sq[:, 0:256], start=True, stop=True), sig=True)
                        R["sq2"].free[qi] = t2
                        ri = R["rs"].nxt()
                        rs = R["rs"].t[ri]
                        kb.W("dve", t2, R["rs"].free[ri], tmm, prev)
                        kb.W("act", ("pe", kb.cnt["pe"]), R["rs"].free[ri])
                        t3a = kb.I("act", lambda e: e.activation(out=rs[:, 0:256], in_=ps2[:, 0:256], func=AF.Sqrt, bias=T["epsb"][:, 1:2], scale=1.0), sig=True)
                        kb.W("dve", t3a)
                        t3 = kb.I("dve", lambda e: e.reciprocal(out=rs[:, 0:256], in_=rs[:, 0:256]), sig=True)
                        kb.W("dve", t3)
                        R["pn"].free[p2] = t3
                        t4 = kb.I("dve", lambda e, b=b: e.scalar_tensor_tensor(out=kx[:, b, :], in0=ps[:, 0:256], scalar=T["hvec"][:, hbase + 7:hbase + 8], in1=rs[:, 0:256], op0=ALU.mult, op1=ALU.mult), sig=True)
                        R["rs"].free[ri] = t4
                        R["pm"].free[pb] = t4
                    tk_done = t4
                    def epiv(t16, ps, tm):
                        kb.W("act", tm)
                        return kb.I("act", lambda e: e.activation(out=vx[:, t16, :], in_=ps[:, :], func=AF.Copy), sig=True)
                    vtoks = []

                    def epiv2(t16, ps, tm):
                        t = epiv(t16, ps, tm)
                        vtoks.append(t)
                        return t
                    proj_tm(kb, T, R, W[li, "xv"], 0, mt, 2, tm_, epiv2)
                    toks = norm_chunk(kb, T, R, XS, c, gbase + 16)

                    def epiq(bi, b, tt, ps, tm):
                        qi = R["sq2"].nxt()
                        sq = R["sq2"].t[qi]
                        kb.W("act", tm, R["sq2"].free[qi])
                        t1 = kb.I("act", lambda e: e.activation(out=sq[:], in_=ps[:, :], func=AF.Square), sig=True)
                        p2 = R["pn"].nxt()
                        ps2 = R["pn"].t[p2]
                        kb.W("pe", t1, R["pn"].free[p2])
                        t2 = kb.I("pe", lambda e: e.matmul(ps2[:, :], T["ones"][:, :], sq[:], start=True, stop=True), sig=True)
                        R["sq2"].free[qi] = t2
                        ri = R["rs"].nxt()
                        rs = R["rs"].t[ri]
                        kb.W("dve", t2, R["rs"].free[ri], tm)
                        kb.W("act", ("pe", kb.cnt["pe"]), R["rs"].free[ri])
                        t3a = kb.I("act", lambda e: e.activation(out=rs[:, :], in_=ps2[:, :], func=AF.Sqrt, bias=T["epsb"][:, 1:2], scale=1.0), sig=True)
                        kb.W("dve", t3a)
                        t3 = kb.I("dve", lambda e: e.reciprocal(out=rs[:, :], in_=rs[:, :]), sig=True)
                        kb.W("dve", t3)
                        R["pn"].free[p2] = t3
                        t4 = kb.I("dve", lambda e: e.scalar_tensor_tensor(out=qx[:, b, tt * 512:(tt + 1) * 512], in0=ps[:, :], scalar=T["hvec"][:, hbase + 6:hbase + 7], in1=rs[:, :], op0=ALU.mult, op1=ALU.mult), sig=True)
                        R["rs"].free[ri] = t4
                        return t4
                    proj_fm(kb, T, R, W[li, "xq"], list(range(4)), 16, big, CH, toks, epiq)
                    qdone = ("dve", kb.cnt["dve"])
                    kb.W("pe", qdone, tk_done, vtoks)
                    lastq = None
                    for h in range(4):
                        lastq = attention_head(
                            kb, T, R, NQT, [(0, 0), (0, 1)], lambda i, g, dl: True,
                            lambda g, i, h=h: qx[:, h, i * 128:(i + 1) * 128],
                            lambda g, dl, i, h=h: kx[:, h, dl * 128:(dl + 1) * 128],
                            lambda g, dl, i, h=h: vx[:, dl, h * 128:(h + 1) * 128],
                            lambda g, dl, i: T["ones"][:, :], None,
                            lambda i, h=h: ox[:, h, i * 128:(i + 1) * 128], prev if h == 0 else None, scale=1.0, LA=2)

                    def epi(bi, b, tt, ps, tm, c=c):
                        sl = slice(c * CH + tt * 512, c * CH + (tt + 1) * 512)
                        return epi_resid(kb, T, R, ps, tm, XS[b, :, sl], XS[b, :, sl])
                    prev = proj_fm(kb, T, R, W[li, "xo"], list(range(16)), 4, ox, CH, lastq, epi)
                end_phase(kb, T, [])

            phase({
                "big": ([128, 16, CH], BF16, "sb"),
                "mt": ([128, 16, 256], BF16, "sb"), "kx": ([128, 4, 256], BF16, "sb"), "vx": ([128, 2, 512], BF16, "sb"),
                "qx": ([128, 4, CH], BF16, "sb"), "ox": ([128, 4, CH], BF16, "sb"),
                "xs0": ([128, 16, 256], F32, "sb"), "xs1": ([128, 16, 256], F32, "sb"),
                "sq0": ([128, 16, 256], BF16, "sb"),
                "rs0": ([128, 512], F32, "sb"), "rs1": ([128, 512], F32, "sb"),
                "wr0": ([128, 2048], BF16, "sb"), "wr1": ([128, 2048], BF16, "sb"), "wr2": ([128, 2048], BF16, "sb"),
                "wv0": ([128, 8192], BF16, "sb"),
                "sq20": ([128, 512], BF16, "sb"), "sq21": ([128, 512], BF16, "sb"),
                "stg0": ([128, 8], BF16, "sb"), "stg1": ([128, 8], BF16, "sb"),
                "pt0": ([128, 512], BF16, "sb"), "pt1": ([128, 512], BF16, "sb"), "pt2": ([128, 512], BF16, "sb"),
                "rc0": ([128, 128], F32, "sb"), "rc1": ([128, 128], F32, "sb"),
                "xr0": ([128, 512], F32, "sb"), "xr1": ([128, 512], F32, "sb"), "xr2": ([128, 512], F32, "sb"),
                "pn0": ([128, 512], F32, "ps"),
                "pm0": ([128, 512], F32, "ps"), "ps1": ([128, 512], F32, "ps"), "ps2": ([128, 512], F32, "ps"),
                "po0": ([128, 512], F32, "ps"), "po1": ([128, 512], F32, "ps"),
                "pd0": ([128, 512], F32, "ps"), "pd1": ([128, 512], F32, "ps"),
            }, cross_script)
            if DBG_STOP == (li, 3):
                break

            xout = yT if last_layer else XS

            def mlpup_script(kb, T, li=li, gbase=gbase):
                R = mk_rings(kb, T, {"xs": 2, "sq": 1, "pn": 1, "rs": 2, "wr": 3, "pm": 4, "rl": 2, "stg": 3})
                big = T["big"]
                prev = None
                for c in range(NCH):
                    for e_ in ("dve", "pool", "sp"):
                        kb.W(e_, prev)
                    toks = norm_chunk(kb, T, R, XS, c, gbase + 48)

                    def epi(bi, b, tt, ps, tm, c=c):
                        ri = R["rl"].nxt()
                        rl = R["rl"].t[ri]
                        kb.W("act", tm, R["rl"].free[ri])
                        t1 = kb.I("act", lambda e: e.activation(out=rl[:], in_=ps[:, :], func=AF.Relu), sig=True)
                        si = R["stg"].nxt()
                        sg = R["stg"].t[si]
                        kb.W("dve", t1, R["stg"].free[si])
                        t2 = kb.I("dve", lambda e: e.tensor_tensor(out=sg[:], in0=rl[:], in1=rl[:], op=ALU.mult), sig=True)
                        R["rl"].free[ri] = t2
                        kb.W("sp", t2)
                        R["stg"].free[si] = kb.DMA("sp", R["stg"].sem[si], AT[b, :, c * CH + tt * 512: c * CH + (tt + 1) * 512], sg[:])
                        return t1
                    proj_fm(kb, T, R, W[li, "up"], list(range(64)), 16, big, CH, toks, epi)
                    prev = ("pe", kb.cnt["pe"])
                end_phase(kb, T, [])

            phase({
                "big": ([128, 16, CH], BF16, "sb"),
                "xs0": ([128, 16, 256], F32, "sb"), "xs1": ([128, 16, 256], F32, "sb"),
                "sq0": ([128, 16, 256], BF16, "sb"),
                "rs0": ([128, 512], F32, "sb"), "rs1": ([128, 512], F32, "sb"),
                "wr0": ([128, 2048], BF16, "sb"), "wr1": ([128, 2048], BF16, "sb"), "wr2": ([128, 2048], BF16, "sb"),
                "rl0": ([128, 512], F32, "sb"), "rl1": ([128, 512], F32, "sb"),
                "stg0": ([128, 512], BF16, "sb"), "stg1": ([128, 512], BF16, "sb"), "stg2": ([128, 512], BF16, "sb"),
                "pn0": ([128, 512], F32, "ps"),
                "pm0": ([128, 512], F32, "ps"), "pm1": ([128, 512], F32, "ps"), "pm2": ([128, 512], F32, "ps"), "pm3": ([128, 512], F32, "ps"),
            }, mlpup_script)

            def mlpdn_script(kb, T, li=li, xout=xout):
                R = mk_rings(kb, T, {"wd": 2, "pm": 4, "xr": 3, "ab": 1, "wb": 2})
                ab = T["abig"]
                twb = [None] * 16
                for s8 in range(T_ // 1024):
                    kb.W("sp", ("pe", kb.cnt["pe"]))
                    tla = []
                    for q4 in range(4):
                        tla.append(kb.DMA("sp", R["ab"].sem[0], ab[:, q4 * 16:(q4 + 1) * 16, :], AT[q4 * 16:(q4 + 1) * 16, :, s8 * 1024:(s8 + 1) * 1024].rearrange("k p t -> p k t")))
                    for b in range(16):
                        wi = R["wd"].nxt()
                        wt = R["wd"].t[wi]
                        if s8 == 0:
                            kb.W("pool", R["wd"].free[wi])
                            tw = kb.DMA("pool", R["wd"].sem[wi], wt[:, :], W[li, "dn"][b, :, :])
                            kb.W("act", tw)
                            twb[b] = kb.DMA("act", R["wb"].sem[wi], WDB[b, :, :], wt[:, :])
                        else:
                            kb.W("act", R["wd"].free[wi], [(sm, kb.cnt[sm]) for sm in R["wb"].sem])
                            tw = kb.DMA("act", R["wd"].sem[wi], wt[:, :], WDB[b, :, :])
                        kb.W("pe", tw, tla)
                        for tt in range(2):
                            pb = R["pm"].nxt()
                            ps = R["pm"].t[pb]
                            kb.W("pe", R["pm"].free[pb])
                            for kc in range(64):
                                tmm = kb.I("pe", lambda e, kc=kc, tt=tt: e.matmul(ps[:, :], wt[:, kc * 128:(kc + 1) * 128], ab[:, kc, tt * 512:(tt + 1) * 512], start=(kc == 0), stop=(kc == 63)), sig=(kc == 63))
                            sl = slice(s8 * 1024 + tt * 512, s8 * 1024 + (tt + 1) * 512)
                            R["pm"].free[pb] = epi_resid(kb, T, R, ps, tmm, XS[b, :, sl], xout[b, :, sl])
                        R["wd"].free[wi] = [tmm, twb[b]] if s8 == 0 else [tmm]
                end_phase(kb, T, [])

            phase({
                "abig": ([128, 64, 1024], BF16, "sb"),
                "wd0": ([128, 8192], BF16, "sb"), "wd1": ([128, 8192], BF16, "sb"),
                "ab0": ([128, 8], BF16, "sb"), "wb0": ([128, 8], BF16, "sb"), "wb1": ([128, 8], BF16, "sb"),
                "xr0": ([128, 512], F32, "sb"), "xr1": ([128, 512], F32, "sb"), "xr2": ([128, 512], F32, "sb"),
                "pm0": ([128, 512], F32, "ps"), "pm1": ([128, 512], F32, "ps"), "pm2": ([128, 512], F32, "ps"), "pm3": ([128, 512], F32, "ps"),
            }, mlpdn_script)
    return nc


def _blk(w, KC):
    K, M = w.shape
    return np.ascontiguousarray(w.reshape(KC, 128, M // 128, 128).transpose(2, 1, 0, 3).reshape(M // 128, 128, KC * 128))


def _blkv(w):
    K, M = w.shape
    return np.ascontiguousarray(w.reshape(16, 128, M // 512, 512).transpose(2, 1, 0, 3).reshape(M // 512, 128, 16 * 512))


def _t5_bucket(rel):
    nb = 16
    max_exact = 8
    ret = np.where(rel > 0, nb, 0)
    n = np.abs(rel)
    n_f = np.maximum(n, 1).astype(np.float32)
    large = max_exact + (np.log(n_f / np.float32(max_exact)) / np.float32(math.log(1024 / max_exact)) * np.float32(nb - max_exact)).astype(np.int32)
    large = np.minimum(large, nb - 1)
    return ret + np.where(n < max_exact, n, large)


def _dil_bias(t5_table):
    out = np.full((NH, 128, 27 * 128), NEG, np.float32)
    j = np.arange(128)[:, None]
    i = np.arange(128)[None, :]
    s = 0
    for g, (win, d) in enumerate(DIL):
        for dl in range(-REACH_DIL[g], REACH_DIL[g] + 1):
            rel = 128 * dl + j - i
            ok = (rel % d == 0) & (np.abs(rel) <= 64 * d)
            bk = _t5_bucket(rel)
            vals = t5_table[bk, g, :]
            blk = np.where(ok[None], vals.transpose(2, 0, 1), np.float32(NEG))
            out[:, :, s * 128:(s + 1) * 128] = blk
            s += 1
    return out


def _na_bias(rpb):
    out = np.full((NH, 128, 5, 7, 128), NEG, np.float32)
    R = 32
    kc = np.arange(64)
    cs = np.clip(kc - 8, 0, 48)
    col_ok = (kc[None, :] >= cs[:, None]) & (kc[None, :] < cs[:, None] + 16)
    dc = np.clip(kc[None, :] - kc[:, None], -15, 15) + 15
    for cls in range(5):
        qt = {0: 8, 1: 0, 2: 1, 3: 14, 4: 15}[cls]
        for si, dl in enumerate(range(-3, 4)):
            for a in range(2):
                r = 2 * qt + a
                if cls == 0:
                    rs = r - 4
                else:
                    rs = min(max(r - 4, 0), R - 8)
                for b in range(2):
                    kr = 2 * (qt + dl) + b
                    if kr < rs or kr >= rs + 8:
                        continue
                    dr = kr - r + 7
                    vals = rpb[:, dr, :][:, dc]
                    blk = np.where(col_ok[None], vals, np.float32(NEG)).transpose(0, 2, 1)
                    out[:, b * 64:(b + 1) * 64, cls, si, a * 64:(a + 1) * 64] = blk
    return np.ascontiguousarray(out.reshape(NH, 128, 5 * 7 * 128))


def _prep_weights(inp, LAYERS):
    Wd = {}
    f = np.float32
    for li, L in enumerate(LAYERS):
        l2 = L // 2
        if L % 2 == 0:
            w = inp["w_qkv_a"][l2]
            wq, wk, wv = w[:, 0:2048], w[:, 2048:4096], w[:, 4096:6144]
            Wd["wqk%d" % li] = _blk(np.concatenate([wq, wk], 1), 16)
            Wd["wv%d" % li] = _blkv(wv)
            Wd["bias%d" % li] = _na_bias(inp["rpb_a"][l2])
            Wd["wo%d" % li] = _blk(inp["w_o_a"][l2], 16)
        else:
            w = inp["w_qkv_b"][l2].reshape(2048, 3, 3, 2048)
            qk = np.concatenate([np.concatenate([w[:, g, 0], w[:, g, 1]], 1) for g in range(3)], 1)
            Wd["wqk%d" % li] = _blk(qk, 16)
            Wd["wv%d" % li] = _blkv(np.concatenate([w[:, g, 2] for g in range(3)], 1))
            Wd["bias%d" % li] = _dil_bias(inp["t5_table"])
            Wd["wo%d" % li] = _blk(inp["w_o_b"][l2], 16)
        Wd["wxq%d" % li] = _blk(inp["w_q_x"][L], 16)
        wkv = inp["w_kv_x"][L]
        Wd["wxk%d" % li] = _blk(wkv[:, 0:512], 16)
        Wd["wxv%d" % li] = _blkv(wkv[:, 512:1024])
        Wd["wxo%d" % li] = _blk(inp["w_o_x"][L], 4)
        Wd["wup%d" % li] = _blk(inp["w_up"][L], 16)
        Wd["wdn%d" % li] = _blk(inp["w_down"][L], 64)
    return Wd


def _vec_inputs(inp, LAYERS):
    gv = np.zeros((128, 4 * len(LAYERS) * 16), np.float32)
    hv = np.zeros((128, len(LAYERS) * 16), np.float32)
    for li, L in enumerate(LAYERS):
        for j, nm in enumerate(("g_mix", "g_cross", "g_mem", "g_mlp")):
            gv[:, li * 64 + j * 16: li * 64 + (j + 1) * 16] = inp[nm][L].reshape(16, 128).T
        l2 = L // 2
        if L % 2 == 0:
            hv[:, li * 16 + 0] = inp["q_norm_a"][l2]
            hv[:, li * 16 + 1] = inp["k_norm_a"][l2]
        else:
            for g in range(3):
                hv[:, li * 16 + 2 * g] = inp["q_norm_b"][l2, g]
                hv[:, li * 16 + 2 * g + 1] = inp["k_norm_b"][l2, g]
        hv[:, li * 16 + 6] = inp["q_norm_x"][L]
        hv[:, li * 16 + 7] = inp["k_norm_x"][L]
    return gv, hv


def _flags(kind, NCH):
    fl = np.zeros((128, 32), np.float32)
    for c in range(NCH):
        if kind == "prompt":
            top = 1.0 if c == 0 else 0.0
            bot = 1.0 if c == NCH - 1 else 0.0
        else:
            top = bot = 1.0
        fl[:, c] = top
        fl[:, 4 + c] = 1.0 - top
        fl[:, 8 + c] = bot
        fl[:, 12 + c] = 1.0 - bot
        fl[:, 16 + c] = 1.0 - top
        fl[:, 20 + c] = 1.0 - bot
    return fl


_CACHE = {}


def _run(inp, NCH, LAYERS, seqs, DBG_STOP=None, ncores=8):
    key = (NCH, tuple(LAYERS), DBG_STOP)
    if key not in _CACHE:
        _CACHE[key] = build(NCH, LAYERS, DBG_STOP)
    nc = _CACHE[key]
    Wd = _prep_weights(inp, LAYERS)
    gv, hv = _vec_inputs(inp, LAYERS)
    in_maps = []
    for (kind, xT, memT) in seqs:
        m = dict(Wd)
        m["xT"] = np.ascontiguousarray(xT.reshape(16, 128, NCH * CH))
        m["memT"] = np.ascontiguousarray(memT.reshape(NCH, 16, 128, 256))
        m["gvec"] = gv
        m["hvec"] = hv
        m["flags"] = _flags(kind, NCH)
        m["onesm"] = np.ones((128, 128), np.float32)
        in_maps.append(m)
    while len(in_maps) < ncores:
        in_maps.append(in_maps[-1])
    res = run_bass_kernel_spmd(nc, in_maps, core_ids=list(range(ncores)))
    return [r["yT"].reshape(2048, NCH * CH) for r in res.results]


def kernel(**inp):
    inp = {k: np.asarray(v) for k, v in inp.items()}
    LAYERS = [0, 1, 2, 3]
    NCH = 4
    xp = inp["x_prompt"][0]
    xs = inp["x_sample"].reshape(4 * 2048, 2048)
    memp = np.broadcast_to(inp["mem_prompt"][0].T[None], (4, 2048, 256))
    mems = inp["mem_sample"].transpose(0, 2, 1)
    seqs = [("prompt", np.ascontiguousarray(xp.T), np.ascontiguousarray(memp)),
            ("sample", np.ascontiguousarray(xs.T), np.ascontiguousarray(mems))]
    outs = _run(inp, NCH, LAYERS, seqs, ncores=2)
    y_prompt = np.ascontiguousarray(outs[0].T).reshape(1, 8192, 2048).astype(np.float32)
    y_sample = np.ascontiguousarray(outs[1].T).reshape(4, 2048, 2048).astype(np.float32)
    return (y_prompt, y_sample)
```

```python
import math
import contextlib
import numpy as np
import concourse.bass as bass
import concourse.mybir as mybir
from concourse.bass_utils import run_bass_kernel_spmd

F32 = mybir.dt.float32
BF16 = mybir.dt.bfloat16
AF = mybir.ActivationFunctionType
ALU = mybir.AluOpType

D = 2048
DH = 128
NH = 16
CH = 2048
NQT = CH // 128
PAD = 1024
EPS = 1e-6
SCALE = 1.0 / math.sqrt(DH)
NEG = -30000.0
DIL = ((128, 1), (512, 4), (2048, 16))
REACH_DIL = (1, 2, 8)
REACH_NA = 3
ENGS = ("sp", "act", "pool", "pe", "dve")


class KB:
    def __init__(self, nc, sems):
        self.nc = nc
        self.sems = sems
        self.cnt = {k: 0 for k in sems}
        self.cur = None
        self.E = None
        self.waited = {}
        self.free_sems = [k for k in sems if k.startswith("r")]
        self.ring_sem = {}

    def I(self, eng, fn, sig=False):
        tok = None
        if sig:
            self.cnt[eng] += 1
            tok = (eng, self.cnt[eng])
        if self.cur == eng:
            ins = fn(self.E)
            if sig:
                ins.then_inc(self.sems[eng], 1)
        return tok

    def DMA(self, q, sem, out, in_):
        self.cnt[sem] += 16
        tok = (sem, self.cnt[sem])
        if self.cur == q:
            self.E.dma_start(out=out, in_=in_).then_inc(self.sems[sem], 16)
        return tok

    def W(self, eng, *toks):
        if self.cur != eng:
            return
        for t in toks:
            if t is None:
                continue
            if isinstance(t, list):
                self.W(eng, *t)
                continue
            s, v = t
            if self.waited.get((eng, s), 0) < v:
                self.E.wait_ge(self.sems[s], v)
                self.waited[(eng, s)] = v

    def state(self):
        return (dict(self.cnt),)

    def restore(self, st):
        self.cnt = dict(st[0])


class Ring:
    def __init__(self, kb, name, tiles, semnames):
        self.kb = kb
        self.t = tiles
        self.n = len(tiles)
        self.sem = semnames
        self.reset()

    def reset(self):
        self.use = 0
        self.free = [None] * self.n

    def nxt(self):
        i = self.use % self.n
        self.use += 1
        return i


def run_phase(nc, kb, tensors, nsem_rings, script):
    with contextlib.ExitStack() as st:
        T = {}
        for name, (shape, dt, kind) in tensors.items():
            if kind == "sb":
                T[name] = st.enter_context(nc.sbuf_tensor(name + "_%d" % kb.phase_id, shape, dt))
            else:
                T[name] = st.enter_context(nc.psum_tensor(name + "_%d" % kb.phase_id, shape, dt))
        block = st.enter_context(nc.Block())
        s0 = kb.state()

        def mk(engname):
            def f(e):
                kb.restore(s0)
                kb.cur = engname
                kb.E = e
                script(kb, T)
                kb.cur = None
            return f

        block.sync(mk("sp"))
        block.scalar(mk("act"))
        block.gpsimd(mk("pool"))
        block.tensor(mk("pe"))
        block.vector(mk("dve"))
    kb.phase_id += 1


def end_phase(kb, T, extra):
    toks = list(extra)
    toks.append(kb.I("act", lambda e: e.activation(out=T["scr"][:, 0:1], in_=T["scr"][:, 1:2], func=AF.Copy), sig=True))
    toks.append(kb.I("dve", lambda e: e.tensor_copy(out=T["scr"][:, 2:3], in_=T["scr"][:, 3:4]), sig=True))
    toks.append(kb.I("pool", lambda e: e.tensor_copy(out=T["scr"][:, 4:5], in_=T["scr"][:, 5:6]), sig=True))
    for s in kb.cnt:
        if s.startswith("r") and kb.cnt[s] > 0:
            toks.append((s, kb.cnt[s]))
    kb.W("pe", *toks)
    toks.append(kb.I("pe", lambda e: e.matmul(T["pm0"][:, 0:8], T["ones"][:, 0:128], T["ones"][:, 0:8], start=True, stop=True), sig=True))
    for s in kb.cnt:
        if s.startswith("r") and kb.cnt[s] > 0:
            toks.append((s, kb.cnt[s]))
    for e in ENGS:
        kb.W(e, *toks)


def build(NCH, LAYERS, DBG_STOP=None):
    T_ = NCH * CH
    TP = T_ + 2 * PAD
    nc = bass.Bass("TRN2", target_bir_lowering=False)

    def din(name, shape, dt=F32):
        return nc.dram_tensor(name, list(shape), dt, kind="ExternalInput")

    xT_in = din("xT", [16, 128, T_])
    memT_in = din("memT", [NCH, 16, 128, 256])
    gvec_in = din("gvec", [128, 4 * len(LAYERS) * 16])
    hvec_in = din("hvec", [128, len(LAYERS) * 16])
    flags_in = din("flags", [128, 32])
    ones_in = din("onesm", [128, 128])
    W = {}
    for li, L in enumerate(LAYERS):
        if L % 2 == 0:
            W[li, "qk"] = din("wqk%d" % li, [32, 128, 16 * 128])
            W[li, "v"] = din("wv%d" % li, [4, 128, 16 * 512])
            W[li, "bias"] = din("bias%d" % li, [NH, 128, 5 * 7 * 128])
        else:
            W[li, "qk"] = din("wqk%d" % li, [96, 128, 16 * 128])
            W[li, "v"] = din("wv%d" % li, [12, 128, 16 * 512])
            W[li, "bias"] = din("bias%d" % li, [NH, 128, 27 * 128])
        W[li, "o"] = din("wo%d" % li, [16, 128, 16 * 128])
        W[li, "xq"] = din("wxq%d" % li, [4, 128, 16 * 128])
        W[li, "xk"] = din("wxk%d" % li, [4, 128, 16 * 128])
        W[li, "xv"] = din("wxv%d" % li, [1, 128, 16 * 512])
        W[li, "xo"] = din("wxo%d" % li, [16, 128, 4 * 128])
        W[li, "up"] = din("wup%d" % li, [64, 128, 16 * 128])
        W[li, "dn"] = din("wdn%d" % li, [16, 128, 64 * 128])
    yT = nc.dram_tensor("yT", [16, 128, T_], F32, kind="ExternalOutput")
    XS = yT if DBG_STOP is not None else nc.dram_tensor("xscr", [16, 128, T_], F32)
    QT = [nc.dram_tensor("qt%d" % g, [NH, 128, T_], BF16) for g in range(3)]
    KT = [nc.dram_tensor("kt%d" % g, [NH, 128, TP], BF16) for g in range(3)]
    VV = [nc.dram_tensor("vv%d" % g, [NH, TP, 128], BF16) for g in range(3)]
    AT = nc.dram_tensor("at", [64, 128, T_], BF16)
    WDB = nc.dram_tensor("wdb", [16, 128, 8192], BF16)

    semnames = list(ENGS) + ["r%d" % i for i in range(64)]
    with contextlib.ExitStack() as st:
        sems = {k: st.enter_context(nc.semaphore("s_" + k)) for k in semnames}
        kb = KB(nc, sems)
        kb.phase_id = 0
        rsem = iter(["r%d" % i for i in range(64)])
        RS = {}

        def ringsem(name, n):
            if name not in RS:
                RS[name] = []
            while len(RS[name]) < n:
                RS[name].append(next(rsem))
            return RS[name]

        ones = st.enter_context(nc.sbuf_tensor("ones", [128, 128], BF16))
        gvec = st.enter_context(nc.sbuf_tensor("gvec_sb", [128, 4 * len(LAYERS) * 16], F32))
        hvec = st.enter_context(nc.sbuf_tensor("hvec_sb", [128, len(LAYERS) * 16], F32))
        flags = st.enter_context(nc.sbuf_tensor("flags_sb", [128, 32], F32))
        scr = st.enter_context(nc.sbuf_tensor("scr", [128, 8], F32))
        zer = st.enter_context(nc.sbuf_tensor("zer", [128, 1024], BF16))
        epsb = st.enter_context(nc.sbuf_tensor("epsb", [128, 2], F32))
        COMMON = {"epsb": epsb, "ones": ones, "gvec": gvec, "hvec": hvec, "flags": flags, "scr": scr, "zer": zer}

        def phase(tensors, script):
            tensors = dict(tensors)
            if "pm0" not in tensors:
                tensors["pm0"] = ([128, 512], F32, "ps")

            def s2(kb, T):
                T = dict(T)
                T.update(COMMON)
                if "ps0" not in T and "ps1" in T:
                    T["ps0"] = T["pm0"]
                script(kb, T)
            run_phase(nc, kb, tensors, 0, s2)

        def init_script(kb, T):
            s = ringsem("init", 1)[0]
            t = []
            t.append(kb.DMA("pool", s, T["ones"][:], ones_in[:, :]))
            t.append(kb.DMA("sp", s, T["gvec"][:], gvec_in[:, :]))
            t.append(kb.DMA("sp", s, T["hvec"][:], hvec_in[:, :]))
            t.append(kb.DMA("sp", s, T["flags"][:], flags_in[:, :]))
            kb.W("dve", t[1], t[2])
            kb.I("dve", lambda e: e.tensor_scalar(out=T["gvec"][:], in0=T["gvec"][:], scalar1=float(math.sqrt(D)), scalar2=None, op0=ALU.mult), sig=True)
            hk = T["hvec"][:].rearrange("p (a two) -> p a two", two=2)[:, :, 1]
            kb.I("dve", lambda e: e.tensor_scalar(out=hk, in0=hk, scalar1=float(math.sqrt(DH)), scalar2=None, op0=ALU.mult), sig=True)
            kb.W("pe", t[0])
            t1 = kb.I("dve", lambda e: e.memset(T["zer"][:], 0.0), sig=True)
            kb.I("dve", lambda e: e.memset(T["epsb"][:, 0:1], float(D * EPS)), sig=True)
            kb.I("dve", lambda e: e.memset(T["epsb"][:, 1:2], float(DH * EPS)), sig=True)
            t2 = kb.I("dve", lambda e: e.memset(T["scr"][:], 0.0), sig=True)
            kb.W("sp", t1)
            for g in range(3):
                for h in range(NH):
                    kb.DMA("sp", s, KT[g][h, :, 0:PAD], T["zer"][:, 0:PAD])
                    kb.DMA("sp", s, KT[g][h, :, PAD + T_:TP], T["zer"][:, 0:PAD])
                    kb.DMA("sp", s, VV[g][h, 0:PAD, :].rearrange("(a p) d -> p a d", p=128), T["zer"][:, 0:PAD].rearrange("p (a d) -> p a d", d=128))
                    kb.DMA("sp", s, VV[g][h, PAD + T_:TP, :].rearrange("(a p) d -> p a d", p=128), T["zer"][:, 0:PAD].rearrange("p (a d) -> p a d", d=128))
            end_phase(kb, T, [t2])

        phase({}, init_script)

        class Ctx:
            pass

        def mk_rings(kb, T, spec):
            R = {}
            for name, n in spec.items():
                R[name] = Ring(kb, name, [T["%s%d" % (name, i)] for i in range(n)], ringsem(name, n))
            return R

        def norm_chunk(kb, T, R, src, c, gcol, ntok=CH, tok0=None, dst=None, srcap=None):
            dst = T["big"] if dst is None else dst
            done = []
            nsub = ntok // 256
            for j in range(nsub):
                i = R["xs"].nxt()
                xs = R["xs"].t[i]
                kb.W("sp", R["xs"].free[i])
                if srcap is None:
                    a = src[:, :, c * CH + j * 256: c * CH + (j + 1) * 256].rearrange("k p t -> p k t")
                else:
                    a = srcap[:, :, j * 256:(j + 1) * 256].rearrange("k p t -> p k t")
                tl = kb.DMA("sp", R["xs"].sem[i], xs[:], a)
                kb.W("act", tl)
                qi = R["sq"].nxt()
                sq = R["sq"].t[qi]
                kb.W("act", R["sq"].free[qi])
                t1 = kb.I("act", lambda e: e.activation(out=sq[:], in_=xs[:], func=AF.Square), sig=True)
                pb = R["pn"].nxt()
                ps = R["pn"].t[pb]
                kb.W("pe", t1, R["pn"].free[pb])
                for kc in range(16):
                    t2 = kb.I("pe", lambda e, kc=kc: e.matmul(ps[:, 0:256], T["ones"][:, :], sq[:, kc, :], start=(kc == 0), stop=(kc == 15)), sig=(kc == 15))
                R["sq"].free[qi] = t2
                ri = R["rs"].nxt()
                rs = R["rs"].t[ri]
                kb.W("dve", t2, R["rs"].free[ri])
                kb.W("act", ("pe", kb.cnt["pe"]), R["rs"].free[ri])
                t3a = kb.I("act", lambda e: e.activation(out=rs[:, 0:256], in_=ps[:, 0:256], func=AF.Sqrt, bias=T["epsb"][:, 0:1], scale=1.0), sig=True)
                kb.W("dve", t3a)
                t3 = kb.I("dve", lambda e: e.reciprocal(out=rs[:, 0:256], in_=rs[:, 0:256]), sig=True)
                kb.W("dve", t3)
                R["pn"].free[pb] = t3
                kb.W("dve", t3, tl)
                last = []
                for kc in range(16):
                    eng = "dve"
                    tk = kb.I(eng, lambda e, kc=kc: e.scalar_tensor_tensor(
                        out=dst[:, kc, j * 256:(j + 1) * 256], in0=xs[:, kc, :], scalar=T["gvec"][:, gcol + kc:gcol + kc + 1],
                        in1=rs[:, 0:256], op0=ALU.mult, op1=ALU.mult), sig=(kc == 15))
                    if tk is not None:
                        last.append(tk)
                R["xs"].free[i] = last
                R["rs"].free[ri] = last
                done += last
            return done

        def proj_fm(kb, T, R, wd, blocks, KC, src, ntok, src_toks, epi, wview=None, prefetch=None):
            ntt = ntok // 512
            pend = None
            tiles = [(bi, b, tt) for bi, b in enumerate(blocks) for tt in range(ntt)]
            pre = {}
            if prefetch is not None:
                pre[0] = prefetch(*tiles[0])
            for bi, b in enumerate(blocks):
                wi = R["wr"].nxt()
                wt = R["wr"].t[wi]
                kb.W("pool", R["wr"].free[wi])
                tw = kb.DMA("pool", R["wr"].sem[wi], wt[:, 0:KC * 128], wd[b, :, :])
                kb.W("pe", tw, src_toks)
                for tt in range(ntt):
                    pb = R["pm"].nxt()
                    ps = R["pm"].t[pb]
                    kb.W("pe", R["pm"].free[pb])
                    for kc in range(KC):
                        tm = kb.I("pe", lambda e, kc=kc, tt=tt: e.matmul(ps[:, :], wt[:, kc * 128:(kc + 1) * 128], src[:, kc, tt * 512:(tt + 1) * 512], start=(kc == 0), stop=(kc == KC - 1)), sig=(kc == KC - 1))
                    if prefetch is not None:
                        idx = bi * ntt + tt
                        if idx + 1 < len(tiles):
                            pre[idx + 1] = prefetch(*tiles[idx + 1])
                        r = epi(bi, b, tt, ps, tm, pre.pop(idx))
                    else:
                        r = epi(bi, b, tt, ps, tm)
                    if pend is not None:
                        R["pm"].free[pend[0]] = pend[1]()
                        pend = None
                    if callable(r):
                        pend = (pb, r)
                    else:
                        R["pm"].free[pb] = r
                R["wr"].free[wi] = tm
            if pend is not None:
                R["pm"].free[pend[0]] = pend[1]()
            return tm

        def epi_qknorm(kb, T, R, ps, tm, gain_ap, dst_ap):
            qi = R["sq2"].nxt()
            sq = R["sq2"].t[qi]
            kb.W("act", tm, R["sq2"].free[qi])
            t1 = kb.I("act", lambda e: e.activation(out=sq[:], in_=ps[:, :], func=AF.Square), sig=True)

            def cont():
                pb = R["pn"].nxt()
                ps2 = R["pn"].t[pb]
                kb.W("pe", t1, R["pn"].free[pb])
                t2 = kb.I("pe", lambda e: e.matmul(ps2[:, :], T["ones"][:, :], sq[:], start=True, stop=True), sig=True)
                R["sq2"].free[qi] = t2
                ri = R["rs"].nxt()
                rs = R["rs"].t[ri]
                kb.W("act", t2, R["rs"].free[ri])
                t3a = kb.I("act", lambda e: e.activation(out=rs[:, :], in_=ps2[:, :], func=AF.Sqrt, bias=T["epsb"][:, 1:2], scale=1.0), sig=True)
                kb.W("dve", t3a, tm)
                t3 = kb.I("dve", lambda e: e.reciprocal(out=rs[:, :], in_=rs[:, :]), sig=True)
                kb.W("dve", t3)
                R["pn"].free[pb] = t3
                si = R["stg"].nxt()
                sg = R["stg"].t[si]
                kb.W("dve", R["stg"].free[si])
                t4 = kb.I("dve", lambda e: e.scalar_tensor_tensor(out=sg[:], in0=ps[:, :], scalar=gain_ap, in1=rs[:, :], op0=ALU.mult, op1=ALU.mult), sig=True)
                R["rs"].free[ri] = t4
                if dst_ap is not None:
                    kb.W("sp", t4)
                    R["stg"].free[si] = kb.DMA("sp", R["stg"].sem[si], dst_ap, sg[:])
                return t4
            return cont

        def resid_load(kb, T, R, src_ap):
            xi = R["xr"].nxt()
            xr = R["xr"].t[xi]
            kb.W("sp", R["xr"].free[xi])
            tl = kb.DMA("sp", R["xr"].sem[xi], xr[:], src_ap)
            return (xi, xr, tl)

        def resid_fin(kb, T, R, ps, tm, pre, dst_ap):
            xi, xr, tl = pre
            kb.W("dve", tl, tm)
            t1 = kb.I("dve", lambda e: e.tensor_tensor(out=xr[:], in0=ps[:, :], in1=xr[:], op=ALU.add), sig=True)
            kb.W("sp", t1)
            R["xr"].free[xi] = kb.DMA("sp", R["xr"].sem[xi], dst_ap, xr[:])
            return t1

        def epi_resid(kb, T, R, ps, tm, src_ap, dst_ap):
            xi = R["xr"].nxt()
            xr = R["xr"].t[xi]
            kb.W("sp", R["xr"].free[xi])
            tl = kb.DMA("sp", R["xr"].sem[xi], xr[:], src_ap)
            kb.W("dve", tl, tm)
            t1 = kb.I("dve", lambda e: e.tensor_tensor(out=xr[:], in0=ps[:, :], in1=xr[:], op=ALU.add), sig=True)
            kb.W("sp", t1)
            R["xr"].free[xi] = kb.DMA("sp", R["xr"].sem[xi], dst_ap, xr[:])
            return t1

        def proj_tm(kb, T, R, wd, g, src, ntiles, src_toks, epi):
            wi = R["wv"].nxt()
            wt = R["wv"].t[wi]
            kb.W("pool", R["wv"].free[wi])
            tw = kb.DMA("pool", R["wv"].sem[wi], wt[:], wd[g, :, :])
            kb.W("pe", tw, src_toks)
            for t16 in range(ntiles):
                pb = R["pm"].nxt()
                ps = R["pm"].t[pb]
                kb.W("pe", R["pm"].free[pb])
                for kc in range(16):
                    tm = kb.I("pe", lambda e, kc=kc, t16=t16: e.matmul(ps[:, :], src[:, kc, t16 * 128:(t16 + 1) * 128], wt[:, kc * 512:(kc + 1) * 512], start=(kc == 0), stop=(kc == 15)), sig=(kc == 15))
                R["pm"].free[pb] = epi(t16, ps, tm)
            R["wv"].free[wi] = tm

        def attention_head(kb, T, R, nqt, slots, active, qtile, ksrc, vsrc, den_lhsT, bias_fn, out_ap, first_dep, scale=None, LA=3):
            units = []
            nb = (len(slots) + 3) // 4
            for i in range(nqt):
                ulist = []
                for bi in range(nb):
                    bs = slots[bi * 4:(bi + 1) * 4]
                    act = [active(i, g, dl) for (g, dl) in bs]
                    if any(act):
                        ulist.append([i, bi, bs, act, False, False])
                ulist[0][4] = True
                ulist[-1][5] = True
                units += ulist
            state = {}
            cur = {}
            res = {"last": None}

            def emit_S(u):
                i, bi, bs, act, fb, lb = units[u]
                n = len(bs)
                si = R["ps"].nxt()
                ps = R["ps"].t[si]
                kb.W("pe", R["ps"].free[si])
                lastj = max(j for j in range(n) if act[j])
                for j, (g, dl) in enumerate(bs):
                    if not act[j]:
                        continue
                    tS = kb.I("pe", lambda e, j=j, g=g, dl=dl: e.matmul(ps[:, j * 128:(j + 1) * 128], ksrc(g, dl, i), qtile(g, i), start=True, stop=True), sig=(j == lastj))
                pi = R["pt"].nxt()
                pt = R["pt"].t[pi]
                if bias_fn is not None:
                    ti = R["tm"].nxt()
                    tmp = R["tm"].t[ti]
                    kb.W("dve", tS, R["tm"].free[ti])
                    tB = bias_fn(i, bi, bs, tmp, ps)
                    R["ps"].free[si] = tB
                    kb.W("act", tB, R["pt"].free[pi])
                    tE = kb.I("act", lambda e: e.activation(out=pt[:, 0:n * 128], in_=tmp[:, 0:n * 128], func=AF.Exp), sig=True)
                    R["tm"].free[ti] = tE
                else:
                    kb.W("act", tS, R["pt"].free[pi])
                    tE = kb.I("act", lambda e: e.activation(out=pt[:, 0:n * 128], in_=ps[:, 0:n * 128], func=AF.Exp, scale=float(scale)), sig=True)
                    R["ps"].free[si] = tE
                state[u] = (pi, pt, tE)

            def emit_PV(u):
                i, bi, bs, act, fb, lb = units[u]
                n = len(bs)
                pi, pt, tE = state.pop(u)
                if fb:
                    oi = R["po"].nxt()
                    cur["oi"] = oi
                    kb.W("pe", R["po"].free[oi])
                oi = cur["oi"]
                po = R["po"].t[oi]
                pd = R["pd"].t[oi]
                kb.W("pe", tE)
                js = [j for j in range(n) if act[j]]
                for j in js:
                    g, dl = bs[j]
                    st_ = (fb and j == js[0])
                    sp_ = (lb and j == js[-1])
                    kb.I("pe", lambda e, j=j, g=g, dl=dl, st_=st_, sp_=sp_: e.matmul(po[:, 0:128], vsrc(g, dl, i), pt[:, j * 128:(j + 1) * 128], start=st_, stop=sp_))
                    tP = kb.I("pe", lambda e, j=j, g=g, dl=dl, st_=st_, sp_=sp_: e.matmul(pd[:, 0:128], den_lhsT(g, dl, i), pt[:, j * 128:(j + 1) * 128], start=st_, stop=sp_), sig=(j == js[-1]))
                R["pt"].free[pi] = tP
                if lb:
                    ri = R["rc"].nxt()
                    rc = R["rc"].t[ri]
                    kb.W("dve", tP, R["rc"].free[ri], first_dep if i == 0 else None)
                    t1 = kb.I("dve", lambda e: e.reciprocal(out=rc[:, :], in_=pd[:, 0:128]), sig=True)
                    kb.W("dve", t1)
                    t2 = kb.I("dve", lambda e: e.tensor_tensor(out=out_ap(i), in0=po[:, 0:128], in1=rc[:, :], op=ALU.mult), sig=True)
                    R["rc"].free[ri] = t2
                    R["po"].free[oi] = t2
                    res["last"] = t2

            nu = len(units)
            for u in range(min(LA, nu)):
                emit_S(u)
            for u in range(nu):
                if u + LA < nu:
                    emit_S(u + LA)
                emit_PV(u)
            return res["last"]

        xsrc = xT_in
        for li, L in enumerate(LAYERS):
            is_na = (L % 2 == 0)
            ng = 1 if is_na else 3
            gbase = li * 64
            hbase = li * 16
            last_layer = (li == len(LAYERS) - 1)

            def qkv_script(kb, T, li=li, is_na=is_na, ng=ng, gbase=gbase, hbase=hbase, xsrc=xsrc):
                R = mk_rings(kb, T, {"xs": 2, "sq": 1, "pn": 2, "rs": 2, "wr": 3, "pm": 4, "sq2": 2, "stg": 3, "wv": 2, "vst": 2})
                big = T["big"]
                prev = None
                for c in range(NCH):
                    for e_ in ("dve", "pool"):
                        kb.W(e_, prev)
                    toks = norm_chunk(kb, T, R, xsrc, c, gbase + 0)

                    def epi(bi, b, tt, ps, tm, c=c):
                        g = b // 32
                        isk = (b % 32) >= 16
                        h = b % 16
                        gain = T["hvec"][:, hbase + g * 2 + (1 if isk else 0): hbase + g * 2 + (1 if isk else 0) + 1]
                        if isk:
                            dst = KT[g][h, :, PAD + c * CH + tt * 512: PAD + c * CH + (tt + 1) * 512]
                        else:
                            dst = QT[g][h, :, c * CH + tt * 512: c * CH + (tt + 1) * 512]
                        return epi_qknorm(kb, T, R, ps, tm, gain, dst)
                    tl1 = proj_fm(kb, T, R, W[li, "qk"], list(range(32 * ng)), 16, big, CH, toks, epi)

                    for vg in range(4 * ng):
                        def epiv(t16, ps, tm, vg=vg, c=c):
                            g = vg // 4
                            vi = R["vst"].nxt()
                            vs = R["vst"].t[vi]
                            kb.W("act", tm, R["vst"].free[vi])
                            t1 = kb.I("act", lambda e: e.activation(out=vs[:], in_=ps[:, :], func=AF.Copy), sig=True)
                            kb.W("sp", t1)
                            r0 = PAD + c * CH + t16 * 128
                            dst = VV[g][(vg % 4) * 4:(vg % 4) * 4 + 4, r0:r0 + 128, :].rearrange("h p d -> p h d")
                            R["vst"].free[vi] = kb.DMA("sp", R["vst"].sem[vi], dst, vs[:].rearrange("p (h d) -> p h d", d=128))
                            return t1
                        proj_tm(kb, T, R, W[li, "v"], vg, big, 16, toks, epiv)
                    prev = ("pe", kb.cnt["pe"])
                end_phase(kb, T, [])

            phase({
                "big": ([128, 16, CH], BF16, "sb"),
                "xs0": ([128, 16, 256], F32, "sb"), "xs1": ([128, 16, 256], F32, "sb"),
                "sq0": ([128, 16, 256], BF16, "sb"),
                "rs0": ([128, 512], F32, "sb"), "rs1": ([128, 512], F32, "sb"),
                "wr0": ([128, 2048], BF16, "sb"), "wr1": ([128, 2048], BF16, "sb"), "wr2": ([128, 2048], BF16, "sb"),
                "wv0": ([128, 8192], BF16, "sb"), "wv1": ([128, 8192], BF16, "sb"),
                "sq20": ([128, 512], BF16, "sb"), "sq21": ([128, 512], BF16, "sb"),
                "stg0": ([128, 512], BF16, "sb"), "stg1": ([128, 512], BF16, "sb"), "stg2": ([128, 512], BF16, "sb"),
                "vst0": ([128, 512], BF16, "sb"), "vst1": ([128, 512], BF16, "sb"),
                "pn0": ([128, 512], F32, "ps"), "pn1": ([128, 512], F32, "ps"),
                "pm0": ([128, 512], F32, "ps"), "pm1": ([128, 512], F32, "ps"), "pm2": ([128, 512], F32, "ps"), "pm3": ([128, 512], F32, "ps"),
            }, qkv_script)
            if DBG_STOP == (li, 1):
                break

            if is_na:
                slots = [(0, d) for d in range(-REACH_NA, REACH_NA + 1)]
                nbias = 5 * 7 * 128
            else:
                slots = [(g, d) for g in range(3) for d in range(-REACH_DIL[g], REACH_DIL[g] + 1)]
                nbias = 27 * 128
            reach = [REACH_NA] if is_na else list(REACH_DIL)
            kw = [CH + 2 * r * 128 for r in reach]
            koff = [sum(kw[:g]) for g in range(ng)]
            vt = [NQT + 2 * r for r in reach]
            voff = [sum(vt[:g]) for g in range(ng)]
            xdst = XS

            def att_script(kb, T, li=li, is_na=is_na, ng=ng, slots=slots, reach=reach, kw=kw, koff=koff, vt=vt, voff=voff, xsrc=xsrc, xdst=xdst, nbias=nbias):
                R = mk_rings(kb, T, {"qk": 2, "vb": 2, "bb": 2, "ps": 4, "po": 2, "pd": 2, "pt": 4, "tm": 4, "rc": 2, "wr": 2, "xr": 2})
                R["pm"] = R["ps"]
                if not is_na:
                    tvo = None
                    for idx in range(8):
                        tvo = kb.I("dve", lambda e, idx=idx: e.tensor_scalar(out=T["vo"][:, idx, :], in0=T["ones"][:, :], scalar1=T["flags"][:, 16 + idx:17 + idx], scalar2=None, op0=ALU.mult), sig=True)
                    kb.W("pe", tvo)
                big = T["big"]
                prev_wo = None
                for c in range(NCH):
                    head_toks = []
                    for h in range(NH):
                        qi = R["qk"].nxt()
                        qk = R["qk"].t[qi]
                        kb.W("sp", R["qk"].free[qi])
                        tl = []
                        for g in range(ng):
                            tl.append(kb.DMA("sp", R["qk"].sem[qi], qk[:, g * CH:(g + 1) * CH], QT[g][h, :, c * CH:(c + 1) * CH]))
                            k0 = PAD + c * CH - reach[g] * 128
                            tl.append(kb.DMA("sp", R["qk"].sem[qi], qk[:, 3 * CH + koff[g]: 3 * CH + koff[g] + kw[g]], KT[g][h, :, k0:k0 + kw[g]]))
                        vi = R["vb"].nxt()
                        vb = R["vb"].t[vi]
                        bi_ = R["bb"].nxt()
                        bb = R["bb"].t[bi_]
                        kb.W("sp", R["vb"].free[vi])
                        kb.W("pool", R["bb"].free[bi_])
                        for g in range(ng):
                            r0 = PAD + c * CH - reach[g] * 128
                            tl.append(kb.DMA("sp", R["vb"].sem[vi], vb[:, voff[g]:voff[g] + vt[g], :], VV[g][h, r0:r0 + vt[g] * 128, :].rearrange("(a p) d -> p a d", p=128)))
                        tl.append(kb.DMA("pool", R["bb"].sem[bi_], bb[:, 0:nbias], W[li, "bias"][h, :, :]))
                        kb.W("pe", tl)
                        kb.W("dve", tl)
                        kb.W("pool", tl)
                        tmask = []
                        if not is_na:
                            kb.W("pool", tl)
                            for g in range(ng):
                                if c > 0:
                                    tmask.append(kb.I("pool", lambda e, g=g: e.tensor_scalar(out=vb[:, voff[g]:voff[g] + reach[g], :], in0=vb[:, voff[g]:voff[g] + reach[g], :], scalar1=T["flags"][:, 16 + c:17 + c], scalar2=None, op0=ALU.mult), sig=True))
                                if c < NCH - 1:
                                    o_ = voff[g] + reach[g] + NQT
                                    tmask.append(kb.I("pool", lambda e, g=g, o_=o_: e.tensor_scalar(out=vb[:, o_:o_ + reach[g], :], in0=vb[:, o_:o_ + reach[g], :], scalar1=T["flags"][:, 20 + c:21 + c], scalar2=None, op0=ALU.mult), sig=True))
                            kb.W("pe", tmask)

                        def active(i, g, dl):
                            if is_na and abs(dl) == 3 and 2 <= i <= NQT - 3:
                                return False
                            if c == 0 and i + dl < 0:
                                return False
                            if c == NCH - 1 and i + dl >= NQT:
                                return False
                            return True

                        def qtile(g, i):
                            return qk[:, g * CH + i * 128: g * CH + (i + 1) * 128]

                        def ksrc(g, dl, i):
                            o = 3 * CH + koff[g] + (i + dl + reach[g]) * 128
                            return qk[:, o:o + 128]

                        def vsrc(g, dl, i):
                            return vb[:, voff[g] + i + dl + reach[g], :]

                        def den_lhsT(g, dl, i):
                            if is_na or (0 <= i + dl < NQT):
                                return T["ones"][:, :]
                            return T["vo"][:, (c if i + dl < 0 else 4 + c), :]

                        def bias_fn(i, bi, bs, tmp, ps):
                            n = len(bs)
                            off = bi * 512
                            if is_na:
                                cls = {0: 1, 1: 2, NQT - 2: 3, NQT - 1: 4}.get(i, 0)
                                if cls == 0:
                                    return kb.I("dve", lambda e: e.tensor_tensor(out=tmp[:, 0:n * 128], in0=ps[:, 0:n * 128], in1=bb[:, off:off + n * 128], op=ALU.add), sig=True)
                                top = cls in (1, 2)
                                fcol = (0 if top else 8) + c
                                tq = kb.I("dve", lambda e: e.scalar_tensor_tensor(out=tmp[:, 0:n * 128], in0=bb[:, off:off + n * 128], scalar=T["flags"][:, fcol + 4:fcol + 5], in1=ps[:, 0:n * 128], op0=ALU.mult, op1=ALU.add), sig=True)
                                kb.W("dve", tq)
                                return kb.I("dve", lambda e: e.scalar_tensor_tensor(out=tmp[:, 0:n * 128], in0=bb[:, cls * 896 + off:cls * 896 + off + n * 128], scalar=T["flags"][:, fcol:fcol + 1], in1=tmp[:, 0:n * 128], op0=ALU.mult, op1=ALU.add), sig=True)
                            return kb.I("dve", lambda e: e.tensor_tensor(out=tmp[:, 0:n * 128], in0=ps[:, 0:n * 128], in1=bb[:, off:off + n * 128], op=ALU.add), sig=True)

                        lastq = attention_head(kb, T, R, NQT, slots, active, qtile, ksrc, vsrc, den_lhsT, bias_fn,
                                               lambda i: big[:, h, i * 128:(i + 1) * 128], prev_wo if h == 0 else None)
                        pe_done = ("pe", kb.cnt["pe"])
                        R["qk"].free[qi] = [pe_done]
                        R["vb"].free[vi] = [pe_done]
                        R["bb"].free[bi_] = [lastq]
                        head_toks.append(lastq)

                    def pref(bi, b, tt, c=c):
                        sl = slice(c * CH + tt * 512, c * CH + (tt + 1) * 512)
                        return resid_load(kb, T, R, xsrc[b, :, sl])

                    def epi(bi, b, tt, ps, tm, pre, c=c):
                        sl = slice(c * CH + tt * 512, c * CH + (tt + 1) * 512)
                        return resid_fin(kb, T, R, ps, tm, pre, xdst[b, :, sl])
                    prev_wo = proj_fm(kb, T, R, W[li, "o"], list(range(16)), 16, big, CH, head_toks[-1], epi, prefetch=pref)
                end_phase(kb, T, [])

            phase({
                "big": ([128, 16, CH], BF16, "sb"),
                "qk0": ([128, 3 * CH + 8960], BF16, "sb"), "qk1": ([128, 3 * CH + 8960], BF16, "sb"),
                "vb0": ([128, 70, 128], BF16, "sb"), "vb1": ([128, 70, 128], BF16, "sb"),
                "bb0": ([128, 5 * 7 * 128], BF16, "sb"), "bb1": ([128, 5 * 7 * 128], BF16, "sb"),
                "pt0": ([128, 512], BF16, "sb"), "pt1": ([128, 512], BF16, "sb"), "pt2": ([128, 512], BF16, "sb"), "pt3": ([128, 512], BF16, "sb"),
                "tm0": ([128, 512], F32, "sb"), "tm1": ([128, 512], F32, "sb"), "tm2": ([128, 512], F32, "sb"), "tm3": ([128, 512], F32, "sb"),
                "vo": ([128, 8, 128], BF16, "sb"),
                "rc0": ([128, 128], F32, "sb"), "rc1": ([128, 128], F32, "sb"),
                "wr0": ([128, 2048], BF16, "sb"), "wr1": ([128, 2048], BF16, "sb"),
                "xr0": ([128, 512], F32, "sb"), "xr1": ([128, 512], F32, "sb"),
                "pm0": ([128, 512], F32, "ps"), "ps1": ([128, 512], F32, "ps"), "ps2": ([128, 512], F32, "ps"), "ps3": ([128, 512], F32, "ps"),
                "po0": ([128, 512], F32, "ps"), "po1": ([128, 512], F32, "ps"),
                "pd0": ([128, 512], F32, "ps"), "pd1": ([128, 512], F32, "ps"),
            }, att_script)
            xsrc = XS
            if DBG_STOP == (li, 2):
                break

            def cross_script(kb, T, li=li, gbase=gbase, hbase=hbase):
                R = mk_rings(kb, T, {"xs": 2, "sq": 1, "pn": 1, "rs": 2, "wr": 3, "sq2": 2, "stg": 2, "wv": 1, "ps": 3, "po": 2, "pd": 2, "pt": 3, "rc": 2, "xr": 3})
                R["pm"] = R["ps"]
                big = T["big"]
                mt = T["mt"]
                kx = T["kx"]
                vx = T["vx"]
                qx = T["qx"]
                ox = T["ox"]
                prev = None
                for c in range(NCH):
                    for e_ in ("dve", "pool", "act"):
                        kb.W(e_, prev)
                    tm_ = norm_chunk(kb, T, R, None, 0, gbase + 32, ntok=256, dst=mt, srcap=memT_in[c])

                    def epik(bi, b, tt, ps, tm):
                        raise RuntimeError
                    for b in range(4):
                        wi = R["wr"].nxt()
                        wt = R["wr"].t[wi]
                        kb.W("pool", R["wr"].free[wi])
                        tw = kb.DMA("pool", R["wr"].sem[wi], wt[:, :], W[li, "xk"][b, :, :])
                        kb.W("pe", tw, tm_)
                        pb = R["pm"].nxt()
                        ps = R["pm"].t[pb]
                        kb.W("pe", R["pm"].free[pb])
                        for kc in range(16):
                            tmm = kb.I("pe", lambda e, kc=kc: e.matmul(ps[:, 0:256], wt[:, kc * 128:(kc + 1) * 128], mt[:, kc, :], start=(kc == 0), stop=(kc == 15)), sig=(kc == 15))
                        R["wr"].free[wi] = tmm
                        qi = R["sq2"].nxt()
                        sq = R["sq2"].t[qi]
                        kb.W("act", tmm, R["sq2"].free[qi])
                        t1 = kb.I("act", lambda e: e.activation(out=sq[:, 0:256], in_=ps[:, 0:256], func=AF.Square), sig=True)
                        p2 = R["pn"].nxt()
                        ps2 = R["pn"].t[p2]
                        kb.W("pe", t1, R["pn"].free[p2])
                        t2 = kb.I("pe", lambda e: e.matmul(ps2[:, 0:256], T["ones"][:, :], sq[:, 0:256], start=True, stop=True), sig=True)
                        R["sq2"].free[qi] = t2
                        ri = R["rs"].nxt()
                        rs = R["rs"].t[ri]
                        kb.W("dve", t2, R["rs"].free[ri], tmm, prev)
                        kb.W("act", ("pe", kb.cnt["pe"]), R["rs"].free[ri])
                        t3a = kb.I("act", lambda e: e.activation(out=rs[:, 0:256], in_=ps2[:, 0:256], func=AF.Sqrt, bias=T["epsb"][:, 1:2], scale=1.0), sig=True)
                        kb.W("dve", t3a)
                        t3 = kb.I("dve", lambda e: e.reciprocal(out=rs[:, 0:256], in_=rs[:, 0:256]), sig=True)
                        kb.W("dve", t3)
                        R["pn"].free[p2] = t3
                        t4 = kb.I("dve", lambda e, b=b: e.scalar_tensor_tensor(out=kx[:, b, :], in0=ps[:, 0:256], scalar=T["hvec"][:, hbase + 7:hbase + 8], in1=rs[:, 0:256], op0=ALU.mult, op1=ALU.mult), sig=True)
                        R["rs"].free[ri] = t4
                        R["pm"].free[pb] = t4
                    tk_done = t4
                    def epiv(t16, ps, tm):
                        kb.W("act", tm)
                        return kb.I("act", lambda e: e.activation(out=vx[:, t16, :], in_=ps[:, :], func=AF.Copy), sig=True)
                    vtoks = []

                    def epiv2(t16, ps, tm):
                        t = epiv(t16, ps, tm)
                        vtoks.append(t)
                        return t
                    proj_tm(kb, T, R, W[li, "xv"], 0, mt, 2, tm_, epiv2)
                    toks = norm_chunk(kb, T, R, XS, c, gbase + 16)

                    def epiq(bi, b, tt, ps, tm):
                        qi = R["sq2"].nxt()
                        sq = R["sq2"].t[qi]
                        kb.W("act", tm, R["sq2"].free[qi])
                        t1 = kb.I("act", lambda e: e.activation(out=sq[:], in_=ps[:, :], func=AF.Square), sig=True)
                        p2 = R["pn"].nxt()
                        ps2 = R["pn"].t[p2]
                        kb.W("pe", t1, R["pn"].free[p2])
                        t2 = kb.I("pe", lambda e: e.matmul(ps2[:, :], T["ones"][:, :], sq[:], start=True, stop=True), sig=True)
                        R["sq2"].free[qi] = t2
                        ri = R["rs"].nxt()
                        rs = R["rs"].t[ri]
                        kb.W("dve", t2, R["rs"].free[ri], tm)
                        kb.W("act", ("pe", kb.cnt["pe"]), R["rs"].free[ri])
                        t3a = kb.I("act", lambda e: e.activation(out=rs[:, :], in_=ps2[:, :], func=AF.Sqrt, bias=T["epsb"][:, 1:2], scale=1.0), sig=True)
                        kb.W("dve", t3a)
                        t3 = kb.I("dve", lambda e: e.reciprocal(out=rs[:, :], in_=rs[:, :]), sig=True)
                        kb.W("dve", t3)
                        R["pn"].free[p2] = t3
                        t4 = kb.I("dve", lambda e: e.scalar_tensor_tensor(out=qx[:, b, tt * 512:(tt + 1) * 512], in0=ps[:, :], scalar=T["hvec"][:, hbase + 6:hbase + 7], in1=rs[:, :], op0=ALU.mult, op1=ALU.mult), sig=True)
                        R["rs"].free[ri] = t4
                        return t4
                    proj_fm(kb, T, R, W[li, "xq"], list(range(4)), 16, big, CH, toks, epiq)
                    qdone = ("dve", kb.cnt["dve"])
                    kb.W("pe", qdone, tk_done, vtoks)
                    lastq = None
                    for h in range(4):
                        lastq = attention_head(
                            kb, T, R, NQT, [(0, 0), (0, 1)], lambda i, g, dl: True,
                            lambda g, i, h=h: qx[:, h, i * 128:(i + 1) * 128],
                            lambda g, dl, i, h=h: kx[:, h, dl * 128:(dl + 1) * 128],
                            lambda g, dl, i, h=h: vx[:, dl, h * 128:(h + 1) * 128],
                            lambda g, dl, i: T["ones"][:, :], None,
                            lambda i, h=h: ox[:, h, i * 128:(i + 1) * 128], prev if h == 0 else None, scale=1.0, LA=2)

                    def pref(bi, b, tt, c=c):
                        sl = slice(c * CH + tt * 512, c * CH + (tt + 1) * 512)
                        return resid_load(kb, T, R, XS[b, :, sl])

                    def epi(bi, b, tt, ps, tm, pre, c=c):
                        sl = slice(c * CH + tt * 512, c * CH + (tt + 1) * 512)
                        return resid_fin(kb, T, R, ps, tm, pre, XS[b, :, sl])
                    prev = proj_fm(kb, T, R, W[li, "xo"], list(range(16)), 4, ox, CH, lastq, epi, prefetch=pref)
                end_phase(kb, T, [])

            phase({
                "big": ([128, 16, CH], BF16, "sb"),
                "mt": ([128, 16, 256], BF16, "sb"), "kx": ([128, 4, 256], BF16, "sb"), "vx": ([128, 2, 512], BF16, "sb"),
                "qx": ([128, 4, CH], BF16, "sb"), "ox": ([128, 4, CH], BF16, "sb"),
                "xs0": ([128, 16, 256], F32, "sb"), "xs1": ([128, 16, 256], F32, "sb"),
                "sq0": ([128, 16, 256], BF16, "sb"),
                "rs0": ([128, 512], F32, "sb"), "rs1": ([128, 512], F32, "sb"),
                "wr0": ([128, 2048], BF16, "sb"), "wr1": ([128, 2048], BF16, "sb"), "wr2": ([128, 2048], BF16, "sb"),
                "wv0": ([128, 8192], BF16, "sb"),
                "sq20": ([128, 512], BF16, "sb"), "sq21": ([128, 512], BF16, "sb"),
                "stg0": ([128, 8], BF16, "sb"), "stg1": ([128, 8], BF16, "sb"),
                "pt0": ([128, 512], BF16, "sb"), "pt1": ([128, 512], BF16, "sb"), "pt2": ([128, 512], BF16, "sb"),
                "rc0": ([128, 128], F32, "sb"), "rc1": ([128, 128], F32, "sb"),
                "xr0": ([128, 512], F32, "sb"), "xr1": ([128, 512], F32, "sb"), "xr2": ([128, 512], F32, "sb"),
                "pn0": ([128, 512], F32, "ps"),
                "pm0": ([128, 512], F32, "ps"), "ps1": ([128, 512], F32, "ps"), "ps2": ([128, 512], F32, "ps"),
                "po0": ([128, 512], F32, "ps"), "po1": ([128, 512], F32, "ps"),
                "pd0": ([128, 512], F32, "ps"), "pd1": ([128, 512], F32, "ps"),
            }, cross_script)
            if DBG_STOP == (li, 3):
                break

            xout = yT if last_layer else XS

            def mlpup_script(kb, T, li=li, gbase=gbase):
                R = mk_rings(kb, T, {"xs": 2, "sq": 1, "pn": 1, "rs": 2, "wr": 3, "pm": 4, "rl": 2, "stg": 3})
                big = T["big"]
                prev = None
                for c in range(NCH):
                    for e_ in ("dve", "pool", "sp"):
                        kb.W(e_, prev)
                    toks = norm_chunk(kb, T, R, XS, c, gbase + 48)

                    def epi(bi, b, tt, ps, tm, c=c):
                        ri = R["rl"].nxt()
                        rl = R["rl"].t[ri]
                        kb.W("act", tm, R["rl"].free[ri])
                        t1 = kb.I("act", lambda e: e.activation(out=rl[:], in_=ps[:, :], func=AF.Relu), sig=True)
                        si = R["stg"].nxt()
                        sg = R["stg"].t[si]
                        kb.W("dve", t1, R["stg"].free[si])
                        t2 = kb.I("dve", lambda e: e.tensor_tensor(out=sg[:], in0=rl[:], in1=rl[:], op=ALU.mult), sig=True)
                        R["rl"].free[ri] = t2
                        kb.W("sp", t2)
                        R["stg"].free[si] = kb.DMA("sp", R["stg"].sem[si], AT[b, :, c * CH + tt * 512: c * CH + (tt + 1) * 512], sg[:])
                        return t1
                    proj_fm(kb, T, R, W[li, "up"], list(range(64)), 16, big, CH, toks, epi)
                    prev = ("pe", kb.cnt["pe"])
                end_phase(kb, T, [])

            phase({
                "big": ([128, 16, CH], BF16, "sb"),
                "xs0": ([128, 16, 256], F32, "sb"), "xs1": ([128, 16, 256], F32, "sb"),
                "sq0": ([128, 16, 256], BF16, "sb"),
                "rs0": ([128, 512], F32, "sb"), "rs1": ([128, 512], F32, "sb"),
                "wr0": ([128, 2048], BF16, "sb"), "wr1": ([128, 2048], BF16, "sb"), "wr2": ([128, 2048], BF16, "sb"),
                "rl0": ([128, 512], F32, "sb"), "rl1": ([128, 512], F32, "sb"),
                "stg0": ([128, 512], BF16, "sb"), "stg1": ([128, 512], BF16, "sb"), "stg2": ([128, 512], BF16, "sb"),
                "pn0": ([128, 512], F32, "ps"),
                "pm0": ([128, 512], F32, "ps"), "pm1": ([128, 512], F32, "ps"), "pm2": ([128, 512], F32, "ps"), "pm3": ([128, 512], F32, "ps"),
            }, mlpup_script)

            def mlpdn_script(kb, T, li=li, xout=xout):
                R = mk_rings(kb, T, {"wd": 2, "pm": 4, "xr": 3, "ab": 1, "wb": 2})
                ab = T["abig"]
                twb = [None] * 16
                for s8 in range(T_ // 1024):
                    kb.W("sp", ("pe", kb.cnt["pe"]))
                    tla = []
                    for q4 in range(4):
                        tla.append(kb.DMA("sp", R["ab"].sem[0], ab[:, q4 * 16:(q4 + 1) * 16, :], AT[q4 * 16:(q4 + 1) * 16, :, s8 * 1024:(s8 + 1) * 1024].rearrange("k p t -> p k t")))
                    for b in range(16):
                        wi = R["wd"].nxt()
                        wt = R["wd"].t[wi]
                        if s8 == 0:
                            kb.W("pool", R["wd"].free[wi])
                            tw = kb.DMA("pool", R["wd"].sem[wi], wt[:, :], W[li, "dn"][b, :, :])
                            kb.W("act", tw)
                            twb[b] = kb.DMA("act", R["wb"].sem[wi], WDB[b, :, :], wt[:, :])
                        else:
                            kb.W("act", R["wd"].free[wi], [(sm, kb.cnt[sm]) for sm in R["wb"].sem])
                            tw = kb.DMA("act", R["wd"].sem[wi], wt[:, :], WDB[b, :, :])
                        kb.W("pe", tw, tla)
                        for tt in range(2):
                            pb = R["pm"].nxt()
                            ps = R["pm"].t[pb]
                            kb.W("pe", R["pm"].free[pb])
                            for kc in range(64):
                                tmm = kb.I("pe", lambda e, kc=kc, tt=tt: e.matmul(ps[:, :], wt[:, kc * 128:(kc + 1) * 128], ab[:, kc, tt * 512:(tt + 1) * 512], start=(kc == 0), stop=(kc == 63)), sig=(kc == 63))
                            sl = slice(s8 * 1024 + tt * 512, s8 * 1024 + (tt + 1) * 512)
                            R["pm"].free[pb] = epi_resid(kb, T, R, ps, tmm, XS[b, :, sl], xout[b, :, sl])
                        R["wd"].free[wi] = [tmm, twb[b]] if s8 == 0 else [tmm]
                end_phase(kb, T, [])

            phase({
                "abig": ([128, 64, 1024], BF16, "sb"),
                "wd0": ([128, 8192], BF16, "sb"), "wd1": ([128, 8192], BF16, "sb"),
                "ab0": ([128, 8], BF16, "sb"), "wb0": ([128, 8], BF16, "sb"), "wb1": ([128, 8], BF16, "sb"),
                "xr0": ([128, 512], F32, "sb"), "xr1": ([128, 512], F32, "sb"), "xr2": ([128, 512], F32, "sb"),
                "pm0": ([128, 512], F32, "ps"), "pm1": ([128, 512], F32, "ps"), "pm2": ([128, 512], F32, "ps"), "pm3": ([128, 512], F32, "ps"),
            }, mlpdn_script)
    return nc


def _blk(w, KC):
    K, M = w.shape
    return np.ascontiguousarray(w.reshape(KC, 128, M // 128, 128).transpose(2, 1, 0, 3).reshape(M // 128, 128, KC * 128))


def _blkv(w):
    K, M = w.shape
    return np.ascontiguousarray(w.reshape(16, 128, M // 512, 512).transpose(2, 1, 0, 3).reshape(M // 512, 128, 16 * 512))


def _t5_bucket(rel):
    nb = 16
    max_exact = 8
    ret = np.where(rel > 0, nb, 0)
    n = np.abs(rel)
    n_f = np.maximum(n, 1).astype(np.float32)
    large = max_exact + (np.log(n_f / np.float32(max_exact)) / np.float32(math.log(1024 / max_exact)) * np.float32(nb - max_exact)).astype(np.int32)
    large = np.minimum(large, nb - 1)
    return ret + np.where(n < max_exact, n, large)


def _dil_bias(t5_table):
    out = np.full((NH, 128, 27 * 128), NEG, np.float32)
    j = np.arange(128)[:, None]
    i = np.arange(128)[None, :]
    s = 0
    for g, (win, d) in enumerate(DIL):
        for dl in range(-REACH_DIL[g], REACH_DIL[g] + 1):
            rel = 128 * dl + j - i
            ok = (rel % d == 0) & (np.abs(rel) <= 64 * d)
            bk = _t5_bucket(rel)
            vals = t5_table[bk, g, :]
            blk = np.where(ok[None], vals.transpose(2, 0, 1), np.float32(NEG))
            out[:, :, s * 128:(s + 1) * 128] = blk
            s += 1
    return out


def _na_bias(rpb):
    out = np.full((NH, 128, 5, 7, 128), NEG, np.float32)
    R = 32
    kc = np.arange(64)
    cs = np.clip(kc - 8, 0, 48)
    col_ok = (kc[None, :] >= cs[:, None]) & (kc[None, :] < cs[:, None] + 16)
    dc = np.clip(kc[None, :] - kc[:, None], -15, 15) + 15
    for cls in range(5):
        qt = {0: 8, 1: 0, 2: 1, 3: 14, 4: 15}[cls]
        for si, dl in enumerate(range(-3, 4)):
            for a in range(2):
                r = 2 * qt + a
                if cls == 0:
                    rs = r - 4
                else:
                    rs = min(max(r - 4, 0), R - 8)
                for b in range(2):
                    kr = 2 * (qt + dl) + b
                    if kr < rs or kr >= rs + 8:
                        continue
                    dr = kr - r + 7
                    vals = rpb[:, dr, :][:, dc]
                    blk = np.where(col_ok[None], vals, np.float32(NEG)).transpose(0, 2, 1)
                    out[:, b * 64:(b + 1) * 64, cls, si, a * 64:(a + 1) * 64] = blk
    return np.ascontiguousarray(out.reshape(NH, 128, 5 * 7 * 128))


def _prep_weights(inp, LAYERS):
    Wd = {}
    f = np.float32
    for li, L in enumerate(LAYERS):
        l2 = L // 2
        if L % 2 == 0:
            w = inp["w_qkv_a"][l2]
            wq, wk, wv = w[:, 0:2048], w[:, 2048:4096], w[:, 4096:6144]
            Wd["wqk%d" % li] = _blk(np.concatenate([wq, wk], 1), 16)
            Wd["wv%d" % li] = _blkv(wv)
            Wd["bias%d" % li] = _na_bias(inp["rpb_a"][l2])
            Wd["wo%d" % li] = _blk(inp["w_o_a"][l2], 16)
        else:
            w = inp["w_qkv_b"][l2].reshape(2048, 3, 3, 2048)
            qk = np.concatenate([np.concatenate([w[:, g, 0], w[:, g, 1]], 1) for g in range(3)], 1)
            Wd["wqk%d" % li] = _blk(qk, 16)
            Wd["wv%d" % li] = _blkv(np.concatenate([w[:, g, 2] for g in range(3)], 1))
            Wd["bias%d" % li] = _dil_bias(inp["t5_table"])
            Wd["wo%d" % li] = _blk(inp["w_o_b"][l2], 16)
        Wd["wxq%d" % li] = _blk(inp["w_q_x"][L], 16)
        wkv = inp["w_kv_x"][L]
        Wd["wxk%d" % li] = _blk(wkv[:, 0:512], 16)
        Wd["wxv%d" % li] = _blkv(wkv[:, 512:1024])
        Wd["wxo%d" % li] = _blk(inp["w_o_x"][L], 4)
        Wd["wup%d" % li] = _blk(inp["w_up"][L], 16)
        Wd["wdn%d" % li] = _blk(inp["w_down"][L], 64)
    return Wd


def _vec_inputs(inp, LAYERS):
    gv = np.zeros((128, 4 * len(LAYERS) * 16), np.float32)
    hv = np.zeros((128, len(LAYERS) * 16), np.float32)
    for li, L in enumerate(LAYERS):
        for j, nm in enumerate(("g_mix", "g_cross", "g_mem", "g_mlp")):
            gv[:, li * 64 + j * 16: li * 64 + (j + 1) * 16] = inp[nm][L].reshape(16, 128).T
        l2 = L // 2
        if L % 2 == 0:
            hv[:, li * 16 + 0] = inp["q_norm_a"][l2]
            hv[:, li * 16 + 1] = inp["k_norm_a"][l2]
        else:
            for g in range(3):
                hv[:, li * 16 + 2 * g] = inp["q_norm_b"][l2, g]
                hv[:, li * 16 + 2 * g + 1] = inp["k_norm_b"][l2, g]
        hv[:, li * 16 + 6] = inp["q_norm_x"][L]
        hv[:, li * 16 + 7] = inp["k_norm_x"][L]
    return gv, hv


def _flags(kind, NCH):
    fl = np.zeros((128, 32), np.float32)
    for c in range(NCH):
        if kind == "prompt":
            top = 1.0 if c == 0 else 0.0
            bot = 1.0 if c == NCH - 1 else 0.0
        else:
            top = bot = 1.0
        fl[:, c] = top
        fl[:, 4 + c] = 1.0 - top
        fl[:, 8 + c] = bot
        fl[:, 12 + c] = 1.0 - bot
        fl[:, 16 + c] = 1.0 - top
        fl[:, 20 + c] = 1.0 - bot
    return fl


_CACHE = {}


def _run(inp, NCH, LAYERS, seqs, DBG_STOP=None, ncores=8):
    key = (NCH, tuple(LAYERS), DBG_STOP)
    if key not in _CACHE:
        _CACHE[key] = build(NCH, LAYERS, DBG_STOP)
    nc = _CACHE[key]
    Wd = _prep_weights(inp, LAYERS)
    gv, hv = _vec_inputs(inp, LAYERS)
    in_maps = []
    for (kind, xT, memT) in seqs:
        m = dict(Wd)
        m["xT"] = np.ascontiguousarray(xT.reshape(16, 128, NCH * CH))
        m["memT"] = np.ascontiguousarray(memT.reshape(NCH, 16, 128, 256))
        m["gvec"] = gv
        m["hvec"] = hv
        m["flags"] = _flags(kind, NCH)
        m["onesm"] = np.ones((128, 128), np.float32)
        in_maps.append(m)
    while len(in_maps) < ncores:
        in_maps.append(in_maps[-1])
    res = run_bass_kernel_spmd(nc, in_maps, core_ids=list(range(ncores)))
    return [r["yT"].reshape(2048, NCH * CH) for r in res.results]


def kernel(**inp):
    inp = {k: np.asarray(v) for k, v in inp.items()}
    LAYERS = [0, 1, 2, 3]
    NCH = 4
    xp = inp["x_prompt"][0]
    xs = inp["x_sample"].reshape(4 * 2048, 2048)
    memp = np.broadcast_to(inp["mem_prompt"][0].T[None], (4, 2048, 256))
    mems = inp["mem_sample"].transpose(0, 2, 1)
    seqs = [("prompt", np.ascontiguousarray(xp.T), np.ascontiguousarray(memp)),
            ("sample", np.ascontiguousarray(xs.T), np.ascontiguousarray(mems))]
    outs = _run(inp, NCH, LAYERS, seqs, ncores=2)
    y_prompt = np.ascontiguousarray(outs[0].T).reshape(1, 8192, 2048).astype(np.float32)
    y_sample = np.ascontiguousarray(outs[1].T).reshape(4, 2048, 2048).astype(np.float32)
    return (y_prompt, y_sample)
```

```python
import math
import contextlib
import numpy as np
import concourse.bass as bass
import concourse.mybir as mybir
from concourse.bass_utils import run_bass_kernel_spmd

F32 = mybir.dt.float32
BF16 = mybir.dt.bfloat16
AF = mybir.ActivationFunctionType
ALU = mybir.AluOpType

D = 2048
DH = 128
NH = 16
CH = 2048
NQT = CH // 128
PAD = 1024
EPS = 1e-6
SCALE = 1.0 / math.sqrt(DH)
NEG = -30000.0
DIL = ((128, 1), (512, 4), (2048, 16))
REACH_DIL = (1, 2, 8)
REACH_NA = 3
ENGS = ("sp", "act", "pool", "pe", "dve")


class KB:
    def __init__(self, nc, sems):
        self.nc = nc
        self.sems = sems
        self.cnt = {k: 0 for k in sems}
        self.cur = None
        self.E = None
        self.waited = {}
        self.free_sems = [k for k in sems if k.startswith("r")]
        self.ring_sem = {}

    def I(self, eng, fn, sig=False):
        tok = None
        if sig:
            self.cnt[eng] += 1
            tok = (eng, self.cnt[eng])
        if self.cur == eng:
            ins = fn(self.E)
            if sig:
                ins.then_inc(self.sems[eng], 1)
        return tok

    def DMA(self, q, sem, out, in_):
        self.cnt[sem] += 16
        tok = (sem, self.cnt[sem])
        if self.cur == q:
            self.E.dma_start(out=out, in_=in_).then_inc(self.sems[sem], 16)
        return tok

    def W(self, eng, *toks):
        if self.cur != eng:
            return
        for t in toks:
            if t is None:
                continue
            if isinstance(t, list):
                self.W(eng, *t)
                continue
            s, v = t
            if self.waited.get((eng, s), 0) < v:
                self.E.wait_ge(self.sems[s], v)
                self.waited[(eng, s)] = v

    def state(self):
        return (dict(self.cnt),)

    def restore(self, st):
        self.cnt = dict(st[0])


class Ring:
    def __init__(self, kb, name, tiles, semnames):
        self.kb = kb
        self.t = tiles
        self.n = len(tiles)
        self.sem = semnames
        self.reset()

    def reset(self):
        self.use = 0
        self.free = [None] * self.n

    def nxt(self):
        i = self.use % self.n
        self.use += 1
        return i


def run_phase(nc, kb, tensors, nsem_rings, script):
    with contextlib.ExitStack() as st:
        T = {}
        for name, (shape, dt, kind) in tensors.items():
            if kind == "sb":
                T[name] = st.enter_context(nc.sbuf_tensor(name + "_%d" % kb.phase_id, shape, dt))
            else:
                T[name] = st.enter_context(nc.psum_tensor(name + "_%d" % kb.phase_id, shape, dt))
        block = st.enter_context(nc.Block())
        s0 = kb.state()

        def mk(engname):
            def f(e):
                kb.restore(s0)
                kb.cur = engname
                kb.E = e
                script(kb, T)
                kb.cur = None
            return f

        block.sync(mk("sp"))
        block.scalar(mk("act"))
        block.gpsimd(mk("pool"))
        block.tensor(mk("pe"))
        block.vector(mk("dve"))
    kb.phase_id += 1


def end_phase(kb, T, extra):
    toks = list(extra)
    toks.append(kb.I("act", lambda e: e.activation(out=T["scr"][:, 0:1], in_=T["scr"][:, 1:2], func=AF.Copy), sig=True))
    toks.append(kb.I("dve", lambda e: e.tensor_copy(out=T["scr"][:, 2:3], in_=T["scr"][:, 3:4]), sig=True))
    toks.append(kb.I("pool", lambda e: e.tensor_copy(out=T["scr"][:, 4:5], in_=T["scr"][:, 5:6]), sig=True))
    for s in kb.cnt:
        if s.startswith("r") and kb.cnt[s] > 0:
            toks.append((s, kb.cnt[s]))
    kb.W("pe", *toks)
    toks.append(kb.I("pe", lambda e: e.matmul(T["pm0"][:, 0:8], T["ones"][:, 0:128], T["ones"][:, 0:8], start=True, stop=True), sig=True))
    for s in kb.cnt:
        if s.startswith("r") and kb.cnt[s] > 0:
            toks.append((s, kb.cnt[s]))
    for e in ENGS:
        kb.W(e, *toks)


def build(NCH, LAYERS, DBG_STOP=None):
    T_ = NCH * CH
    TP = T_ + 2 * PAD
    nc = bass.Bass("TRN2", target_bir_lowering=False)

    def din(name, shape, dt=F32):
        return nc.dram_tensor(name, list(shape), dt, kind="ExternalInput")

    xT_in = din("xT", [16, 128, T_])
    memT_in = din("memT", [NCH, 16, 128, 256])
    gvec_in = din("gvec", [128, 4 * len(LAYERS) * 16])
    hvec_in = din("hvec", [128, len(LAYERS) * 16])
    flags_in = din("flags", [128, 32])
    ones_in = din("onesm", [128, 128])
    W = {}
    for li, L in enumerate(LAYERS):
        if L % 2 == 0:
            W[li, "qk"] = din("wqk%d" % li, [32, 128, 16 * 128])
            W[li, "v"] = din("wv%d" % li, [4, 128, 16 * 512])
            W[li, "bias"] = din("bias%d" % li, [NH, 128, 5 * 7 * 128])
        else:
            W[li, "qk"] = din("wqk%d" % li, [96, 128, 16 * 128])
            W[li, "v"] = din("wv%d" % li, [12, 128, 16 * 512])
            W[li, "bias"] = din("bias%d" % li, [NH, 128, 27 * 128])
        W[li, "o"] = din("wo%d" % li, [16, 128, 16 * 128])
        W[li, "xq"] = din("wxq%d" % li, [4, 128, 16 * 128])
        W[li, "xk"] = din("wxk%d" % li, [4, 128, 16 * 128])
        W[li, "xv"] = din("wxv%d" % li, [1, 128, 16 * 512])
        W[li, "xo"] = din("wxo%d" % li, [16, 128, 4 * 128])
        W[li, "up"] = din("wup%d" % li, [64, 128, 16 * 128])
        W[li, "dn"] = din("wdn%d" % li, [16, 128, 64 * 128])
    yT = nc.dram_tensor("yT", [16, 128, T_], F32, kind="ExternalOutput")
    XS = yT if DBG_STOP is not None else nc.dram_tensor("xscr", [16, 128, T_], F32)
    QT = [nc.dram_tensor("qt%d" % g, [NH, 128, T_], BF16) for g in range(3)]
    KT = [nc.dram_tensor("kt%d" % g, [NH, 128, TP], BF16) for g in range(3)]
    VV = [nc.dram_tensor("vv%d" % g, [NH, TP, 128], BF16) for g in range(3)]
    AT = nc.dram_tensor("at", [64, 128, T_], BF16)
    WDB = nc.dram_tensor("wdb", [16, 128, 8192], BF16)

    semnames = list(ENGS) + ["r%d" % i for i in range(64)]
    with contextlib.ExitStack() as st:
        sems = {k: st.enter_context(nc.semaphore("s_" + k)) for k in semnames}
        kb = KB(nc, sems)
        kb.phase_id = 0
        rsem = iter(["r%d" % i for i in range(64)])
        RS = {}

        def ringsem(name, n):
            if name not in RS:
                RS[name] = []
            while len(RS[name]) < n:
                RS[name].append(next(rsem))
            return RS[name]

        ones = st.enter_context(nc.sbuf_tensor("ones", [128, 128], BF16))
        gvec = st.enter_context(nc.sbuf_tensor("gvec_sb", [128, 4 * len(LAYERS) * 16], F32))
        hvec = st.enter_context(nc.sbuf_tensor("hvec_sb", [128, len(LAYERS) * 16], F32))
        flags = st.enter_context(nc.sbuf_tensor("flags_sb", [128, 32], F32))
        scr = st.enter_context(nc.sbuf_tensor("scr", [128, 8], F32))
        zer = st.enter_context(nc.sbuf_tensor("zer", [128, 1024], BF16))
        epsb = st.enter_context(nc.sbuf_tensor("epsb", [128, 2], F32))
        COMMON = {"epsb": epsb, "ones": ones, "gvec": gvec, "hvec": hvec, "flags": flags, "scr": scr, "zer": zer}

        def phase(tensors, script):
            tensors = dict(tensors)
            if "pm0" not in tensors:
                tensors["pm0"] = ([128, 512], F32, "ps")

            def s2(kb, T):
                T = dict(T)
                T.update(COMMON)
                if "ps0" not in T and "ps1" in T:
                    T["ps0"] = T["pm0"]
                script(kb, T)
            run_phase(nc, kb, tensors, 0, s2)

        def init_script(kb, T):
            s = ringsem("init", 1)[0]
            t = []
            t.append(kb.DMA("pool", s, T["ones"][:], ones_in[:, :]))
            t.append(kb.DMA("sp", s, T["gvec"][:], gvec_in[:, :]))
            t.append(kb.DMA("sp", s, T["hvec"][:], hvec_in[:, :]))
            t.append(kb.DMA("sp", s, T["flags"][:], flags_in[:, :]))
            kb.W("dve", t[1], t[2])
            kb.I("dve", lambda e: e.tensor_scalar(out=T["gvec"][:], in0=T["gvec"][:], scalar1=float(math.sqrt(D)), scalar2=None, op0=ALU.mult), sig=True)
            hk = T["hvec"][:].rearrange("p (a two) -> p a two", two=2)[:, :, 1]
            kb.I("dve", lambda e: e.tensor_scalar(out=hk, in0=hk, scalar1=float(math.sqrt(DH)), scalar2=None, op0=ALU.mult), sig=True)
            kb.W("pe", t[0])
            t1 = kb.I("dve", lambda e: e.memset(T["zer"][:], 0.0), sig=True)
            kb.I("dve", lambda e: e.memset(T["epsb"][:, 0:1], float(D * EPS)), sig=True)
            kb.I("dve", lambda e: e.memset(T["epsb"][:, 1:2], float(DH * EPS)), sig=True)
            t2 = kb.I("dve", lambda e: e.memset(T["scr"][:], 0.0), sig=True)
            kb.W("sp", t1)
            for g in range(3):
                for h in range(NH):
                    kb.DMA("sp", s, KT[g][h, :, 0:PAD], T["zer"][:, 0:PAD])
                    kb.DMA("sp", s, KT[g][h, :, PAD + T_:TP], T["zer"][:, 0:PAD])
                    kb.DMA("sp", s, VV[g][h, 0:PAD, :].rearrange("(a p) d -> p a d", p=128), T["zer"][:, 0:PAD].rearrange("p (a d) -> p a d", d=128))
                    kb.DMA("sp", s, VV[g][h, PAD + T_:TP, :].rearrange("(a p) d -> p a d", p=128), T["zer"][:, 0:PAD].rearrange("p (a d) -> p a d", d=128))
            end_phase(kb, T, [t2])

        phase({}, init_script)

        class Ctx:
            pass

        def mk_rings(kb, T, spec):
            R = {}
            for name, n in spec.items():
                R[name] = Ring(kb, name, [T["%s%d" % (name, i)] for i in range(n)], ringsem(name, n))
            return R

        def norm_chunk(kb, T, R, src, c, gcol, ntok=CH, tok0=None, dst=None, srcap=None):
            dst = T["big"] if dst is None else dst
            done = []
            nsub = ntok // 256
            for j in range(nsub):
                i = R["xs"].nxt()
                xs = R["xs"].t[i]
                kb.W("sp", R["xs"].free[i])
                if srcap is None:
                    a = src[:, :, c * CH + j * 256: c * CH + (j + 1) * 256].rearrange("k p t -> p k t")
                else:
                    a = srcap[:, :, j * 256:(j + 1) * 256].rearrange("k p t -> p k t")
                tl = kb.DMA("sp", R["xs"].sem[i], xs[:], a)
                kb.W("act", tl)
                qi = R["sq"].nxt()
                sq = R["sq"].t[qi]
                kb.W("act", R["sq"].free[qi])
                t1 = kb.I("act", lambda e: e.activation(out=sq[:], in_=xs[:], func=AF.Square), sig=True)
                pb = R["pn"].nxt()
                ps = R["pn"].t[pb]
                kb.W("pe", t1, R["pn"].free[pb])
                for kc in range(16):
                    t2 = kb.I("pe", lambda e, kc=kc: e.matmul(ps[:, 0:256], T["ones"][:, :], sq[:, kc, :], start=(kc == 0), stop=(kc == 15)), sig=(kc == 15))
                R["sq"].free[qi] = t2
                ri = R["rs"].nxt()
                rs = R["rs"].t[ri]
                kb.W("dve", t2, R["rs"].free[ri])
                kb.W("act", ("pe", kb.cnt["pe"]), R["rs"].free[ri])
                t3a = kb.I("act", lambda e: e.activation(out=rs[:, 0:256], in_=ps[:, 0:256], func=AF.Sqrt, bias=T["epsb"][:, 0:1], scale=1.0), sig=True)
                kb.W("dve", t3a)
                t3 = kb.I("dve", lambda e: e.reciprocal(out=rs[:, 0:256], in_=rs[:, 0:256]), sig=True)
                kb.W("dve", t3)
                R["pn"].free[pb] = t3
                kb.W("dve", t3, tl)
                last = []
                for kc in range(16):
                    eng = "dve"
                    tk = kb.I(eng, lambda e, kc=kc: e.scalar_tensor_tensor(
                        out=dst[:, kc, j * 256:(j + 1) * 256], in0=xs[:, kc, :], scalar=T["gvec"][:, gcol + kc:gcol + kc + 1],
                        in1=rs[:, 0:256], op0=ALU.mult, op1=ALU.mult), sig=(kc == 15))
                    if tk is not None:
                        last.append(tk)
                R["xs"].free[i] = last
                R["rs"].free[ri] = last
                done += last
            return done

        def proj_fm(kb, T, R, wd, blocks, KC, src, ntok, src_toks, epi, wview=None, prefetch=None):
            ntt = ntok // 512
            pend = None
            tiles = [(bi, b, tt) for bi, b in enumerate(blocks) for tt in range(ntt)]
            pre = {}
            if prefetch is not None:
                pre[0] = prefetch(*tiles[0])
            for bi, b in enumerate(blocks):
                wi = R["wr"].nxt()
                wt = R["wr"].t[wi]
                kb.W("pool", R["wr"].free[wi])
                tw = kb.DMA("pool", R["wr"].sem[wi], wt[:, 0:KC * 128], wd[b, :, :])
                kb.W("pe", tw, src_toks)
                for tt in range(ntt):
                    pb = R["pm"].nxt()
                    ps = R["pm"].t[pb]
                    kb.W("pe", R["pm"].free[pb])
                    for kc in range(KC):
                        tm = kb.I("pe", lambda e, kc=kc, tt=tt: e.matmul(ps[:, :], wt[:, kc * 128:(kc + 1) * 128], src[:, kc, tt * 512:(tt + 1) * 512], start=(kc == 0), stop=(kc == KC - 1)), sig=(kc == KC - 1))
                    if prefetch is not None:
                        idx = bi * ntt + tt
                        if idx + 1 < len(tiles):
                            pre[idx + 1] = prefetch(*tiles[idx + 1])
                        r = epi(bi, b, tt, ps, tm, pre.pop(idx))
                    else:
                        r = epi(bi, b, tt, ps, tm)
                    if pend is not None:
                        R["pm"].free[pend[0]] = pend[1]()
                        pend = None
                    if callable(r):
                        pend = (pb, r)
                    else:
                        R["pm"].free[pb] = r
                R["wr"].free[wi] = tm
            if pend is not None:
                R["pm"].free[pend[0]] = pend[1]()
            return tm

        def epi_qknorm(kb, T, R, ps, tm, gain_ap, dst_ap):
            qi = R["sq2"].nxt()
            sq = R["sq2"].t[qi]
            kb.W("act", tm, R["sq2"].free[qi])
            t1 = kb.I("act", lambda e: e.activation(out=sq[:], in_=ps[:, :], func=AF.Square), sig=True)

            def cont():
                pb = R["pn"].nxt()
                ps2 = R["pn"].t[pb]
                kb.W("pe", t1, R["pn"].free[pb])
                t2 = kb.I("pe", lambda e: e.matmul(ps2[:, :], T["ones"][:, :], sq[:], start=True, stop=True), sig=True)
                R["sq2"].free[qi] = t2
                ri = R["rs"].nxt()
                rs = R["rs"].t[ri]
                kb.W("act", t2, R["rs"].free[ri])
                t3a = kb.I("act", lambda e: e.activation(out=rs[:, :], in_=ps2[:, :], func=AF.Sqrt, bias=T["epsb"][:, 1:2], scale=1.0), sig=True)
                kb.W("dve", t3a, tm)
                t3 = kb.I("dve", lambda e: e.reciprocal(out=rs[:, :], in_=rs[:, :]), sig=True)
                kb.W("dve", t3)
                R["pn"].free[pb] = t3
                si = R["stg"].nxt()
                sg = R["stg"].t[si]
                kb.W("dve", R["stg"].free[si])
                t4 = kb.I("dve", lambda e: e.scalar_tensor_tensor(out=sg[:], in0=ps[:, :], scalar=gain_ap, in1=rs[:, :], op0=ALU.mult, op1=ALU.mult), sig=True)
                R["rs"].free[ri] = t4
                if dst_ap is not None:
                    kb.W("sp", t4)
                    R["stg"].free[si] = kb.DMA("sp", R["stg"].sem[si], dst_ap, sg[:])
                return t4
            return cont

        def resid_load(kb, T, R, src_ap):
            xi = R["xr"].nxt()
            xr = R["xr"].t[xi]
            kb.W("sp", R["xr"].free[xi])
            tl = kb.DMA("sp", R["xr"].sem[xi], xr[:], src_ap)
            return (xi, xr, tl)

        def resid_fin(kb, T, R, ps, tm, pre, dst_ap):
            xi, xr, tl = pre
            kb.W("dve", tl, tm)
            t1 = kb.I("dve", lambda e: e.tensor_tensor(out=xr[:], in0=ps[:, :], in1=xr[:], op=ALU.add), sig=True)
            kb.W("sp", t1)
            R["xr"].free[xi] = kb.DMA("sp", R["xr"].sem[xi], dst_ap, xr[:])
            return t1

        def epi_resid(kb, T, R, ps, tm, src_ap, dst_ap):
            xi = R["xr"].nxt()
            xr = R["xr"].t[xi]
            kb.W("sp", R["xr"].free[xi])
            tl = kb.DMA("sp", R["xr"].sem[xi], xr[:], src_ap)
            kb.W("dve", tl, tm)
            t1 = kb.I("dve", lambda e: e.tensor_tensor(out=xr[:], in0=ps[:, :], in1=xr[:], op=ALU.add), sig=True)
            kb.W("sp", t1)
            R["xr"].free[xi] = kb.DMA("sp", R["xr"].sem[xi], dst_ap, xr[:])
            return t1

        def proj_tm(kb, T, R, wd, g, src, ntiles, src_toks, epi):
            wi = R["wv"].nxt()
            wt = R["wv"].t[wi]
            kb.W("pool", R["wv"].free[wi])
            tw = kb.DMA("pool", R["wv"].sem[wi], wt[:], wd[g, :, :])
            kb.W("pe", tw, src_toks)
            for t16 in range(ntiles):
                pb = R["pm"].nxt()
                ps = R["pm"].t[pb]
                kb.W("pe", R["pm"].free[pb])
                for kc in range(16):
                    tm = kb.I("pe", lambda e, kc=kc, t16=t16: e.matmul(ps[:, :], src[:, kc, t16 * 128:(t16 + 1) * 128], wt[:, kc * 512:(kc + 1) * 512], start=(kc == 0), stop=(kc == 15)), sig=(kc == 15))
                R["pm"].free[pb] = epi(t16, ps, tm)
            R["wv"].free[wi] = tm

        def attention_head(kb, T, R, nqt, slots, active, qtile, ksrc, vsrc, den_lhsT, bias_fn, out_ap, first_dep, scale=None, LA=3):
            units = []
            nb = (len(slots) + 3) // 4
            for i in range(nqt):
                ulist = []
                for bi in range(nb):
                    bs = slots[bi * 4:(bi + 1) * 4]
                    act = [active(i, g, dl) for (g, dl) in bs]
                    if any(act):
                        ulist.append([i, bi, bs, act, False, False])
                ulist[0][4] = True
                ulist[-1][5] = True
                units += ulist
            state = {}
            cur = {}
            res = {"last": None}

            def emit_S(u):
                i, bi, bs, act, fb, lb = units[u]
                n = len(bs)
                si = R["ps"].nxt()
                ps = R["ps"].t[si]
                kb.W("pe", R["ps"].free[si])
                lastj = max(j for j in range(n) if act[j])
                for j, (g, dl) in enumerate(bs):
                    if not act[j]:
                        continue
                    tS = kb.I("pe", lambda e, j=j, g=g, dl=dl: e.matmul(ps[:, j * 128:(j + 1) * 128], ksrc(g, dl, i), qtile(g, i), start=True, stop=True), sig=(j == lastj))
                pi = R["pt"].nxt()
                pt = R["pt"].t[pi]
                if bias_fn is not None:
                    ti = R["tm"].nxt()
                    tmp = R["tm"].t[ti]
                    kb.W("dve", tS, R["tm"].free[ti])
                    tB = bias_fn(i, bi, bs, tmp, ps)
                    R["ps"].free[si] = tB
                    kb.W("act", tB, R["pt"].free[pi])
                    tE = kb.I("act", lambda e: e.activation(out=pt[:, 0:n * 128], in_=tmp[:, 0:n * 128], func=AF.Exp), sig=True)
                    R["tm"].free[ti] = tE
                else:
                    kb.W("act", tS, R["pt"].free[pi])
                    tE = kb.I("act", lambda e: e.activation(out=pt[:, 0:n * 128], in_=ps[:, 0:n * 128], func=AF.Exp, scale=float(scale)), sig=True)
                    R["ps"].free[si] = tE
                state[u] = (pi, pt, tE)

            def emit_PV(u):
                i, bi, bs, act, fb, lb = units[u]
                n = len(bs)
                pi, pt, tE = state.pop(u)
                if fb:
                    oi = R["po"].nxt()
                    cur["oi"] = oi
                    kb.W("pe", R["po"].free[oi])
                oi = cur["oi"]
                po = R["po"].t[oi]
                pd = R["pd"].t[oi]
                kb.W("pe", tE)
                js = [j for j in range(n) if act[j]]
                for j in js:
                    g, dl = bs[j]
                    st_ = (fb and j == js[0])
                    sp_ = (lb and j == js[-1])
                    kb.I("pe", lambda e, j=j, g=g, dl=dl, st_=st_, sp_=sp_: e.matmul(po[:, 0:128], vsrc(g, dl, i), pt[:, j * 128:(j + 1) * 128], start=st_, stop=sp_))
                    tP = kb.I("pe", lambda e, j=j, g=g, dl=dl, st_=st_, sp_=sp_: e.matmul(pd[:, 0:128], den_lhsT(g, dl, i), pt[:, j * 128:(j + 1) * 128], start=st_, stop=sp_), sig=(j == js[-1]))
                R["pt"].free[pi] = tP
                if lb:
                    ri = R["rc"].nxt()
                    rc = R["rc"].t[ri]
                    kb.W("dve", tP, R["rc"].free[ri], first_dep if i == 0 else None)
                    t1 = kb.I("dve", lambda e: e.reciprocal(out=rc[:, :], in_=pd[:, 0:128]), sig=True)
                    kb.W("dve", t1)
                    t2 = kb.I("dve", lambda e: e.tensor_tensor(out=out_ap(i), in0=po[:, 0:128], in1=rc[:, :], op=ALU.mult), sig=True)
                    R["rc"].free[ri] = t2
                    R["po"].free[oi] = t2
                    res["last"] = t2

            nu = len(units)
            for u in range(min(LA, nu)):
                emit_S(u)
            for u in range(nu):
                if u + LA < nu:
                    emit_S(u + LA)
                emit_PV(u)
            return res["last"]

        xsrc = xT_in
        for li, L in enumerate(LAYERS):
            is_na = (L % 2 == 0)
            ng = 1 if is_na else 3
            gbase = li * 64
            hbase = li * 16
            last_layer = (li == len(LAYERS) - 1)

            def qkv_script(kb, T, li=li, is_na=is_na, ng=ng, gbase=gbase, hbase=hbase, xsrc=xsrc):
                R = mk_rings(kb, T, {"xs": 2, "sq": 1, "pn": 2, "rs": 2, "wr": 3, "pm": 4, "sq2": 2, "stg": 3, "wv": 2, "vst": 2})
                big = T["big"]
                prev = None
                for c in range(NCH):
                    for e_ in ("dve", "pool"):
                        kb.W(e_, prev)
                    toks = norm_chunk(kb, T, R, xsrc, c, gbase + 0)

                    def epi(bi, b, tt, ps, tm, c=c):
                        g = b // 32
                        isk = (b % 32) >= 16
                        h = b % 16
                        gain = T["hvec"][:, hbase + g * 2 + (1 if isk else 0): hbase + g * 2 + (1 if isk else 0) + 1]
                        if isk:
                            dst = KT[g][h, :, PAD + c * CH + tt * 512: PAD + c * CH + (tt + 1) * 512]
                        else:
                            dst = QT[g][h, :, c * CH + tt * 512: c * CH + (tt + 1) * 512]
                        return epi_qknorm(kb, T, R, ps, tm, gain, dst)
                    tl1 = proj_fm(kb, T, R, W[li, "qk"], list(range(32 * ng)), 16, big, CH, toks, epi)

                    for vg in range(4 * ng):
                        def epiv(t16, ps, tm, vg=vg, c=c):
                            g = vg // 4
                            vi = R["vst"].nxt()
                            vs = R["vst"].t[vi]
                            kb.W("act", tm, R["vst"].free[vi])
                            t1 = kb.I("act", lambda e: e.activation(out=vs[:], in_=ps[:, :], func=AF.Copy), sig=True)
                            kb.W("sp", t1)
                            r0 = PAD + c * CH + t16 * 128
                            dst = VV[g][(vg % 4) * 4:(vg % 4) * 4 + 4, r0:r0 + 128, :].rearrange("h p d -> p h d")
                            R["vst"].free[vi] = kb.DMA("sp", R["vst"].sem[vi], dst, vs[:].rearrange("p (h d) -> p h d", d=128))
                            return t1
                        proj_tm(kb, T, R, W[li, "v"], vg, big, 16, toks, epiv)
                    prev = ("pe", kb.cnt["pe"])
                end_phase(kb, T, [])

            phase({
                "big": ([128, 16, CH], BF16, "sb"),
                "xs0": ([128, 16, 256], F32, "sb"), "xs1": ([128, 16, 256], F32, "sb"),
                "sq0": ([128, 16, 256], BF16, "sb"),
                "rs0": ([128, 512], F32, "sb"), "rs1": ([128, 512], F32, "sb"),
                "wr0": ([128, 2048], BF16, "sb"), "wr1": ([128, 2048], BF16, "sb"), "wr2": ([128, 2048], BF16, "sb"),
                "wv0": ([128, 8192], BF16, "sb"), "wv1": ([128, 8192], BF16, "sb"),
                "sq20": ([128, 512], BF16, "sb"), "sq21": ([128, 512], BF16, "sb"),
                "stg0": ([128, 512], BF16, "sb"), "stg1": ([128, 512], BF16, "sb"), "stg2": ([128, 512], BF16, "sb"),
                "vst0": ([128, 512], BF16, "sb"), "vst1": ([128, 512], BF16, "sb"),
                "pn0": ([128, 512], F32, "ps"), "pn1": ([128, 512], F32, "ps"),
                "pm0": ([128, 512], F32, "ps"), "pm1": ([128, 512], F32, "ps"), "pm2": ([128, 512], F32, "ps"), "pm3": ([128, 512], F32, "ps"),
            }, qkv_script)
            if DBG_STOP == (li, 1):
                break

            if is_na:
                slots = [(0, d) for d in range(-REACH_NA, REACH_NA + 1)]
                nbias = 5 * 7 * 128
            else:
                slots = [(g, d) for g in range(3) for d in range(-REACH_DIL[g], REACH_DIL[g] + 1)]
                nbias = 27 * 128
            reach = [REACH_NA] if is_na else list(REACH_DIL)
            kw = [CH + 2 * r * 128 for r in reach]
            koff = [sum(kw[:g]) for g in range(ng)]
            vt = [NQT + 2 * r for r in reach]
            voff = [sum(vt[:g]) for g in range(ng)]
            xdst = XS

            def att_script(kb, T, li=li, is_na=is_na, ng=ng, slots=slots, reach=reach, kw=kw, koff=koff, vt=vt, voff=voff, xsrc=xsrc, xdst=xdst, nbias=nbias):
                R = mk_rings(kb, T, {"qk": 2, "vb": 2, "bb": 2, "ps": 4, "po": 2, "pd": 2, "pt": 4, "tm": 4, "rc": 2, "wr": 2, "xr": 2})
                R["pm"] = R["ps"]
                if not is_na:
                    tvo = None
                    for idx in range(8):
                        tvo = kb.I("dve", lambda e, idx=idx: e.tensor_scalar(out=T["vo"][:, idx, :], in0=T["ones"][:, :], scalar1=T["flags"][:, 16 + idx:17 + idx], scalar2=None, op0=ALU.mult), sig=True)
                    kb.W("pe", tvo)
                big = T["big"]
                prev_wo = None
                for c in range(NCH):
                    head_toks = []
                    for h in range(NH):
                        qi = R["qk"].nxt()
                        qk = R["qk"].t[qi]
                        kb.W("sp", R["qk"].free[qi])
                        tl = []
                        for g in range(ng):
                            tl.append(kb.DMA("sp", R["qk"].sem[qi], qk[:, g * CH:(g + 1) * CH], QT[g][h, :, c * CH:(c + 1) * CH]))
                            k0 = PAD + c * CH - reach[g] * 128
                            tl.append(kb.DMA("sp", R["qk"].sem[qi], qk[:, 3 * CH + koff[g]: 3 * CH + koff[g] + kw[g]], KT[g][h, :, k0:k0 + kw[g]]))
                        vi = R["vb"].nxt()
                        vb = R["vb"].t[vi]
                        bi_ = R["bb"].nxt()
                        bb = R["bb"].t[bi_]
                        kb.W("sp", R["vb"].free[vi])
                        kb.W("pool", R["bb"].free[bi_])
                        for g in range(ng):
                            r0 = PAD + c * CH - reach[g] * 128
                            tl.append(kb.DMA("sp", R["vb"].sem[vi], vb[:, voff[g]:voff[g] + vt[g], :], VV[g][h, r0:r0 + vt[g] * 128, :].rearrange("(a p) d -> p a d", p=128)))
                        tl.append(kb.DMA("pool", R["bb"].sem[bi_], bb[:, 0:nbias], W[li, "bias"][h, :, :]))
                        kb.W("pe", tl)
                        kb.W("dve", tl)
                        kb.W("pool", tl)
                        tmask = []
                        if not is_na:
                            kb.W("pool", tl)
                            for g in range(ng):
                                if c > 0:
                                    tmask.append(kb.I("pool", lambda e, g=g: e.tensor_scalar(out=vb[:, voff[g]:voff[g] + reach[g], :], in0=vb[:, voff[g]:voff[g] + reach[g], :], scalar1=T["flags"][:, 16 + c:17 + c], scalar2=None, op0=ALU.mult), sig=True))
                                if c < NCH - 1:
                                    o_ = voff[g] + reach[g] + NQT
                                    tmask.append(kb.I("pool", lambda e, g=g, o_=o_: e.tensor_scalar(out=vb[:, o_:o_ + reach[g], :], in0=vb[:, o_:o_ + reach[g], :], scalar1=T["flags"][:, 20 + c:21 + c], scalar2=None, op0=ALU.mult), sig=True))
                            kb.W("pe", tmask)

                        def active(i, g, dl):
                            if is_na and abs(dl) == 3 and 2 <= i <= NQT - 3:
                                return False
                            if c == 0 and i + dl < 0:
                                return False
                            if c == NCH - 1 and i + dl >= NQT:
                                return False
                            return True

                        def qtile(g, i):
                            return qk[:, g * CH + i * 128: g * CH + (i + 1) * 128]

                        def ksrc(g, dl, i):
                            o = 3 * CH + koff[g] + (i + dl + reach[g]) * 128
                            return qk[:, o:o + 128]

                        def vsrc(g, dl, i):
                            return vb[:, voff[g] + i + dl + reach[g], :]

                        def den_lhsT(g, dl, i):
                            if is_na or (0 <= i + dl < NQT):
                                return T["ones"][:, :]
                            return T["vo"][:, (c if i + dl < 0 else 4 + c), :]

                        def bias_fn(i, bi, bs, tmp, ps):
                            n = len(bs)
                            off = bi * 512
                            if is_na:
                                cls = {0: 1, 1: 2, NQT - 2: 3, NQT - 1: 4}.get(i, 0)
                                if cls == 0:
                                    return kb.I("dve", lambda e: e.tensor_tensor(out=tmp[:, 0:n * 128], in0=ps[:, 0:n * 128], in1=bb[:, off:off + n * 128], op=ALU.add), sig=True)
                                top = cls in (1, 2)
                                fcol = (0 if top else 8) + c
                                tq = kb.I("dve", lambda e: e.scalar_tensor_tensor(out=tmp[:, 0:n * 128], in0=bb[:, off:off + n * 128], scalar=T["flags"][:, fcol + 4:fcol + 5], in1=ps[:, 0:n * 128], op0=ALU.mult, op1=ALU.add), sig=True)
                                kb.W("dve", tq)
                                return kb.I("dve", lambda e: e.scalar_tensor_tensor(out=tmp[:, 0:n * 128], in0=bb[:, cls * 896 + off:cls * 896 + off + n * 128], scalar=T["flags"][:, fcol:fcol + 1], in1=tmp[:, 0:n * 128], op0=ALU.mult, op1=ALU.add), sig=True)
                            return kb.I("dve", lambda e: e.tensor_tensor(out=tmp[:, 0:n * 128], in0=ps[:, 0:n * 128], in1=bb[:, off:off + n * 128], op=ALU.add), sig=True)

                        lastq = attention_head(kb, T, R, NQT, slots, active, qtile, ksrc, vsrc, den_lhsT, bias_fn,
                                               lambda i: big[:, h, i * 128:(i + 1) * 128], prev_wo if h == 0 else None)
                        pe_done = ("pe", kb.cnt["pe"])
                        R["qk"].free[qi] = [pe_done]
                        R["vb"].free[vi] = [pe_done]
                        R["bb"].free[bi_] = [lastq]
                        head_toks.append(lastq)

                    def pref(bi, b, tt, c=c):
                        sl = slice(c * CH + tt * 512, c * CH + (tt + 1) * 512)
                        return resid_load(kb, T, R, xsrc[b, :, sl])

                    def epi(bi, b, tt, ps, tm, pre, c=c):
                        sl = slice(c * CH + tt * 512, c * CH + (tt + 1) * 512)
                        return resid_fin(kb, T, R, ps, tm, pre, xdst[b, :, sl])
                    prev_wo = proj_fm(kb, T, R, W[li, "o"], list(range(16)), 16, big, CH, head_toks[-1], epi, prefetch=pref)
                end_phase(kb, T, [])

            phase({
                "big": ([128, 16, CH], BF16, "sb"),
                "qk0": ([128, 3 * CH + 8960], BF16, "sb"), "qk1": ([128, 3 * CH + 8960], BF16, "sb"),
                "vb0": ([128, 70, 128], BF16, "sb"), "vb1": ([128, 70, 128], BF16, "sb"),
                "bb0": ([128, 5 * 7 * 128], BF16, "sb"), "bb1": ([128, 5 * 7 * 128], BF16, "sb"),
                "pt0": ([128, 512], BF16, "sb"), "pt1": ([128, 512], BF16, "sb"), "pt2": ([128, 512], BF16, "sb"), "pt3": ([128, 512], BF16, "sb"),
                "tm0": ([128, 512], F32, "sb"), "tm1": ([128, 512], F32, "sb"), "tm2": ([128, 512], F32, "sb"), "tm3": ([128, 512], F32, "sb"),
                "vo": ([128, 8, 128], BF16, "sb"),
                "rc0": ([128, 128], F32, "sb"), "rc1": ([128, 128], F32, "sb"),
                "wr0": ([128, 2048], BF16, "sb"), "wr1": ([128, 2048], BF16, "sb"),
                "xr0": ([128, 512], F32, "sb"), "xr1": ([128, 512], F32, "sb"),
                "pm0": ([128, 512], F32, "ps"), "ps1": ([128, 512], F32, "ps"), "ps2": ([128, 512], F32, "ps"), "ps3": ([128, 512], F32, "ps"),
                "po0": ([128, 512], F32, "ps"), "po1": ([128, 512], F32, "ps"),
                "pd0": ([128, 512], F32, "ps"), "pd1": ([128, 512], F32, "ps"),
            }, att_script)
            xsrc = XS
            if DBG_STOP == (li, 2):
                break

            def cross_script(kb, T, li=li, gbase=gbase, hbase=hbase):
                R = mk_rings(kb, T, {"xs": 2, "sq": 1, "pn": 1, "rs": 2, "wr": 3, "sq2": 2, "stg": 2, "wv": 1, "ps": 3, "po": 2, "pd": 2, "pt": 3, "rc": 2, "xr": 3})
                R["pm"] = R["ps"]
                big = T["big"]
                mt = T["mt"]
                kx = T["kx"]
                vx = T["vx"]
                qx = T["qx"]
                ox = T["ox"]
                prev = None
                for c in range(NCH):
                    for e_ in ("dve", "pool", "act"):
                        kb.W(e_, prev)
                    tm_ = norm_chunk(kb, T, R, None, 0, gbase + 32, ntok=256, dst=mt, srcap=memT_in[c])

                    def epik(bi, b, tt, ps, tm):
                        raise RuntimeError
                    for b in range(4):
                        wi = R["wr"].nxt()
                        wt = R["wr"].t[wi]
                        kb.W("pool", R["wr"].free[wi])
                        tw = kb.DMA("pool", R["wr"].sem[wi], wt[:, :], W[li, "xk"][b, :, :])
                        kb.W("pe", tw, tm_)
                        pb = R["pm"].nxt()
                        ps = R["pm"].t[pb]
                        kb.W("pe", R["pm"].free[pb])
                        for kc in range(16):
                            tmm = kb.I("pe", lambda e, kc=kc: e.matmul(ps[:, 0:256], wt[:, kc * 128:(kc + 1) * 128], mt[:, kc, :], start=(kc == 0), stop=(kc == 15)), sig=(kc == 15))
                        R["wr"].free[wi] = tmm
                        qi = R["sq2"].nxt()
                        sq = R["sq2"].t[qi]
                        kb.W("act", tmm, R["sq2"].free[qi])
                        t1 = kb.I("act", lambda e: e.activation(out=sq[:, 0:256], in_=ps[:, 0:256], func=AF.Square), sig=True)
                        p2 = R["pn"].nxt()
                        ps2 = R["pn"].t[p2]
                        kb.W("pe", t1, R["pn"].free[p2])
                        t2 = kb.I("pe", lambda e: e.matmul(ps2[:, 0:256], T["ones"][:, :], sq[:, 0:256], start=True, stop=True), sig=True)
                        R["sq2"].free[qi] = t2
                        ri = R["rs"].nxt()
                        rs = R["rs"].t[ri]
                        kb.W("dve", t2, R["rs"].free[ri], tmm, prev)
                        kb.W("act", ("pe", kb.cnt["pe"]), R["rs"].free[ri])
                        t3a = kb.I("act", lambda e: e.activation(out=rs[:, 0:256], in_=ps2[:, 0:256], func=AF.Sqrt, bias=T["epsb"][:, 1:2], scale=1.0), sig=True)
                        kb.W("dve", t3a)
                        t3 = kb.I("dve", lambda e: e.reciprocal(out=rs[:, 0:256], in_=rs[:, 0:256]), sig=True)
                        kb.W("dve", t3)
                        R["pn"].free[p2] = t3
                        t4 = kb.I("dve", lambda e, b=b: e.scalar_tensor_tensor(out=kx[:, b, :], in0=ps[:, 0:256], scalar=T["hvec"][:, hbase + 7:hbase + 8], in1=rs[:, 0:256], op0=ALU.mult, op1=ALU.mult), sig=True)
                        R["rs"].free[ri] = t4
                        R["pm"].free[pb] = t4
                    tk_done = t4
                    def epiv(t16, ps, tm):
                        kb.W("act", tm)
                        return kb.I("act", lambda e: e.activation(out=vx[:, t16, :], in_=ps[:, :], func=AF.Copy), sig=True)
                    vtoks = []

                    def epiv2(t16, ps, tm):
                        t = epiv(t16, ps, tm)
                        vtoks.append(t)
                        return t
                    proj_tm(kb, T, R, W[li, "xv"], 0, mt, 2, tm_, epiv2)
                    toks = norm_chunk(kb, T, R, XS, c, gbase + 16)

                    def epiq(bi, b, tt, ps, tm):
                        qi = R["sq2"].nxt()
                        sq = R["sq2"].t[qi]
                        kb.W("act", tm, R["sq2"].free[qi])
                        t1 = kb.I("act", lambda e: e.activation(out=sq[:], in_=ps[:, :], func=AF.Square), sig=True)
                        p2 = R["pn"].nxt()
                        ps2 = R["pn"].t[p2]
                        kb.W("pe", t1, R["pn"].free[p2])
                        t2 = kb.I("pe", lambda e: e.matmul(ps2[:, :], T["ones"][:, :], sq[:], start=True, stop=True), sig=True)
                        R["sq2"].free[qi] = t2
                        ri = R["rs"].nxt()
                        rs = R["rs"].t[ri]
                        kb.W("dve", t2, R["rs"].free[ri], tm)
                        kb.W("act", ("pe", kb.cnt["pe"]), R["rs"].free[ri])
                        t3a = kb.I("act", lambda e: e.activation(out=rs[:, :], in_=ps2[:, :], func=AF.Sqrt, bias=T["epsb"][:, 1:2], scale=1.0), sig=True)
                        kb.W("dve", t3a)
                        t3 = kb.I("dve", lambda e: e.reciprocal(out=rs[:, :], in_=rs[:, :]), sig=True)
                        kb.W("dve", t3)
                        R["pn"].free[p2] = t3
                        t4 = kb.I("dve", lambda e: e.scalar_tensor_tensor(out=qx[:, b, tt * 512:(tt + 1) * 512], in0=ps[:, :], scalar=T["hvec"][:, hbase + 6:hbase + 7], in1=rs[:, :], op0=ALU.mult, op1=ALU.mult), sig=True)
                        R["rs"].free[ri] = t4
                        return t4
                    proj_fm(kb, T, R, W[li, "xq"], list(range(4)), 16, big, CH, toks, epiq)
                    qdone = ("dve", kb.cnt["dve"])
                    kb.W("pe", qdone, tk_done, vtoks)
                    lastq = None
                    for h in range(4):
                        lastq = attention_head(
                            kb, T, R, NQT, [(0, 0), (0, 1)], lambda i, g, dl: True,
                            lambda g, i, h=h: qx[:, h, i * 128:(i + 1) * 128],
                            lambda g, dl, i, h=h: kx[:, h, dl * 128:(dl + 1) * 128],
                            lambda g, dl, i, h=h: vx[:, dl, h * 128:(h + 1) * 128],
                            lambda g, dl, i: T["ones"][:, :], None,
                            lambda i, h=h: ox[:, h, i * 128:(i + 1) * 128], prev if h == 0 else None, scale=1.0, LA=2)

                    def pref(bi, b, tt, c=c):
                        sl = slice(c * CH + tt * 512, c * CH + (tt + 1) * 512)
                        return resid_load(kb, T, R, XS[b, :, sl])

                    def epi(bi, b, tt, ps, tm, pre, c=c):
                        sl = slice(c * CH + tt * 512, c * CH + (tt + 1) * 512)
                        return resid_fin(kb, T, R, ps, tm, pre, XS[b, :, sl])
                    prev = proj_fm(kb, T, R, W[li, "xo"], list(range(16)), 4, ox, CH, lastq, epi, prefetch=pref)
                end_phase(kb, T, [])

            phase({
                "big": ([128, 16, CH], BF16, "sb"),
                "mt": ([128, 16, 256], BF16, "sb"), "kx": ([128, 4, 256], BF16, "sb"), "vx": ([128, 2, 512], BF16, "sb"),
                "qx": ([128, 4, CH], BF16, "sb"), "ox": ([128, 4, CH], BF16, "sb"),
                "xs0": ([128, 16, 256], F32, "sb"), "xs1": ([128, 16, 256], F32, "sb"),
                "sq0": ([128, 16, 256], BF16, "sb"),
                "rs0": ([128, 512], F32, "sb"), "rs1": ([128, 512], F32, "sb"),
                "wr0": ([128, 2048], BF16, "sb"), "wr1": ([128, 2048], BF16, "sb"), "wr2": ([128, 2048], BF16, "sb"),
                "wv0": ([128, 8192], BF16, "sb"),
                "sq20": ([128, 512], BF16, "sb"), "sq21": ([128, 512], BF16, "sb"),
                "stg0": ([128, 8], BF16, "sb"), "stg1": ([128, 8], BF16, "sb"),
                "pt0": ([128, 512], BF16, "sb"), "pt1": ([128, 512], BF16, "sb"), "pt2": ([128, 512], BF16, "sb"),
                "rc0": ([128, 128], F32, "sb"), "rc1": ([128, 128], F32, "sb"),
                "xr0": ([128, 512], F32, "sb"), "xr1": ([128, 512], F32, "sb"), "xr2": ([128, 512], F32, "sb"),
                "pn0": ([128, 512], F32, "ps"),
                "pm0": ([128, 512], F32, "ps"), "ps1": ([128, 512], F32, "ps"), "ps2": ([128, 512], F32, "ps"),
                "po0": ([128, 512], F32, "ps"), "po1": ([128, 512], F32, "ps"),
                "pd0": ([128, 512], F32, "ps"), "pd1": ([128, 512], F32, "ps"),
            }, cross_script)
            if DBG_STOP == (li, 3):
                break

            xout = yT if last_layer else XS

            def mlpup_script(kb, T, li=li, gbase=gbase):
                R = mk_rings(kb, T, {"xs": 2, "sq": 1, "pn": 1, "rs": 2, "wr": 3, "pm": 4, "rl": 2, "stg": 3})
                big = T["big"]
                prev = None
                for c in range(NCH):
                    for e_ in ("dve", "pool", "sp"):
                        kb.W(e_, prev)
                    toks = norm_chunk(kb, T, R, XS, c, gbase + 48)

                    def epi(bi, b, tt, ps, tm, c=c):
                        ri = R["rl"].nxt()
                        rl = R["rl"].t[ri]
                        kb.W("act", tm, R["rl"].free[ri])
                        t1 = kb.I("act", lambda e: e.activation(out=rl[:], in_=ps[:, :], func=AF.Relu), sig=True)
                        si = R["stg"].nxt()
                        sg = R["stg"].t[si]
                        kb.W("dve", t1, R["stg"].free[si])
                        t2 = kb.I("dve", lambda e: e.tensor_tensor(out=sg[:], in0=rl[:], in1=rl[:], op=ALU.mult), sig=True)
                        R["rl"].free[ri] = t2
                        kb.W("sp", t2)
                        R["stg"].free[si] = kb.DMA("sp", R["stg"].sem[si], AT[b, :, c * CH + tt * 512: c * CH + (tt + 1) * 512], sg[:])
                        return t1
                    proj_fm(kb, T, R, W[li, "up"], list(range(64)), 16, big, CH, toks, epi)
                    prev = ("pe", kb.cnt["pe"])
                end_phase(kb, T, [])

            phase({
                "big": ([128, 16, CH], BF16, "sb"),
                "xs0": ([128, 16, 256], F32, "sb"), "xs1": ([128, 16, 256], F32, "sb"),
                "sq0": ([128, 16, 256], BF16, "sb"),
                "rs0": ([128, 512], F32, "sb"), "rs1": ([128, 512], F32, "sb"),
                "wr0": ([128, 2048], BF16, "sb"), "wr1": ([128, 2048], BF16, "sb"), "wr2": ([128, 2048], BF16, "sb"),
                "rl0": ([128, 512], F32, "sb"), "rl1": ([128, 512], F32, "sb"),
                "stg0": ([128, 512], BF16, "sb"), "stg1": ([128, 512], BF16, "sb"), "stg2": ([128, 512], BF16, "sb"),
                "pn0": ([128, 512], F32, "ps"),
                "pm0": ([128, 512], F32, "ps"), "pm1": ([128, 512], F32, "ps"), "pm2": ([128, 512], F32, "ps"), "pm3": ([128, 512], F32, "ps"),
            }, mlpup_script)

            def mlpdn_script(kb, T, li=li, xout=xout):
                R = mk_rings(kb, T, {"wd": 2, "pm": 4, "xr": 3, "ab": 4, "wb": 2})
                ab = T["abig"]
                twb = [None] * 16
                for s8 in range(T_ // 1024):
                    kb.W("sp", ("pe", kb.cnt["pe"]))
                    tla = []
                    for q4 in range(4):
                        tla.append(kb.DMA("sp", R["ab"].sem[q4], ab[:, q4 * 16:(q4 + 1) * 16, :], AT[q4 * 16:(q4 + 1) * 16, :, s8 * 1024:(s8 + 1) * 1024].rearrange("k p t -> p k t")))
                    for b in range(16):
                        wi = R["wd"].nxt()
                        wt = R["wd"].t[wi]
                        if s8 == 0:
                            kb.W("pool", R["wd"].free[wi])
                            tw = kb.DMA("pool", R["wd"].sem[wi], wt[:, :], W[li, "dn"][b, :, :])
                            kb.W("act", tw)
                            twb[b] = kb.DMA("act", R["wb"].sem[wi], WDB[b, :, :], wt[:, :])
                        else:
                            kb.W("act", R["wd"].free[wi], [(sm, kb.cnt[sm]) for sm in R["wb"].sem])
                            tw = kb.DMA("act", R["wd"].sem[wi], wt[:, :], WDB[b, :, :])
                        kb.W("pe", tw)
                        for tt in range(2):
                            pb = R["pm"].nxt()
                            ps = R["pm"].t[pb]
                            kb.W("pe", R["pm"].free[pb])
                            for kc in range(64):
                                if kc % 16 == 0:
                                    kb.W("pe", tla[kc // 16])
                                tmm = kb.I("pe", lambda e, kc=kc, tt=tt: e.matmul(ps[:, :], wt[:, kc * 128:(kc + 1) * 128], ab[:, kc, tt * 512:(tt + 1) * 512], start=(kc == 0), stop=(kc == 63)), sig=(kc == 63))
                            sl = slice(s8 * 1024 + tt * 512, s8 * 1024 + (tt + 1) * 512)
                            R["pm"].free[pb] = epi_resid(kb, T, R, ps, tmm, XS[b, :, sl], xout[b, :, sl])
                        R["wd"].free[wi] = [tmm, twb[b]] if s8 == 0 else [tmm]
                end_phase(kb, T, [])

            phase({
                "abig": ([128, 64, 1024], BF16, "sb"),
                "wd0": ([128, 8192], BF16, "sb"), "wd1": ([128, 8192], BF16, "sb"),
                "ab0": ([128, 8], BF16, "sb"), "ab1": ([128, 8], BF16, "sb"), "ab2": ([128, 8], BF16, "sb"), "ab3": ([128, 8], BF16, "sb"), "wb0": ([128, 8], BF16, "sb"), "wb1": ([128, 8], BF16, "sb"),
                "xr0": ([128, 512], F32, "sb"), "xr1": ([128, 512], F32, "sb"), "xr2": ([128, 512], F32, "sb"),
                "pm0": ([128, 512], F32, "ps"), "pm1": ([128, 512], F32, "ps"), "pm2": ([128, 512], F32, "ps"), "pm3": ([128, 512], F32, "ps"),
            }, mlpdn_script)
    return nc


def _blk(w, KC):
    K, M = w.shape
    return np.ascontiguousarray(w.reshape(KC, 128, M // 128, 128).transpose(2, 1, 0, 3).reshape(M // 128, 128, KC * 128))


def _blkv(w):
    K, M = w.shape
    return np.ascontiguousarray(w.reshape(16, 128, M // 512, 512).transpose(2, 1, 0, 3).reshape(M // 512, 128, 16 * 512))


def _t5_bucket(rel):
    nb = 16
    max_exact = 8
    ret = np.where(rel > 0, nb, 0)
    n = np.abs(rel)
    n_f = np.maximum(n, 1).astype(np.float32)
    large = max_exact + (np.log(n_f / np.float32(max_exact)) / np.float32(math.log(1024 / max_exact)) * np.float32(nb - max_exact)).astype(np.int32)
    large = np.minimum(large, nb - 1)
    return ret + np.where(n < max_exact, n, large)


def _dil_bias(t5_table):
    out = np.full((NH, 128, 27 * 128), NEG, np.float32)
    j = np.arange(128)[:, None]
    i = np.arange(128)[None, :]
    s = 0
    for g, (win, d) in enumerate(DIL):
        for dl in range(-REACH_DIL[g], REACH_DIL[g] + 1):
            rel = 128 * dl + j - i
            ok = (rel % d == 0) & (np.abs(rel) <= 64 * d)
            bk = _t5_bucket(rel)
            vals = t5_table[bk, g, :]
            blk = np.where(ok[None], vals.transpose(2, 0, 1), np.float32(NEG))
            out[:, :, s * 128:(s + 1) * 128] = blk
            s += 1
    return out


def _na_bias(rpb):
    out = np.full((NH, 128, 5, 7, 128), NEG, np.float32)
    R = 32
    kc = np.arange(64)
    cs = np.clip(kc - 8, 0, 48)
    col_ok = (kc[None, :] >= cs[:, None]) & (kc[None, :] < cs[:, None] + 16)
    dc = np.clip(kc[None, :] - kc[:, None], -15, 15) + 15
    for cls in range(5):
        qt = {0: 8, 1: 0, 2: 1, 3: 14, 4: 15}[cls]
        for si, dl in enumerate(range(-3, 4)):
            for a in range(2):
                r = 2 * qt + a
                if cls == 0:
                    rs = r - 4
                else:
                    rs = min(max(r - 4, 0), R - 8)
                for b in range(2):
                    kr = 2 * (qt + dl) + b
                    if kr < rs or kr >= rs + 8:
                        continue
                    dr = kr - r + 7
                    vals = rpb[:, dr, :][:, dc]
                    blk = np.where(col_ok[None], vals, np.float32(NEG)).transpose(0, 2, 1)
                    out[:, b * 64:(b + 1) * 64, cls, si, a * 64:(a + 1) * 64] = blk
    return np.ascontiguousarray(out.reshape(NH, 128, 5 * 7 * 128))


def _prep_weights(inp, LAYERS):
    Wd = {}
    f = np.float32
    for li, L in enumerate(LAYERS):
        l2 = L // 2
        if L % 2 == 0:
            w = inp["w_qkv_a"][l2]
            wq, wk, wv = w[:, 0:2048], w[:, 2048:4096], w[:, 4096:6144]
            Wd["wqk%d" % li] = _blk(np.concatenate([wq, wk], 1), 16)
            Wd["wv%d" % li] = _blkv(wv)
            Wd["bias%d" % li] = _na_bias(inp["rpb_a"][l2])
            Wd["wo%d" % li] = _blk(inp["w_o_a"][l2], 16)
        else:
            w = inp["w_qkv_b"][l2].reshape(2048, 3, 3, 2048)
            qk = np.concatenate([np.concatenate([w[:, g, 0], w[:, g, 1]], 1) for g in range(3)], 1)
            Wd["wqk%d" % li] = _blk(qk, 16)
            Wd["wv%d" % li] = _blkv(np.concatenate([w[:, g, 2] for g in range(3)], 1))
            Wd["bias%d" % li] = _dil_bias(inp["t5_table"])
            Wd["wo%d" % li] = _blk(inp["w_o_b"][l2], 16)
        Wd["wxq%d" % li] = _blk(inp["w_q_x"][L], 16)
        wkv = inp["w_kv_x"][L]
        Wd["wxk%d" % li] = _blk(wkv[:, 0:512], 16)
        Wd["wxv%d" % li] = _blkv(wkv[:, 512:1024])
        Wd["wxo%d" % li] = _blk(inp["w_o_x"][L], 4)
        Wd["wup%d" % li] = _blk(inp["w_up"][L], 16)
        Wd["wdn%d" % li] = _blk(inp["w_down"][L], 64)
    return Wd


def _vec_inputs(inp, LAYERS):
    gv = np.zeros((128, 4 * len(LAYERS) * 16), np.float32)
    hv = np.zeros((128, len(LAYERS) * 16), np.float32)
    for li, L in enumerate(LAYERS):
        for j, nm in enumerate(("g_mix", "g_cross", "g_mem", "g_mlp")):
            gv[:, li * 64 + j * 16: li * 64 + (j + 1) * 16] = inp[nm][L].reshape(16, 128).T
        l2 = L // 2
        if L % 2 == 0:
            hv[:, li * 16 + 0] = inp["q_norm_a"][l2]
            hv[:, li * 16 + 1] = inp["k_norm_a"][l2]
        else:
            for g in range(3):
                hv[:, li * 16 + 2 * g] = inp["q_norm_b"][l2, g]
                hv[:, li * 16 + 2 * g + 1] = inp["k_norm_b"][l2, g]
        hv[:, li * 16 + 6] = inp["q_norm_x"][L]
        hv[:, li * 16 + 7] = inp["k_norm_x"][L]
    return gv, hv


def _flags(kind, NCH):
    fl = np.zeros((128, 32), np.float32)
    for c in range(NCH):
        if kind == "prompt":
            top = 1.0 if c == 0 else 0.0
            bot = 1.0 if c == NCH - 1 else 0.0
        else:
            top = bot = 1.0
        fl[:, c] = top
        fl[:, 4 + c] = 1.0 - top
        fl[:, 8 + c] = bot
        fl[:, 12 + c] = 1.0 - bot
        fl[:, 16 + c] = 1.0 - top
        fl[:, 20 + c] = 1.0 - bot
    return fl


_CACHE = {}


def _run(inp, NCH, LAYERS, seqs, DBG_STOP=None, ncores=8):
    key = (NCH, tuple(LAYERS), DBG_STOP)
    if key not in _CACHE:
        _CACHE[key] = build(NCH, LAYERS, DBG_STOP)
    nc = _CACHE[key]
    Wd = _prep_weights(inp, LAYERS)
    gv, hv = _vec_inputs(inp, LAYERS)
    in_maps = []
    for (kind, xT, memT) in seqs:
        m = dict(Wd)
        m["xT"] = np.ascontiguousarray(xT.reshape(16, 128, NCH * CH))
        m["memT"] = np.ascontiguousarray(memT.reshape(NCH, 16, 128, 256))
        m["gvec"] = gv
        m["hvec"] = hv
        m["flags"] = _flags(kind, NCH)
        m["onesm"] = np.ones((128, 128), np.float32)
        in_maps.append(m)
    while len(in_maps) < ncores:
        in_maps.append(in_maps[-1])
    res = run_bass_kernel_spmd(nc, in_maps, core_ids=list(range(ncores)))
    return [r["yT"].reshape(2048, NCH * CH) for r in res.results]


def kernel(**inp):
    inp = {k: np.asarray(v) for k, v in inp.items()}
    LAYERS = [0, 1, 2, 3]
    NCH = 4
    xp = inp["x_prompt"][0]
    xs = inp["x_sample"].reshape(4 * 2048, 2048)
    memp = np.broadcast_to(inp["mem_prompt"][0].T[None], (4, 2048, 256))
    mems = inp["mem_sample"].transpose(0, 2, 1)
    seqs = [("prompt", np.ascontiguousarray(xp.T), np.ascontiguousarray(memp)),
            ("sample", np.ascontiguousarray(xs.T), np.ascontiguousarray(mems))]
    outs = _run(inp, NCH, LAYERS, seqs, ncores=2)
    y_prompt = np.ascontiguousarray(outs[0].T).reshape(1, 8192, 2048).astype(np.float32)
    y_sample = np.ascontiguousarray(outs[1].T).reshape(4, 2048, 2048).astype(np.float32)
    return (y_prompt, y_sample)
```
